# Optimizing a Trainium2 kernel written in Bass

```python
import math
import jax, jax.numpy as jnp
from jax import lax
import numpy as np

D_MODEL = 1024
BATCH = 4
SEQ = 4096
DEPTH = 1
DEC_BATCH = 4
DEC_SEQ = 8192
PAST_LEN = 128

ATTN_GROUPS = ((128, 1), (512, 4), (2048, 16))
N_GROUPS = 3
ATTN_HEADS_PER_GROUP = 8
ATTN_HEAD_DIM = 64
ATTN_WIDTH = N_GROUPS * ATTN_HEADS_PER_GROUP * ATTN_HEAD_DIM
ATTN_OUT_WIDTH = ATTN_HEADS_PER_GROUP * ATTN_HEAD_DIM
HALF_TAPS = ATTN_GROUPS[0][0] // (2 * ATTN_GROUPS[0][1])
ATTN_BLOCK = HALF_TAPS
ALIBI_SLOPES = tuple(2.0 ** (-8.0 * (j + 1) / ATTN_HEADS_PER_GROUP) for j in range(ATTN_HEADS_PER_GROUP))

HGRN_EXPAND = 128
HGRN_HEADS = D_MODEL // HGRN_EXPAND
HGRN_KEY_DIM = HGRN_EXPAND
HGRN_VAL_DIM = D_MODEL // HGRN_HEADS
HGRN_KEY_WIDTH = HGRN_HEADS * HGRN_KEY_DIM
HGRN_WIDTH = HGRN_HEADS * HGRN_VAL_DIM
HGRN_CHUNK = 64

D_FF = 2816
RMS_EPS = 1e-6
NEG_INF = -1e30

IN_SIZES = (ATTN_WIDTH, ATTN_WIDTH, ATTN_WIDTH,
            HGRN_KEY_WIDTH, HGRN_KEY_WIDTH, HGRN_KEY_WIDTH,
            HGRN_WIDTH, HGRN_WIDTH, D_MODEL, D_MODEL)
IN_WIDTH = sum(IN_SIZES)
IN_SPLITS = tuple(int(v) for v in np.cumsum(IN_SIZES)[:-1])

kernel_name = "dilated_attn_hgrn2_gated_encoder"


def rms_norm(x, g):
    xf = x.astype(jnp.float32)
    y = xf * lax.rsqrt(jnp.mean(xf * xf, axis=-1, keepdims=True) + RMS_EPS)
    return (y * g.astype(jnp.float32)).astype(x.dtype)


def swiglu(x, w_gu, w_down):
    a, b = jnp.split(x @ w_gu, 2, axis=-1)
    return (jax.nn.silu(a) * b) @ w_down


def dilated_window_attn(q, k, v, dilation, slopes):
    B, S, H, E = q.shape
    f32 = jnp.float32
    L = S // dilation
    nb = -(-L // ATTN_BLOCK)
    Lp = nb * ATTN_BLOCK

    def to_sub(t):
        return t.reshape(B, L, dilation, H, E).transpose(0, 2, 1, 3, 4).astype(f32)

    qs = jnp.pad(to_sub(q), ((0, 0), (0, 0), (0, Lp - L), (0, 0), (0, 0)))
    pad_kv = ((0, 0), (0, 0), (ATTN_BLOCK, Lp - L + ATTN_BLOCK), (0, 0), (0, 0))
    ks = jnp.pad(to_sub(k), pad_kv)
    vs = jnp.pad(to_sub(v), pad_kv)

    def band(t):
        return jnp.concatenate(
            [t[:, :, j * ATTN_BLOCK: j * ATTN_BLOCK + Lp].reshape(B, dilation, nb, ATTN_BLOCK, H, E)
             for j in range(3)], axis=3)

    qb = qs.reshape(B, dilation, nb, ATTN_BLOCK, H, E)
    kb, vb = band(ks), band(vs)
    s = jnp.einsum('bdnqhe,bdnkhe->bdnhqk', qb, kb) * (1.0 / math.sqrt(E))

    qi = jnp.arange(ATTN_BLOCK)
    kj = jnp.arange(3 * ATTN_BLOCK) - ATTN_BLOCK
    rel = kj[None, :] - qi[:, None]
    key_pos = jnp.arange(nb)[:, None] * ATTN_BLOCK + kj[None, :]
    valid = (jnp.abs(rel) <= HALF_TAPS)[None] & ((key_pos >= 0) & (key_pos < L))[:, None, :]
    bias = -(slopes * dilation)[:, None, None] * jnp.abs(rel).astype(f32)[None]
    s = jnp.where(valid[:, None], s + bias[None], NEG_INF)

    m = jnp.max(s, axis=-1, keepdims=True)
    p = jnp.exp(s - m)
    den = jnp.sum(p, axis=-1, keepdims=True)
    o = jnp.einsum('bdnhqk,bdnkhe->bdnqhe', p / den, vb)
    lse = (m + jnp.log(den))[..., 0]

    o = o.reshape(B, dilation, Lp, H, E)[:, :, :L].transpose(0, 2, 1, 3, 4).reshape(B, S, H, E)
    lse = lse.transpose(0, 1, 2, 4, 3).reshape(B, dilation, Lp, H)[:, :, :L]
    lse = lse.transpose(0, 2, 1, 3).reshape(B, S, H)
    return o, lse


def dilated_mixture_attention(a_q, a_k, a_v):
    B, S, _ = a_q.shape
    shp = (B, S, N_GROUPS, ATTN_HEADS_PER_GROUP, ATTN_HEAD_DIM)
    q, k, v = a_q.reshape(shp), a_k.reshape(shp), a_v.reshape(shp)
    slopes = jnp.asarray(ALIBI_SLOPES, jnp.float32)
    outs, lses = [], []
    for g, (_, dil) in enumerate(ATTN_GROUPS):
        o, lse = dilated_window_attn(q[:, :, g], k[:, :, g], v[:, :, g], dil, slopes)
        outs.append(o)
        lses.append(lse)
    w = jax.nn.softmax(jnp.stack(lses, axis=0), axis=0)
    o = jnp.sum(w[..., None] * jnp.stack(outs, axis=0), axis=0)
    return o.reshape(B, S, ATTN_OUT_WIDTH).astype(a_q.dtype)


def hgrn2_chunk_scan(q, k, log_f, v):
    B, S, H, K = q.shape
    V = v.shape[-1]
    C = HGRN_CHUNK
    nc = S // C

    def chunks(t):
        return t.reshape(B, nc, C, H, t.shape[-1]).transpose(1, 0, 3, 2, 4)

    qc, kc, gc, vc = chunks(q), chunks(k), chunks(log_f), chunks(v)
    b = jnp.cumsum(gc, axis=3)
    b_last = b[:, :, :, -1:]
    q_t = qc * jnp.exp(b)
    k_t = kc * jnp.exp(-b)
    k_end = kc * jnp.exp(b_last - b)
    causal = jnp.tril(jnp.ones((C, C), dtype=bool))
    A = jnp.where(causal, jnp.einsum('nbhtk,nbhsk->nbhts', q_t, k_t), 0.0)
    o_intra = jnp.einsum('nbhts,nbhsv->nbhtv', A, vc)

    def step(state, inp):
        q_i, k_i, v_i, dec = inp
        o = jnp.einsum('bhtk,bhkv->bhtv', q_i, state)
        state = state * dec[:, :, 0, :, None] + jnp.einsum('bhsk,bhsv->bhkv', k_i, v_i)
        return state, o

    state0 = jnp.zeros((B, H, K, V), jnp.float32)
    _, o_inter = lax.scan(step, state0, (q_t, k_end, vc, jnp.exp(b_last)))
    o = o_intra + o_inter
    return o.transpose(1, 0, 3, 2, 4).reshape(B, S, H, V)


def hgrn2_bidirectional(hq, hf_f, hf_b, hi, hg, lb_f, lb_b, norm_g):
    B, S, _ = hq.shape
    f32 = jnp.float32
    kshape = (B, S, HGRN_HEADS, HGRN_KEY_DIM)
    q = jax.nn.silu(hq.astype(f32)).reshape(kshape)
    v = hi.astype(f32).reshape(B, S, HGRN_HEADS, HGRN_VAL_DIM)

    def gates(raw, lb):
        f = lb + (1.0 - lb) * jax.nn.sigmoid(raw.astype(f32))
        return (1.0 - f).reshape(kshape), jnp.log(f).reshape(kshape)

    k_f, lf_f = gates(hf_f, lb_f)
    k_b, lf_b = gates(hf_b, lb_b)
    rev = lambda t: jnp.flip(t, axis=1)
    o = hgrn2_chunk_scan(q, k_f, lf_f, v) + rev(hgrn2_chunk_scan(rev(q), rev(k_b), rev(lf_b), rev(v)))
    o = o * lax.rsqrt(jnp.mean(o * o, axis=-1, keepdims=True) + RMS_EPS)
    o = o.reshape(B, S, HGRN_WIDTH) * norm_g.astype(f32) * jax.nn.silu(hg.astype(f32))
    return o.astype(hq.dtype)


def encoder_layer(x, l, ffn1_norm, ffn1_w_gu, ffn1_w_down, mix_norm, w_in,
                  hgrn_lb_fwd, hgrn_lb_bwd, hgrn_norm, w_branch_a, w_branch_b, w_out,
                  ffn2_norm, ffn2_w_gu, ffn2_w_down):
    x = x + 0.5 * swiglu(rms_norm(x, ffn1_norm[l]), ffn1_w_gu[l], ffn1_w_down[l])
    h = rms_norm(x, mix_norm[l])
    proj = h @ w_in[l]
    a_q, a_k, a_v, h_q, h_ff, h_fb, h_i, h_g, g_a, g_b = jnp.split(proj, IN_SPLITS, axis=-1)
    ya = dilated_mixture_attention(a_q, a_k, a_v)
    lb_f = jnp.cumsum(jax.nn.softmax(hgrn_lb_fwd.astype(jnp.float32), axis=0), axis=0)[l]
    lb_b = jnp.cumsum(jax.nn.softmax(hgrn_lb_bwd.astype(jnp.float32), axis=0), axis=0)[l]
    yb = hgrn2_bidirectional(h_q, h_ff, h_fb, h_i, h_g, lb_f, lb_b, hgrn_norm[l])
    merged = jax.nn.sigmoid(g_a) * (ya @ w_branch_a[l]) + jax.nn.sigmoid(g_b) * (yb @ w_branch_b[l])
    x = x + merged @ w_out[l]
    x = x + 0.5 * swiglu(rms_norm(x, ffn2_norm[l]), ffn2_w_gu[l], ffn2_w_down[l])
    return x


def setup_inputs(seed: int = 0) -> dict:
    key = jax.random.key(seed)
    ks = jax.random.split(key, 20)
    f32 = jnp.float32
    nrm = lambda k, shape, fan_in: jax.random.normal(k, shape, f32) * (fan_in ** -0.5)
    gain = lambda k, shape: 1.0 + 0.01 * jax.random.normal(k, shape, f32)
    return {
        "x_prompt": jax.random.normal(ks[0], (BATCH, SEQ, D_MODEL), f32),
        "x_sample": jax.random.normal(ks[1], (DEC_BATCH, DEC_SEQ, D_MODEL), f32),
        "ffn1_norm": gain(ks[2], (DEPTH, D_MODEL)),
        "ffn1_w_gu": nrm(ks[3], (DEPTH, D_MODEL, 2 * D_FF), D_MODEL),
        "ffn1_w_down": nrm(ks[4], (DEPTH, D_FF, D_MODEL), D_FF),
        "mix_norm": gain(ks[5], (DEPTH, D_MODEL)),
        "w_in": nrm(ks[6], (DEPTH, D_MODEL, IN_WIDTH), D_MODEL),
        "hgrn_lb_fwd": 0.1 * jax.random.normal(ks[7], (DEPTH + 1, HGRN_KEY_WIDTH), f32),
        "hgrn_lb_bwd": 0.1 * jax.random.normal(ks[8], (DEPTH + 1, HGRN_KEY_WIDTH), f32),
        "hgrn_norm": gain(ks[9], (DEPTH, HGRN_WIDTH)),
        "w_branch_a": nrm(ks[10], (DEPTH, ATTN_OUT_WIDTH, D_MODEL), ATTN_OUT_WIDTH),
        "w_branch_b": nrm(ks[11], (DEPTH, HGRN_WIDTH, D_MODEL), HGRN_WIDTH),
        "w_out": nrm(ks[12], (DEPTH, D_MODEL, D_MODEL), D_MODEL),
        "ffn2_norm": gain(ks[13], (DEPTH, D_MODEL)),
        "ffn2_w_gu": nrm(ks[14], (DEPTH, D_MODEL, 2 * D_FF), D_MODEL),
        "ffn2_w_down": nrm(ks[15], (DEPTH, D_FF, D_MODEL), D_FF),
        "final_norm": gain(ks[16], (D_MODEL,)),
    }


def reference(x_prompt, x_sample, ffn1_norm, ffn1_w_gu, ffn1_w_down, mix_norm, w_in,
              hgrn_lb_fwd, hgrn_lb_bwd, hgrn_norm, w_branch_a, w_branch_b, w_out,
              ffn2_norm, ffn2_w_gu, ffn2_w_down, final_norm):
    def trunk(x):
        for l in range(DEPTH):
            x = encoder_layer(x, l, ffn1_norm, ffn1_w_gu, ffn1_w_down, mix_norm, w_in,
                              hgrn_lb_fwd, hgrn_lb_bwd, hgrn_norm, w_branch_a, w_branch_b, w_out,
                              ffn2_norm, ffn2_w_gu, ffn2_w_down)
        return rms_norm(x, final_norm)

    y_prompt = trunk(x_prompt)
    y_sample = trunk(x_sample)
    return (y_prompt, y_sample)
```

```python
import math
import os
import numpy as np
import ml_dtypes
import concourse.bass as bass
import concourse.mybir as mybir
from concourse.bass_utils import run_bass_kernel_spmd

F32 = mybir.dt.float32
BF16 = mybir.dt.bfloat16
ALU = mybir.AluOpType
AF = mybir.ActivationFunctionType

ENGS = ("pe", "act", "dve", "pool", "sp")
EPOCH = 12000
NDMASEM = 12


class Op:
    __slots__ = ("eng", "fn", "reads", "writes", "dma", "deps", "inc", "waits", "idx")

    def __init__(self, eng, fn, reads, writes, dma):
        self.eng, self.fn, self.reads, self.writes, self.dma = eng, fn, reads, writes, dma
        self.deps = set()
        self.inc = None
        self.waits = ()


class _Rec:
    def __getattr__(self, name):
        def f(*a, **kw):
            self.call = (name, a, kw)
            return self
        return f


class Prog:
    def __init__(self, nc):
        self.nc = nc
        self.ops = []
        self.barriers = []

    def op(self, eng, fn, reads=(), writes=(), dma=False):
        rec = _Rec()
        fn(rec)
        name, a, kw = rec.call
        o = Op(eng, (lambda e, name=name, a=a, kw=kw: getattr(e, name)(*a, **kw)), tuple(reads), tuple(writes), dma)
        o.idx = len(self.ops)
        self.ops.append(o)
        return o

    def dma(self, q, out, in_, reads=(), writes=()):
        return self.op(q, lambda e: e.dma_start(out=out, in_=in_), reads, writes, dma=True)

    def barrier(self):
        self.barriers.append(len(self.ops))

    def analyze(self):
        ops = self.ops
        last_writer = {}
        readers = {}
        for o in ops:
            deps = set()
            for t in o.reads:
                w = last_writer.get(t)
                if w is not None:
                    deps.add(w)
            for t in o.writes:
                w = last_writer.get(t)
                if w is not None:
                    deps.add(w)
                for r in readers.get(t, ()):
                    deps.add(r)
            for t in o.reads:
                readers.setdefault(t, []).append(o.idx)
            for t in o.writes:
                last_writer[t] = o.idx
                readers[t] = []
            deps.discard(o.idx)
            if o.eng == "pe" and not o.dma:
                deps = {d for d in deps if not (ops[d].eng == "pe" and not ops[d].dma)}
            o.deps = deps
        for bp in self.barriers:
            if bp == 0 or bp >= len(ops):
                continue
            bdeps = set()
            seen_c = set()
            dcnt = {e: 0 for e in ENGS}
            for i in range(bp - 1, -1, -1):
                o = ops[i]
                if o.dma:
                    if dcnt[o.eng] < NDMASEM:
                        dcnt[o.eng] += 1
                        bdeps.add(i)
                elif o.eng not in seen_c:
                    seen_c.add(o.eng)
                    bdeps.add(i)
            first = set()
            for i in range(bp, len(ops)):
                o = ops[i]
                if o.eng not in first:
                    first.add(o.eng)
                    o.deps |= bdeps
                    if len(first) == len(ENGS):
                        break
        has_dep = [False] * len(ops)
        dma_count = {e: 0 for e in ENGS}
        dma_hist = {e: [] for e in ENGS}
        for o in ops:
            if o.dma:
                j = dma_count[o.eng]
                dma_count[o.eng] += 1
                o.inc = (("dma", o.eng, j % NDMASEM), 16 * (j // NDMASEM + 1))
                if j >= NDMASEM:
                    o.deps.add(dma_hist[o.eng][j - NDMASEM])
                dma_hist[o.eng].append(o.idx)
                has_dep[o.idx] = True
        for o in ops:
            for d in o.deps:
                has_dep[d] = True
        cnt = {e: 0 for e in ENGS}
        for o in ops:
            if o.dma:
                continue
            if has_dep[o.idx]:
                c = cnt[o.eng]
                cnt[o.eng] += 1
                o.inc = (("eng", o.eng, c // EPOCH), c % EPOCH + 1)
        known = {e: {} for e in ENGS}
        nw = 0
        for o in ops:
            need = {}
            for d in o.deps:
                k, v = ops[d].inc
                if need.get(k, 0) < v:
                    need[k] = v
            kn = known[o.eng]
            w = []
            for k, v in need.items():
                if kn.get(k, 0) < v:
                    kn[k] = v
                    w.append((k, v))
            o.waits = tuple(w)
            nw += len(w)
        self.semkeys = sorted({o.inc[0] for o in ops if o.inc is not None})
        return dict(n_ops=len(ops), n_waits=nw, n_sems=len(self.semkeys),
                    per_eng={e: sum(1 for o in ops if o.eng == e) for e in ENGS})

    def emit(self):
        from contextlib import ExitStack
        nc = self.nc
        ops = self.ops
        final = {}
        for o in ops:
            if o.inc is not None:
                k, v = o.inc
                if final.get(k, 0) < v:
                    final[k] = v
        with ExitStack() as st:
            sems = {}
            for k in self.semkeys:
                sems[k] = st.enter_context(nc.semaphore("s_%s_%s_%d" % k))
            block = st.enter_context(nc.Block())
            per_eng = {e: [o for o in ops if o.eng == e] for e in ENGS}

            def run(handle, ename):
                for o in per_eng[ename]:
                    for k, v in o.waits:
                        handle.wait_ge(sems[k], v)
                    ins = o.fn(handle)
                    if o.inc is not None:
                        ins.then_inc(sems[o.inc[0]], 16 if o.dma else 1)
                if ename == "sp":
                    for k, v in final.items():
                        handle.wait_ge(sems[k], v)

            @block.tensor
            def _(e):
                run(e, "pe")

            @block.scalar
            def _(e):
                run(e, "act")

            @block.vector
            def _(e):
                run(e, "dve")

            @block.gpsimd
            def _(e):
                run(e, "pool")

            @block.sync
            def _(e):
                run(e, "sp")


D = 1024
DFF = 2816
PAD = 1024
NPRM = 80
EPS = 1e-6
DILS = (1, 4, 16)
SLOPES = tuple(2.0 ** (-8.0 * (j + 1) / 8) for j in range(8))
C_AQ, C_AK, C_AV = 0, 1536, 3072
C_HQ, C_HFF, C_HFB, C_HI, C_HG, C_GA, C_GB = 4608, 5632, 6656, 7680, 8704, 9728, 10752


def build(S):
    SEG = S // 2
    NSU = S // 1024
    NT = S // 512
    NCH = S // 64
    nc = bass.Bass("TRN2", target_bir_lowering=False)

    def din(name, shape, dt=F32):
        return nc.dram_tensor(name, list(shape), dt, kind="ExternalInput").ap()

    xT = din("xT", [D, S])
    w_gu1 = din("w_gu1", [D, 2 * DFF]); w_d1 = din("w_d1", [DFF, D])
    w_in = din("w_in", [D, 11776])
    w_a = din("w_a", [512, D]); w_b = din("w_b", [D, D]); w_o = din("w_o", [D, D])
    w_gu2 = din("w_gu2", [D, 2 * DFF]); w_d2 = din("w_d2", [DFF, D])
    prm_d = din("prm", [128, NPRM])
    etab_d = din("etab", [6, 128, 1024])
    cmask_d = din("cmask", [2, 64, 512])
    rmask_d = din("rmask", [128, 512])
    ident_d = din("ident", [128, 128], BF16)
    yT = nc.dram_tensor("yT", [D, S], F32, kind="ExternalOutput").ap()

    def dscr(name, shape, dt):
        return nc.dram_tensor(name, list(shape), dt).ap()

    x1T = dscr("x1T", [D, S], F32)
    QT = dscr("QT", [1536, S], BF16)
    KT = dscr("KT", [1536, S + 2 * PAD], BF16)
    VA = dscr("VA", [S + 2 * PAD, 24 * 65], BF16)
    HQ = dscr("HQ", [2, D, S], BF16)
    HK = dscr("HK", [2, D, S], BF16)
    HKT = dscr("HKT", [2, 8, S, 128], BF16)
    HV = dscr("HV", [S, D], BF16)
    U = dscr("U", [3, S, 520], F32)
    OFB = dscr("OFB", [2, D, S], F32)

    P = Prog(nc)

    prm = nc.alloc_sbuf_tensor("prm_s", [128, NPRM], F32)
    lbs = nc.alloc_sbuf_tensor("lbs", [128, 16], F32)
    oml = nc.alloc_sbuf_tensor("oml", [128, 16], F32)
    noml = nc.alloc_sbuf_tensor("noml", [128, 16], F32)
    onesD = nc.alloc_sbuf_tensor("onesD", [128, 128], BF16)
    onesH = nc.alloc_sbuf_tensor("onesH", [128, 128], BF16)
    ident = nc.alloc_sbuf_tensor("ident_s", [128, 128], BF16)
    rmask = nc.alloc_sbuf_tensor("rmask_s", [128, 512], F32)
    dec_sb = nc.alloc_sbuf_tensor("dec_sb", [128, 2, 8, NCH], F32)
    zt_t = nc.alloc_sbuf_tensor("zt_t", [128, 1560], BF16)
    PB = [nc.alloc_psum_tensor("pb%d" % i, [128, 1024], F32) for i in range(4)]
    ARENA_W = (nc.sbuf_bytes_remaining - 2048) // 4
    arena = nc.alloc_sbuf_tensor("arena", [128, ARENA_W], F32)
    ar = {"off": 0}

    def a_f32(n):
        o = ar["off"]
        ar["off"] = o + n
        assert ar["off"] <= ARENA_W, ("arena overflow", ar["off"], ARENA_W)
        return arena[:, o:o + n]

    def a_bf(n):
        return a_f32((n + 1) // 2).bitcast(BF16)

    def a_reset():
        ar.setdefault("peaks", []).append(ar["off"])
        ar["off"] = 0

    def bank(i):
        return PB[i // 2][:, (i % 2) * 512:(i % 2 + 1) * 512]

    nbc = {"i": 0, "res": set()}

    def nb():
        while True:
            i = nbc["i"]
            nbc["i"] = (i + 1) % 8
            if i not in nbc["res"]:
                return i

    def ts(j):
        return slice(j * 512, (j + 1) * 512)

    P.dma("sp", prm[:], prm_d, writes=["prm"])
    P.dma("sp", ident[:], ident_d, writes=["ident"])
    P.dma("sp", rmask[:], rmask_d, writes=["rmask"])
    P.op("dve", lambda e: e.memset(onesD[:], 1.0 / 1024.0), writes=["onesD"])
    P.op("dve", lambda e: e.memset(onesH[:], 1.0 / 128.0), writes=["onesH"])
    P.op("dve", lambda e: e.tensor_tensor(out=lbs[:, 0:8], in0=prm[:, 40:48], in1=prm[:, 48:56], op=ALU.subtract), reads=["prm"], writes=["lbs"])
    P.op("dve", lambda e: e.tensor_tensor(out=lbs[:, 8:16], in0=prm[:, 56:64], in1=prm[:, 64:72], op=ALU.subtract), reads=["prm", "lbs"], writes=["lbs"])
    P.op("act", lambda e: e.activation(out=lbs[:], in_=lbs[:], func=AF.Sigmoid), reads=["lbs"], writes=["lbs"])
    P.op("dve", lambda e: e.tensor_scalar(out=oml[:], in0=lbs[:], scalar1=-1.0, scalar2=1.0, op0=ALU.mult, op1=ALU.add), reads=["lbs"], writes=["oml"])
    P.op("dve", lambda e: e.tensor_scalar(out=noml[:], in0=lbs[:], scalar1=1.0, scalar2=-1.0, op0=ALU.mult, op1=ALU.add), reads=["lbs"], writes=["noml"])
    link = prm[:, 72:73]
    P.op("dve", lambda e: e.memset(zt_t[:], 0.0), writes=["zt"])

    def emit_pads():
        for c in range(12):
            for side in range(2):
                c0 = 0 if side == 0 else PAD + S
                P.dma("sp", KT[c * 128:(c + 1) * 128, c0:c0 + PAD], zt_t[:, 0:PAD], reads=["zt"], writes=[("KTpad", c, side)])
        for side in range(2):
            r0 = 0 if side == 0 else PAD + S
            for rb in range(PAD // 128):
                P.dma("sp", VA[r0 + rb * 128:r0 + (rb + 1) * 128, :], zt_t[:, 0:1560], reads=["zt"], writes=[("VApad", side, rb)])

    def alloc_common(nw=3):
        A = {}
        A["xs"] = a_f32(8 * 1024).rearrange("p (c t) -> p c t", c=8)
        A["hn"] = a_bf(8 * 1024).rearrange("p (c t) -> p c t", c=8)
        A["gact"] = a_bf(22 * 1024).rearrange("p (c t) -> p c t", c=22)
        A["w"] = [a_bf(4096) for _ in range(nw)]
        A["sq"] = [a_bf(512) for _ in range(3)]
        A["rs"] = [a_f32(512) for _ in range(2)]
        A["sil"] = [a_f32(512) for _ in range(2)]
        A["cnt"] = {"w": 0, "sq": 0, "rs": 0, "sil": 0}
        return A

    def rot(A, name, n):
        i = A["cnt"][name] % n
        A["cnt"][name] += 1
        return i

    def norm_stat_chunk(A, j, c, b):
        xs = A["xs"]
        si = rot(A, "sq", 3)
        P.op("act", lambda e: e.activation(out=A["sq"][si], in_=xs[:, c, ts(j)], func=AF.Square),
             reads=[("xs", j, c)], writes=[("sq", si)])
        P.op("pe", lambda e: e.matmul(bank(b), lhsT=onesD[:], rhs=A["sq"][si], start=(c == 0), stop=(c == 7)),
             reads=[("sq", si), "onesD"], writes=[("pb", b)])

    def norm(A, gcol, j, final=False):
        b = nb()
        for c in range(8):
            norm_stat_chunk(A, j, c, b)
        norm_apply(A, gcol, j, b, final)

    def norm_apply(A, gcol, j, b, final=False):
        xs, hn = A["xs"], A["hn"]
        ri = rot(A, "rs", 2)
        P.op("act", lambda e, b=b, ri=ri: e.activation(out=A["rs"][ri], in_=bank(b), func=AF.Ln, bias=EPS),
             reads=[("pb", b)], writes=[("rs", ri)])
        P.op("act", lambda e, ri=ri: e.activation(out=A["rs"][ri], in_=A["rs"][ri], func=AF.Exp, scale=-0.5),
             reads=[("rs", ri)], writes=[("rs", ri)])
        for c in range(8):
            if final:
                yc = A["gact"][:, 8 * j + c, :].bitcast(F32)
                P.op("dve", lambda e, c=c, ri=ri, yc=yc: e.scalar_tensor_tensor(out=yc, in0=xs[:, c, ts(j)], scalar=prm[:, gcol + c:gcol + c + 1],
                                                                              in1=A["rs"][ri], op0=ALU.mult, op1=ALU.mult),
                     reads=[("xs", j, c), ("rs", ri), "prm"], writes=[("gact", 0, 8 * j + c), ("gact", 1, 8 * j + c)])
            else:
                P.op("dve", lambda e, c=c, ri=ri: e.scalar_tensor_tensor(out=hn[:, c, ts(j)], in0=xs[:, c, ts(j)], scalar=prm[:, gcol + c:gcol + c + 1],
                                                                       in1=A["rs"][ri], op0=ALU.mult, op1=ALU.mult),
                     reads=[("xs", j, c), ("rs", ri), "prm"], writes=[("hn", j, c)])

    def wload(A, pieces):
        s = rot(A, "w", len(A["w"]))
        slot = A["w"][s]
        for dv, src in pieces:
            P.dma("pool", dv(slot), src, writes=[("w", s)])
        return s, slot

    def ffn(A, w_gu, w_d, stat=False):
        xs, hn, gact = A["xs"], A["hn"], A["gact"]
        wguv = w_gu.rearrange("(k p) n -> p k n", p=128)
        wdv = w_d.rearrange("(k p) n -> p k n", p=128)
        for i in range(11):
            view = lambda sl: sl.rearrange("p (k a n) -> p k a n", k=8, a=2)
            s, slot = wload(A, [(lambda sl: view(sl)[:, :, 0, :], wguv[:, :, i * 256:(i + 1) * 256]),
                                (lambda sl: view(sl)[:, :, 1, :], wguv[:, :, DFF + i * 256:DFF + (i + 1) * 256])])
            v = view(slot)
            for j in range(2):
                for m2 in range(2):
                    m = 2 * i + m2
                    ba, bb = nb(), nb()
                    for k in range(8):
                        P.op("pe", lambda e, k=k, ba=ba, m2=m2, j=j, v=v: e.matmul(bank(ba), lhsT=v[:, k, 0, m2 * 128:(m2 + 1) * 128], rhs=hn[:, k, ts(j)],
                                                                              start=(k == 0), stop=(k == 7)),
                             reads=[("w", s), ("hn", j, k)], writes=[("pb", ba)])
                    for k in range(8):
                        P.op("pe", lambda e, k=k, bb=bb, m2=m2, j=j, v=v: e.matmul(bank(bb), lhsT=v[:, k, 1, m2 * 128:(m2 + 1) * 128], rhs=hn[:, k, ts(j)],
                                                                              start=(k == 0), stop=(k == 7)),
                             reads=[("w", s), ("hn", j, k)], writes=[("pb", bb)])
                    ti = rot(A, "sil", 2)
                    P.op("act", lambda e, ba=ba, ti=ti: e.activation(out=A["sil"][ti], in_=bank(ba), func=AF.Silu),
                         reads=[("pb", ba)], writes=[("sil", ti)])
                    P.op("dve", lambda e, bb=bb, ti=ti, m=m, j=j: e.tensor_tensor(out=gact[:, m, ts(j)], in0=A["sil"][ti], in1=bank(bb), op=ALU.mult),
                         reads=[("sil", ti), ("pb", bb)], writes=[("gact", j, m)])
        bs = None
        if stat:
            bs = [nb(), nb()]
            nbc["res"].update(bs)
        pend = None
        for mo in range(8):
            view = lambda sl: sl[:, 0:2816].rearrange("p (k n) -> p k n", k=22)
            s, slot = wload(A, [(view, wdv[:, :, mo * 128:(mo + 1) * 128])])
            v = view(slot)
            for j in range(2):
                b = nb()
                for k in range(22):
                    P.op("pe", lambda e, k=k, b=b, j=j, v=v: e.matmul(bank(b), lhsT=v[:, k, :], rhs=gact[:, k, ts(j)], start=(k == 0), stop=(k == 21)),
                         reads=[("w", s), ("gact", j, k)], writes=[("pb", b)])
                P.op("dve", lambda e, b=b, j=j, mo=mo: e.scalar_tensor_tensor(out=xs[:, mo, ts(j)], in0=bank(b), scalar=0.5, in1=xs[:, mo, ts(j)],
                                                                          op0=ALU.mult, op1=ALU.add),
                     reads=[("pb", b), ("xs", j, mo)], writes=[("xs", j, mo)])
            if stat:
                if pend is not None:
                    for j in range(2):
                        norm_stat_chunk(A, j, pend, bs[j])
                pend = mo
        if stat:
            for j in range(2):
                norm_stat_chunk(A, j, pend, bs[j])
            nbc["res"].difference_update(bs)
        return bs

    winv = w_in.rearrange("(k p) n -> p k n", p=128)
    xTv = xT.rearrange("(c p) t -> p c t", p=128)
    x1Tv = x1T.rearrange("(c p) t -> p c t", p=128)
    yTv = yT.rearrange("(c p) t -> p c t", p=128)

    if os.environ.get("KSTOP", "") == "0":
        print("PROG", P.analyze(), flush=True)
        P.emit()
        return nc
    A = alloc_common()
    kendT = [a_bf(512) for _ in range(3)]
    X = [[a_f32(512) for _ in range(6)] for _ in range(2)]
    sgq = a_f32(512)
    qsb = a_f32(512)
    qst = [a_bf(512) for _ in range(3)]
    hqst = [a_bf(512) for _ in range(4)]
    hkst = [a_bf(512) for _ in range(4)]
    vst = [a_bf(520).rearrange("p (h e) -> p h e", h=8) for _ in range(3)]
    hvst = [a_bf(512) for _ in range(2)]
    kst = [a_bf(512) for _ in range(2)]
    A["cnt"].update(qst=0, hqst=0, hkst=0, vst=0, hvst=0, kst=0, kendT=0)
    _g = A["gact"]
    _gt = lambda t: (_g[:, t, :].bitcast(F32), [("gact", 0, t), ("gact", 1, t)])
    TS = [dict(sgq=(sgq, ["sgq"]), qsb=(qsb, ["qsb"]), X=[[(X[dr][i], [("X", dr, i)]) for i in range(6)] for dr in range(2)]),
          dict(sgq=_gt(12), qsb=_gt(13), X=[[_gt(dr * 6 + i) for i in range(6)] for dr in range(2)])]
    KEND = [(_g[:, 14 + b_ // 2, ts(b_ % 2)], ("gact", b_ % 2, 14 + b_ // 2)) for b_ in range(16)]
    for i in range(3):
        P.op("dve", lambda e, i=i: e.memset(vst[i][:, :, 64:65], 1.0), writes=[("vst", i)])

    def s1_xload(su_):
        for j in range(2):
            P.dma("sp", A["xs"][:, :, ts(j)], xTv[:, :, su_ * 1024 + j * 512:su_ * 1024 + (j + 1) * 512], writes=[("xs", j, c) for c in range(8)])
    _nsu1 = 0 if os.environ.get('KSKIP1') else NSU
    if _nsu1:
        s1_xload(0)
    emit_pads()
    for su in range(_nsu1):
        t0 = su * 1024
        for j in range(2):
            norm(A, 0, j)
        bs1 = ffn(A, w_gu1, w_d1, stat=True)
        for j in range(2):
            P.dma("sp", x1Tv[:, :, t0 + j * 512:t0 + (j + 1) * 512], A["xs"][:, :, ts(j)], reads=[("xs", j, c) for c in range(8)], writes=[("x1T", su, j)])
        for j in range(2):
            norm_apply(A, 8, j, bs1[j])
        if su + 1 < _nsu1:
            s1_xload(su + 1)
        hn = A["hn"]
        def aqk_block(which, blk):
            cbase = C_AQ if which == 0 else C_AK
            view = lambda sl: sl[:, 0:2048].rearrange("p (k n) -> p k n", k=8)
            s, slot = wload(A, [(view, winv[:, :, cbase + blk * 256:cbase + (blk + 1) * 256])])
            v = view(slot)
            for m2 in range(2):
                ch = blk * 2 + m2
                for j in range(2):
                    b = nb()
                    for k in range(8):
                        P.op("pe", lambda e: e.matmul(bank(b), lhsT=v[:, k, m2 * 128:(m2 + 1) * 128], rhs=hn[:, k, ts(j)], start=(k == 0), stop=(k == 7)),
                             reads=[("w", s), ("hn", j, k)], writes=[("pb", b)])
                    qi = rot(A, "qst", 3)
                    if which == 0:
                        P.op("act", lambda e: e.activation(out=qst[qi], in_=bank(b), func=AF.Copy, scale=0.125), reads=[("pb", b)], writes=[("qst", qi)])
                        P.dma("sp", QT[ch * 128:(ch + 1) * 128, t0 + j * 512:t0 + (j + 1) * 512], qst[qi], reads=[("qst", qi)], writes=[("QT", ch, su, j)])
                    else:
                        P.op("dve", lambda e: e.tensor_copy(out=qst[qi], in_=bank(b)), reads=[("pb", b)], writes=[("qst", qi)])
                        P.dma("sp", KT[ch * 128:(ch + 1) * 128, PAD + t0 + j * 512:PAD + t0 + (j + 1) * 512], qst[qi], reads=[("qst", qi)], writes=[("KT", ch, su, j)])

        def tok_block(i):
            cb = C_AV + i * 512 if i < 3 else C_HI + (i - 3) * 512
            view = lambda sl: sl.rearrange("p (k n) -> p k n", k=8)
            s, slot = wload(A, [(view, winv[:, :, cb:cb + 512])])
            v = view(slot)
            for tb in range(8):
                b = nb()
                j = tb // 4
                for k in range(8):
                    P.op("pe", lambda e: e.matmul(bank(b), lhsT=hn[:, k, tb * 128:(tb + 1) * 128], rhs=v[:, k, :], start=(k == 0), stop=(k == 7)),
                         reads=[("w", s), ("hn", j, k)], writes=[("pb", b)])
                if i < 3:
                    vi = rot(A, "vst", 3)
                    P.op("act", lambda e: e.activation(out=vst[vi][:, :, 0:64], in_=bank(b).rearrange("p (h e) -> p h e", h=8), func=AF.Copy),
                         reads=[("pb", b)], writes=[("vst", vi)])
                    P.dma("sp", VA[PAD + t0 + tb * 128:PAD + t0 + (tb + 1) * 128, i * 520:(i + 1) * 520], vst[vi].rearrange("p h e -> p (h e)"),
                          reads=[("vst", vi)], writes=[("VA", i, su, tb)])
                else:
                    vi = rot(A, "hvst", 2)
                    P.op("dve", lambda e: e.tensor_copy(out=hvst[vi], in_=bank(b)), reads=[("pb", b)], writes=[("hvst", vi)])
                    P.dma("sp", HV[t0 + tb * 128:t0 + (tb + 1) * 128, (i - 3) * 512:(i - 2) * 512], hvst[vi], reads=[("hvst", vi)], writes=[("HV", i, su, tb)])

        def hgrn_proj(h):
            view = lambda sl: sl[:, 0:3072].rearrange("p (k a n) -> p k a n", k=8, a=3)
            s, slot = wload(A, [(lambda sl, a=a: view(sl)[:, :, a, :], winv[:, :, cb + h * 128:cb + (h + 1) * 128])
                                for a, cb in enumerate((C_HQ, C_HFF, C_HFB))])
            v = view(slot)
            out = []
            for j in range(2):
                bks = [nb(), nb(), nb()]
                for a in range(3):
                    for k in range(8):
                        P.op("pe", lambda e: e.matmul(bank(bks[a]), lhsT=v[:, k, a, :], rhs=hn[:, k, ts(j)], start=(k == 0), stop=(k == 7)),
                             reads=[("w", s), ("hn", j, k)], writes=[("pb", bks[a])])
                out.append(bks)
            return out

        def q_ops(h, j, bq):
            T = TS[j]
            sgq_, kq1 = T["sgq"]
            qsb_, kq2 = T["qsb"]
            return [lambda: P.op("act", lambda e: e.activation(out=sgq_, in_=bank(bq), func=AF.Sigmoid), reads=[("pb", bq)], writes=kq1),
                    lambda: P.op("dve", lambda e: e.tensor_tensor(out=qsb_, in0=bank(bq), in1=sgq_, op=ALU.mult), reads=[("pb", bq)] + kq1, writes=kq2)]

        def chain_ops(h, j, dr, bk, kei):
            T = TS[j]
            tg = su * 2 + j
            x = [T["X"][dr][i][0] for i in range(6)]
            xk = [T["X"][dr][i][1] for i in range(6)]
            qsb_, kq2 = T["qsb"]
            col = dr * 8 + h
            kb, kbkey = KEND[kei]
            ops = []
            ops.append(lambda: P.op("act", lambda e: e.activation(out=x[0], in_=bank(bk), func=AF.Sigmoid), reads=[("pb", bk)], writes=xk[0]))
            ops.append(lambda: P.op("act", lambda e: e.activation(out=x[1], in_=x[0], func=AF.Ln, scale=oml[:, col:col + 1], bias=lbs[:, col:col + 1]),
                                    reads=xk[0] + ["oml", "lbs"], writes=xk[1]))
            ops.append(lambda: P.op("dve", lambda e: e.tensor_scalar(out=x[2], in0=x[0], scalar1=noml[:, col:col + 1], scalar2=oml[:, col:col + 1], op0=ALU.mult, op1=ALU.add),
                                    reads=xk[0] + ["oml", "noml"], writes=xk[2]))
            ops.append(lambda: P.op("dve", lambda e: e.tensor_tensor_scan(out=x[3], data0=rmask[:], data1=x[1], initial=0.0, op0=ALU.mult, op1=ALU.add),
                                    reads=xk[1] + ["rmask"], writes=xk[3]))
            if dr == 0:
                bfin, bkey = x[3], xk[3]
            else:
                x33 = x[3].rearrange("p (c t) -> p c t", t=64)
                x43 = x[4].rearrange("p (c t) -> p c t", t=64)
                ops.append(lambda: P.op("dve", lambda e: e.scalar_tensor_tensor(out=x[4], in0=x[3], scalar=-1.0, in1=x[1], op0=ALU.mult, op1=ALU.add),
                                        reads=xk[3] + xk[1], writes=xk[4]))
                ops.append(lambda: P.op("dve", lambda e: e.tensor_tensor(out=x43, in0=x43, in1=x33[:, :, 63:64].broadcast_to([128, 8, 64]), op=ALU.add),
                                        reads=xk[3] + xk[4], writes=xk[4]))
                bfin, bkey = x[4], xk[4]
            ops.append(lambda: P.op("act", lambda e: e.activation(out=x[5], in_=bfin, func=AF.Exp), reads=bkey, writes=xk[5]))
            ops.append(lambda: P.op("act", lambda e: e.activation(out=x[0], in_=bfin, func=AF.Exp, scale=-1.0), reads=bkey, writes=xk[0]))
            eb3 = x[5].rearrange("p (c t) -> p c t", t=64)
            dsel = eb3[:, :, 63:64] if dr == 0 else eb3[:, :, 0:1]
            ops.append(lambda: P.op("dve", lambda e: e.tensor_copy(out=dec_sb[:, dr, h, tg * 8:(tg + 1) * 8].rearrange("p (c o) -> p c o", o=1), in_=dsel),
                                    reads=xk[5], writes=[("dec", dr, h, tg)]))

            def qtil():
                qi = rot(A, "hqst", 4)
                P.op("dve", lambda e: e.tensor_tensor(out=hqst[qi], in0=qsb_, in1=x[5], op=ALU.mult), reads=kq2 + xk[5], writes=[("hqst", qi)])
                P.dma("sp", HQ[dr, h * 128:(h + 1) * 128, t0 + j * 512:t0 + (j + 1) * 512], hqst[qi], reads=[("hqst", qi)], writes=[("HQ", dr, h, tg)])
            ops.append(qtil)
            ops.append(lambda: P.op("dve", lambda e: e.tensor_tensor(out=x[1], in0=x[2], in1=x[0], op=ALU.mult), reads=xk[2] + xk[0], writes=xk[1]))

            def ktil():
                ki = rot(A, "hkst", 4)
                P.op("act", lambda e: e.activation(out=hkst[ki], in_=x[1], func=AF.Copy), reads=xk[1], writes=[("hkst", ki)])
                P.dma("sp", HK[dr, h * 128:(h + 1) * 128, t0 + j * 512:t0 + (j + 1) * 512], hkst[ki], reads=[("hkst", ki)], writes=[("HK", dr, h, tg)])
            ops.append(ktil)
            x13 = x[1].rearrange("p (c t) -> p c t", t=64)
            ops.append(lambda: P.op("dve", lambda e: e.tensor_tensor(out=kb.rearrange("p (c t) -> p c t", t=64), in0=x13, in1=dsel.broadcast_to([128, 8, 64]), op=ALU.mult),
                                    reads=xk[1] + xk[5], writes=[kbkey]))
            return ops

        def hgrn_tr(h, info):
            for (j, dr, kei) in info:
                tg = su * 2 + j
                kb, kbkey = KEND[kei]
                bt = nb()
                pbf = bank(bt).bitcast(BF16)
                for jb in range(4):
                    P.op("pe", lambda e: e.transpose(out=pbf[:, jb * 128:(jb + 1) * 128], in_=kb[:, jb * 128:(jb + 1) * 128], identity=ident[:]),
                         reads=[kbkey, "ident"], writes=[("pb", bt)])
                ksi = rot(A, "kst", 2)
                P.op("act", lambda e: e.activation(out=kst[ksi], in_=pbf[:, 0:512], func=AF.Copy), reads=[("pb", bt)], writes=[("kst", ksi)])
                P.dma("sp", HKT[dr, h, t0 + j * 512:t0 + (j + 1) * 512, :].rearrange("(jb p) k -> p jb k", p=128), kst[ksi].rearrange("p (jb k) -> p jb k", jb=4),
                      reads=[("kst", ksi)], writes=[("HKT", dr, h, tg)])

        other = [("aqk", 0, b_) for b_ in range(6)] + [("aqk", 1, b_) for b_ in range(6)] + [("tok", i_) for i_ in range(5)]

        def emit_other(n):
            for _ in range(n):
                if other:
                    it = other.pop(0)
                    if it[0] == "aqk":
                        aqk_block(it[1], it[2])
                    else:
                        tok_block(it[1])
        pend = {}
        for h in range(8):
            bks = hgrn_proj(h)
            lists = []
            info = []
            qlists = []
            for j in range(2):
                qlists.append(q_ops(h, j, bks[j][0]))
                for dr in range(2):
                    kei = A["cnt"]["kendT"] % len(KEND)
                    A["cnt"]["kendT"] += 1
                    lists.append(chain_ops(h, j, dr, bks[j][1 + dr], kei))
                    info.append((j, dr, kei))
            for i_ in range(2):
                for l_ in qlists:
                    l_[i_]()
            for l_ in lists:
                l_[0]()
            emit_other(2)
            if h - 2 in pend:
                hgrn_tr(h - 2, pend.pop(h - 2))
            mx = max(len(l_) for l_ in lists)
            for i_ in range(1, mx):
                for l_ in lists:
                    if i_ < len(l_):
                        l_[i_]()
            pend[h] = info
        emit_other(len(other))
        for h in sorted(pend):
            hgrn_tr(h, pend[h])
    P.barrier()
    a_reset()

    if os.environ.get("KSTOP", "") == "1":
        print("PROG", P.analyze(), flush=True)
        P.emit()
        return nc
    Qg = a_bf(4 * S).rearrange("p (c t) -> p c t", c=4)
    Kg = a_bf(4 * (S + 2 * PAD)).rearrange("p (c t) -> p c t", c=4)
    Eg = a_f32(2 * 1024).rearrange("p (k n) -> p k n", k=2)
    pex = [a_f32(1024) for _ in range(4)]
    pT = [a_bf(1024) for _ in range(4)]
    vt = [a_bf(520).rearrange("p (h e) -> p h e", h=8) for _ in range(4)]
    ust = [a_f32(520) for _ in range(2)]
    Qz = [[a_bf(512).rearrange("p (c q) -> p c q", c=4) for _ in range(2)] for _ in range(2)]
    for bq in range(2):
        for hp in range(2):
            P.op("dve", lambda e: e.memset(Qz[bq][hp], 0.0), writes=[("Qz", bq, hp)])
    cnt2 = {"pex": 0, "pT": 0, "vt": 0, "ust": 0, "po": 0, "qz": 0}
    _katt = [int(v) for v in os.environ.get('KATT', '3,99,9999').split(',')]
    for g in range(min(3, _katt[0])):
        d = DILS[g]
        L = S // d
        PADd = PAD // d
        Bs = SEG // d
        for c in range(4):
            P.dma("sp", Qg[:, c, :], QT[g * 512 + c * 128:g * 512 + (c + 1) * 128, :], writes=[("Qg", c)])
            P.dma("sp", Kg[:, c, :], KT[g * 512 + c * 128:g * 512 + (c + 1) * 128, :], writes=[("Kg", c)])
        for kt in range(2):
            P.dma("sp", Eg[:, kt, :], etab_d[g * 2 + kt], writes=[("Eg", kt)])
        Qv = Qg.rearrange("p c (i r) -> p c r i", r=d)
        Kv = Kg.rearrange("p c (i r) -> p c r i", r=d)
        VAv = VA.rearrange("(i r) f -> r i f", r=d)
        Uv = U[g].rearrange("(i r) f -> r i f", r=d)
        blocks = [(r_, m_) for r_ in range(min(d, _katt[1])) for m_ in range(min(L // 128, _katt[2]))]
        vslots = {}

        qzof = {}

        def att_qz(r, m):
            bq = cnt2["qz"] % 2
            cnt2["qz"] += 1
            qzof[(r, m)] = bq
            for hp in range(2):
                P.op("dve",
                     lambda e: e.tensor_copy(out=Qz[bq][hp][hp * 64:(hp + 1) * 64, :, :], in_=Qv[hp * 64:(hp + 1) * 64, :, r, 128 * m:128 * m + 128]),
                     reads=[("Qg", c_) for c_ in range(4)], writes=[("Qz", bq, hp)])

        def att_front(r, m, nb_=None):
            pts = []
            if (r, m) not in qzof:
                att_qz(r, m)
            bq = qzof[(r, m)]
            if nb_ is not None:
                att_qz(*nb_)
            for kt in range(2):
                jt = m + kt
                if (r, jt) not in vslots:
                    vi = cnt2["vt"] % 4
                    cnt2["vt"] += 1
                    vslots[(r, jt)] = vi
                    i0 = PADd + 128 * jt - 64
                    P.dma("sp", vt[vi].rearrange("p h e -> p (h e)"), VAv[r, i0:i0 + 128, g * 520:(g + 1) * 520], writes=[("vt", vi)])
                k0 = PADd + 128 * jt - 64
                for h in (0, 2, 4, 6, 1, 3, 5, 7):
                    c, hp = h // 2, h % 2
                    P.op("pe", lambda e: e.matmul(PB[kt][:, h * 128:(h + 1) * 128], lhsT=Kv[:, c, r, k0:k0 + 128],
                                                  rhs=Qz[bq][hp][:, c, :], start=True, stop=True),
                         reads=[("Kg", c), ("Qz", bq, hp)], writes=[("pS", kt, hp)])
                pi = cnt2["pex"] % 4
                cnt2["pex"] += 1
                for hb in range(2):
                    P.op("act", lambda e: e.activation(out=pex[pi][:, hb * 512:(hb + 1) * 512], in_=PB[kt][:, hb * 512:(hb + 1) * 512], func=AF.Exp),
                         reads=[("pS", kt, 0), ("pS", kt, 1)], writes=[("pex", pi)])
                ti = cnt2["pT"] % 4
                cnt2["pT"] += 1
                if 128 * jt == Bs:
                    lc = 74 if kt == 1 else 73
                    P.op("dve", lambda e: e.scalar_tensor_tensor(out=pT[ti], in0=pex[pi], scalar=prm[:, lc:lc + 1], in1=Eg[:, kt, :], op0=ALU.mult, op1=ALU.mult),
                         reads=[("pex", pi), ("Eg", kt), "prm"], writes=[("pT", ti)])
                else:
                    P.op("dve", lambda e: e.tensor_tensor(out=pT[ti], in0=pex[pi], in1=Eg[:, kt, :], op=ALU.mult), reads=[("pex", pi), ("Eg", kt)], writes=[("pT", ti)])
                pts.append(ti)
            return pts

        def att_back(r, m, pts):
            po = cnt2["po"] % 2
            cnt2["po"] += 1
            pO = PB[2 + po]
            for h in range(8):
                col = (h // 4) * 512 + (h % 4) * 65
                for kt in range(2):
                    vsl = vslots[(r, m + kt)]
                    P.op("pe", lambda e: e.matmul(pO[:, col:col + 65], lhsT=pT[pts[kt]][:, h * 128:(h + 1) * 128], rhs=vt[vsl][:, h, :],
                                                  start=(kt == 0), stop=(kt == 1)),
                         reads=[("pT", pts[kt]), ("vt", vsl)], writes=[("pO", po)])
            ui = cnt2["ust"] % 2
            cnt2["ust"] += 1
            for hb in range(2):
                P.op("act", lambda e: e.activation(out=ust[ui][:, hb * 260:(hb + 1) * 260], in_=pO[:, hb * 512:hb * 512 + 260], func=AF.Copy),
                     reads=[("pO", po)], writes=[("ust", ui)])
            P.dma("sp", Uv[r, 128 * m:128 * m + 128, :], ust[ui], reads=[("ust", ui)], writes=[("U", g, r, m)])

        nxt = att_front(*blocks[0], nb_=(blocks[1] if len(blocks) > 1 else None))
        for bi, (r, m) in enumerate(blocks):
            cur = nxt
            if bi + 1 < len(blocks):
                nxt = att_front(*blocks[bi + 1], nb_=(blocks[bi + 2] if bi + 2 < len(blocks) else None))
            att_back(r, m, cur)
    P.barrier()
    a_reset()

    if os.environ.get("KSTOP", "") == "2":
        print("PROG", P.analyze(), flush=True)
        P.emit()
        return nc
    qT_s = [a_bf(8 * 512).rearrange("p (h t) -> p h t", h=8) for _ in range(2)]
    kT_s = [a_bf(8 * 512).rearrange("p (h t) -> p h t", h=8) for _ in range(2)]
    ktok_s = [a_bf(8 * 1024).rearrange("p (c f) -> p c f", c=8) for _ in range(2)]
    v_s = [a_bf(8 * 1024).rearrange("p (c f) -> p c f", c=8) for _ in range(2)]
    st_f = a_f32(1024).rearrange("p (h v) -> p h v", h=8)
    st_b = a_bf(1024).rearrange("p (h v) -> p h v", h=8)
    atm = [a_bf(512).rearrange("p (h t) -> p h t", h=8) for _ in range(2)]
    o_sb = [a_f32(8 * 512).rearrange("p (h t) -> p h t", h=8) for _ in range(2)]
    cm = a_f32(2 * 512).rearrange("p (d n) -> p d n", d=2)
    for dr in range(2):
        P.dma("sp", cm[0:64, dr, :], cmask_d[dr], writes=["cm"])
    st_b2 = [st_b, a_bf(1024).rearrange("p (h v) -> p h v", h=8)]
    gcount = 0
    nseq = 0
    for dr in (1, 0):
        for h in range(8):
            P.op("dve", lambda e: e.memset(st_f[:, h, :], 0.0), writes=[("st_f", h)])
        P.op("dve", lambda e: e.memset(st_b2[nseq % 2], 0.0), writes=[("st_b", nseq % 2)])
        glist = list(range(NT)) if dr == 0 else list(range(NT - 1, -1, -1))
        steps = []
        for gi in glist:
            sl = gcount % 2
            gcount += 1
            clist = list(range(8)) if dr == 0 else list(range(7, -1, -1))
            for ci, c in enumerate(clist):
                steps.append(dict(gi=gi, sl=sl, c=c, first=(ci == 0), last=(ci == 7), n=nseq, gidx=len(steps) // 8))
                nseq += 1

        def emit_loads(stp):
            gi, sl = stp["gi"], stp["sl"]
            t0 = gi * 512
            P.dma("sp", qT_s[sl], HQ[dr].rearrange("(h k) t -> k h t", k=128)[:, :, t0:t0 + 512], reads=[("HQ", dr, h, gi) for h in range(8)], writes=[("qT_s", sl)])
            P.dma("sp", kT_s[sl], HK[dr].rearrange("(h k) t -> k h t", k=128)[:, :, t0:t0 + 512], reads=[("HK", dr, h, gi) for h in range(8)], writes=[("kT_s", sl)])
            for h in range(8):
                P.dma("sp", ktok_s[sl][0:64, :, h * 128:(h + 1) * 128], HKT[dr, h, t0:t0 + 512, :].rearrange("(c s) k -> s c k", s=64),
                      reads=[("HKT", dr, h, gi)], writes=[("ktok_s", sl)])
            P.dma("sp", v_s[sl][0:64, :, :], HV[t0:t0 + 512, :].rearrange("(c s) f -> s c f", s=64), writes=[("v_s", sl)])

        def emit_front(stp):
            sl, c, n = stp["sl"], stp["c"], stp["n"]
            par = n % 2
            pAT = PB[0][:, par * 512:(par + 1) * 512]
            pKV = PB[2 + par]
            cs = slice(c * 64, (c + 1) * 64)
            for h in range(8):
                P.op("pe", lambda e: e.matmul(pAT[0:64, h * 64:(h + 1) * 64], lhsT=kT_s[sl][:, h, cs], rhs=qT_s[sl][:, h, cs], start=True, stop=True),
                     reads=[("kT_s", sl), ("qT_s", sl)], writes=[("pAT", par)])
            for h in range(8):
                P.op("pe", lambda e: e.matmul(pKV[:, h * 128:(h + 1) * 128], lhsT=ktok_s[sl][0:64, c, h * 128:(h + 1) * 128], rhs=v_s[sl][0:64, c, h * 128:(h + 1) * 128],
                                              start=True, stop=True),
                     reads=[("ktok_s", sl), ("v_s", sl)], writes=[("pKV", par, h // 4)])
            P.op("dve", lambda e: e.tensor_tensor(out=atm[par][0:64].rearrange("p h t -> p (h t)"), in0=pAT[0:64, :], in1=cm[0:64, dr, :], op=ALU.mult),
                 reads=[("pAT", par), "cm"], writes=[("atm", par)])

        def emit_back(stp):
            gi, sl, c, n = stp["gi"], stp["sl"], stp["c"], stp["n"]
            par = n % 2
            cg = gi * 8 + c
            pOo = PB[1][:, par * 512:(par + 1) * 512]
            pKV = PB[2 + par]
            cs = slice(c * 64, (c + 1) * 64)
            sb_in, sb_out = st_b2[n % 2], st_b2[(n + 1) % 2]
            for h in range(8):
                P.op("pe", lambda e: e.matmul(pOo[:, h * 64:(h + 1) * 64], lhsT=v_s[sl][0:64, c, h * 128:(h + 1) * 128], rhs=atm[par][0:64, h, :], start=True, stop=False),
                     reads=[("v_s", sl), ("atm", par)], writes=[("pOo", par)])
                P.op("pe", lambda e: e.matmul(pOo[:, h * 64:(h + 1) * 64], lhsT=sb_in[:, h, :], rhs=qT_s[sl][:, h, cs], start=False, stop=True),
                     reads=[("qT_s", sl), ("st_b", n % 2)], writes=[("pOo", par)])
            for h in range(8):
                P.op("dve", lambda e: e.scalar_tensor_tensor(out=st_f[:, h, :], in0=st_f[:, h, :], scalar=dec_sb[:, dr, h, cg:cg + 1], in1=pKV[:, h * 128:(h + 1) * 128],
                                                             op0=ALU.mult, op1=ALU.add),
                     reads=[("st_f", h), ("pKV", par, h // 4), ("dec", dr, h, gi)], writes=[("st_f", h)])
            bnd = (dr == 0 and cg == SEG // 64 - 1) or (dr == 1 and cg == SEG // 64)
            if bnd:
                P.op("dve", lambda e: e.tensor_scalar(out=st_f.rearrange("p h v -> p (h v)"), in0=st_f.rearrange("p h v -> p (h v)"), scalar1=prm[:, 72:73], scalar2=None, op0=ALU.mult),
                     reads=[("st_f", h) for h in range(8)] + ["prm"], writes=[("st_f", h) for h in range(8)])
            P.op("act", lambda e: e.activation(out=sb_out.rearrange("p h v -> p (h v)"), in_=st_f.rearrange("p h v -> p (h v)"), func=AF.Copy),
                 reads=[("st_f", h) for h in range(8)], writes=[("st_b", (n + 1) % 2)])
            P.op("act", lambda e: e.activation(out=o_sb[sl][:, :, cs], in_=pOo.rearrange("p (h t) -> p h t", h=8), func=AF.Copy), reads=[("pOo", par)], writes=[("o_sb", sl)])
            if stp["last"]:
                t0 = gi * 512
                P.dma("sp", OFB[dr].rearrange("(h v) t -> v h t", v=128)[:, :, t0:t0 + 512], o_sb[sl], reads=[("o_sb", sl)], writes=[("OFB", dr, gi)])

        emit_loads(steps[0])
        emit_front(steps[0])
        for i, stp in enumerate(steps):
            if stp["first"] and i + 8 < len(steps):
                emit_loads(steps[i + 8])
            if i + 1 < len(steps):
                emit_front(steps[i + 1])
            emit_back(stp)
    P.barrier()
    a_reset()

    if os.environ.get("KSTOP", "") == "3":
        print("PROG", P.analyze(), flush=True)
        P.emit()
        return nc
    A = alloc_common(2)
    xs, hn = A["xs"], A["hn"]
    yaT = a_bf(4 * 1024).rearrange("p (c t) -> p c t", c=4)
    ybT = a_bf(8 * 1024).rearrange("p (c t) -> p c t", c=8)
    mg = a_bf(8 * 1024).rearrange("p (c t) -> p c t", c=8)
    ul = [a_f32(3 * 520).rearrange("p (g f) -> p g f", g=3) for _ in range(2)]
    rden = a_f32(8)
    yatok = [a_bf(512) for _ in range(2)]
    T3 = [a_f32(512) for _ in range(8)]
    c3 = {"ul": 0, "yatok": 0, "ofl": 0}
    rden2 = [rden, a_f32(8)]
    _g3 = A["gact"]
    _gk = lambda t: [("gact", 0, t), ("gact", 1, t)]
    OFL = [(_g3[:, 2 * i:2 * i + 2, :].bitcast(F32).rearrange("p d t -> p d t"), _gk(2 * i) + _gk(2 * i + 1)) for i in range(4)]
    RS4 = [(_g3[:, 8 + i, :].bitcast(F32), _gk(8 + i)) for i in range(4)]
    SIL4 = [(_g3[:, 12 + i, :].bitcast(F32), _gk(12 + i)) for i in range(4)]
    SQ4 = [(_g3[:, 16 + i // 2, ts(i % 2)], [("gact", i % 2, 16 + i // 2)]) for i in range(4)]
    Ut = U.rearrange("g t f -> t g f")
    wav = w_a.rearrange("(k p) n -> p k n", p=128)
    wbv = w_b.rearrange("(k p) n -> p k n", p=128)
    wov = w_o.rearrange("(k p) n -> p k n", p=128)
    OFBv = OFB.rearrange("d (h v) t -> v d h t", v=128)
    def s3_xload(su_):
        for j in range(2):
            P.dma("sp", xs[:, :, ts(j)], x1Tv[:, :, su_ * 1024 + j * 512:su_ * 1024 + (j + 1) * 512], reads=[("x1T", su_, j)], writes=[("xs", j, c) for c in range(8)])
    s3_xload(0)
    for su in range(NSU):
        t0 = su * 1024
        for j in range(2):
            norm(A, 8, j)
        def ya_ops(tb):
            j = tb // 4
            li = c3["ul"] % 2
            c3["ul"] += 1
            rd = rden2[li]
            u3 = ul[li][:, 0, :].rearrange("p (h e) -> p h e", h=8)
            yi = c3["yatok"] % 2
            c3["yatok"] += 1
            ops = []
            ops.append(lambda: P.dma("sp", ul[li], Ut[t0 + tb * 128:t0 + (tb + 1) * 128, :, :], writes=[("ul", li)]))
            ops.append(lambda: P.op("dve", lambda e: e.tensor_tensor(out=ul[li][:, 0, :], in0=ul[li][:, 0, :], in1=ul[li][:, 1, :], op=ALU.add), reads=[("ul", li)], writes=[("ul", li)]))
            ops.append(lambda: P.op("dve", lambda e: e.tensor_tensor(out=ul[li][:, 0, :], in0=ul[li][:, 0, :], in1=ul[li][:, 2, :], op=ALU.add), reads=[("ul", li)], writes=[("ul", li)]))
            ops.append(lambda: P.op("dve", lambda e: e.reciprocal(out=rd.rearrange("p (h o) -> p h o", o=1), in_=u3[:, :, 64:65]), reads=[("ul", li)], writes=[("rden", li)]))
            ops.append(lambda: P.op("dve", lambda e: e.tensor_tensor(out=yatok[yi].rearrange("p (h e) -> p h e", h=8), in0=u3[:, :, 0:64],
                                                                 in1=rd.rearrange("p (h o) -> p h o", o=1).broadcast_to([128, 8, 64]), op=ALU.mult),
                                    reads=[("ul", li), ("rden", li)], writes=[("yatok", yi)]))

            def trs():
                bt = nb()
                pbf = bank(bt).bitcast(BF16)
                for fc in range(4):
                    P.op("pe", lambda e: e.transpose(out=pbf[:, fc * 128:(fc + 1) * 128], in_=yatok[yi][:, fc * 128:(fc + 1) * 128], identity=ident[:]),
                         reads=[("yatok", yi), "ident"], writes=[("pb", bt)])
                P.op("act", lambda e: e.activation(out=yaT[:, :, tb * 128:(tb + 1) * 128], in_=pbf[:, 0:512].rearrange("p (c t) -> p c t", c=4), func=AF.Copy),
                     reads=[("pb", bt)], writes=[("yaT", j, tb)])
            ops.append(trs)
            return ops

        def yb_ops(h, j, v, s, slotn):
            tg = su * 2 + j
            of_, ofk = OFL[slotn]
            sq_, sqk = SQ4[slotn]
            rs_, rsk = RS4[slotn]
            sl_, slk = SIL4[slotn]
            bg = nb()
            bn = nb()
            for k in range(8):
                P.op("pe", lambda e: e.matmul(bank(bg), lhsT=v[:, k, :], rhs=hn[:, k, ts(j)], start=(k == 0), stop=(k == 7)),
                     reads=[("w", s), ("hn", j, k)], writes=[("pb", bg)])
            ops = []
            ops.append(lambda: P.dma("sp", of_, OFBv[:, :, h, t0 + j * 512:t0 + (j + 1) * 512], reads=[("OFB", 0, tg), ("OFB", 1, tg)], writes=ofk))
            ops.append(lambda: P.op("dve", lambda e: e.tensor_tensor(out=of_[:, 0, :], in0=of_[:, 0, :], in1=of_[:, 1, :], op=ALU.add), reads=ofk, writes=ofk))
            ops.append(lambda: P.op("act", lambda e: e.activation(out=sq_, in_=of_[:, 0, :], func=AF.Square), reads=ofk, writes=sqk))
            ops.append(lambda: P.op("pe", lambda e: e.matmul(bank(bn), lhsT=onesH[:], rhs=sq_, start=True, stop=True), reads=sqk + ["onesH"], writes=[("pb", bn)]))
            ops.append(lambda: P.op("act", lambda e: e.activation(out=rs_, in_=bank(bn), func=AF.Ln, bias=EPS), reads=[("pb", bn)], writes=rsk))
            ops.append(lambda: P.op("act", lambda e: e.activation(out=rs_, in_=rs_, func=AF.Exp, scale=-0.5), reads=rsk, writes=rsk))
            ops.append(lambda: P.op("act", lambda e: e.activation(out=sl_, in_=bank(bg), func=AF.Silu), reads=[("pb", bg)], writes=slk))
            ops.append(lambda: P.op("dve", lambda e: e.scalar_tensor_tensor(out=of_[:, 1, :], in0=of_[:, 0, :], scalar=prm[:, 32 + h:33 + h], in1=rs_, op0=ALU.mult, op1=ALU.mult),
                                    reads=ofk + rsk + ["prm"], writes=ofk))
            ops.append(lambda: P.op("dve", lambda e: e.tensor_tensor(out=ybT[:, h, ts(j)], in0=of_[:, 1, :], in1=sl_, op=ALU.mult), reads=ofk + slk, writes=[("ybT", j, h)]))
            return ops

        def interleave(lists):
            mx = max(len(l_) for l_ in lists)
            for i_ in range(mx):
                for l_ in lists:
                    if i_ < len(l_):
                        l_[i_]()

        for hb in range(4):
            lists = []
            for hh in range(2):
                h = hb * 2 + hh
                view = lambda sl_: sl_[:, 0:1024].rearrange("p (k n) -> p k n", k=8)
                s, slot = wload(A, [(view, winv[:, :, C_HG + h * 128:C_HG + (h + 1) * 128])])
                v = view(slot)
                for j in range(2):
                    lists.append(yb_ops(h, j, v, s, hh * 2 + j))
            interleave(lists)
            interleave([ya_ops(2 * hb), ya_ops(2 * hb + 1)])
        for dc in range(8):
            def view(sl_):
                return (sl_[:, 0:1024].rearrange("p (k n) -> p k n", k=8), sl_[:, 1024:2048].rearrange("p (k n) -> p k n", k=8),
                        sl_[:, 2048:2560].rearrange("p (k n) -> p k n", k=4), sl_[:, 2560:3584].rearrange("p (k n) -> p k n", k=8))
            s, slot = wload(A, [(lambda sl_: view(sl_)[0], winv[:, :, C_GA + dc * 128:C_GA + (dc + 1) * 128]),
                                (lambda sl_: view(sl_)[1], winv[:, :, C_GB + dc * 128:C_GB + (dc + 1) * 128]),
                                (lambda sl_: view(sl_)[2], wav[:, :, dc * 128:(dc + 1) * 128]),
                                (lambda sl_: view(sl_)[3], wbv[:, :, dc * 128:(dc + 1) * 128])])
            vga, vgb, va, vb = view(slot)
            for j in range(2):
                b1, b2, b3, b4 = nb(), nb(), nb(), nb()
                tp_ = 4 * ((dc * 2 + j) % 2)
                for k in range(8):
                    P.op("pe", lambda e: e.matmul(bank(b1), lhsT=vga[:, k, :], rhs=hn[:, k, ts(j)], start=(k == 0), stop=(k == 7)), reads=[("w", s), ("hn", j, k)], writes=[("pb", b1)])
                for k in range(8):
                    P.op("pe", lambda e: e.matmul(bank(b2), lhsT=vgb[:, k, :], rhs=hn[:, k, ts(j)], start=(k == 0), stop=(k == 7)), reads=[("w", s), ("hn", j, k)], writes=[("pb", b2)])
                for k in range(4):
                    P.op("pe", lambda e: e.matmul(bank(b3), lhsT=va[:, k, :], rhs=yaT[:, k, ts(j)], start=(k == 0), stop=(k == 3)),
                         reads=[("w", s)] + [("yaT", j, tb) for tb in range(4 * j, 4 * j + 4)], writes=[("pb", b3)])
                for k in range(8):
                    P.op("pe", lambda e: e.matmul(bank(b4), lhsT=vb[:, k, :], rhs=ybT[:, k, ts(j)], start=(k == 0), stop=(k == 7)), reads=[("w", s), ("ybT", j, k)], writes=[("pb", b4)])
                P.op("act", lambda e: e.activation(out=T3[tp_ + 0], in_=bank(b1), func=AF.Sigmoid), reads=[("pb", b1)], writes=[("T3", tp_ + 0)])
                P.op("act", lambda e: e.activation(out=T3[tp_ + 1], in_=bank(b2), func=AF.Sigmoid), reads=[("pb", b2)], writes=[("T3", tp_ + 1)])
                P.op("dve", lambda e: e.tensor_tensor(out=T3[tp_ + 2], in0=T3[tp_ + 0], in1=bank(b3), op=ALU.mult), reads=[("T3", tp_ + 0), ("pb", b3)], writes=[("T3", tp_ + 2)])
                P.op("dve", lambda e: e.tensor_tensor(out=T3[tp_ + 3], in0=T3[tp_ + 1], in1=bank(b4), op=ALU.mult), reads=[("T3", tp_ + 1), ("pb", b4)], writes=[("T3", tp_ + 3)])
                P.op("dve", lambda e: e.tensor_tensor(out=mg[:, dc, ts(j)], in0=T3[tp_ + 2], in1=T3[tp_ + 3], op=ALU.add), reads=[("T3", tp_ + 2), ("T3", tp_ + 3)], writes=[("mg", j, dc)])
        bs2 = [nb(), nb()]
        nbc["res"].update(bs2)
        pend = None
        for dc in range(8):
            view = lambda sl_: sl_[:, 0:1024].rearrange("p (k n) -> p k n", k=8)
            s, slot = wload(A, [(view, wov[:, :, dc * 128:(dc + 1) * 128])])
            v = view(slot)
            for j in range(2):
                b = nb()
                for k in range(8):
                    P.op("pe", lambda e: e.matmul(bank(b), lhsT=v[:, k, :], rhs=mg[:, k, ts(j)], start=(k == 0), stop=(k == 7)), reads=[("w", s), ("mg", j, k)], writes=[("pb", b)])
                P.op("dve", lambda e: e.tensor_tensor(out=xs[:, dc, ts(j)], in0=xs[:, dc, ts(j)], in1=bank(b), op=ALU.add), reads=[("pb", b), ("xs", j, dc)], writes=[("xs", j, dc)])
            if pend is not None:
                for j in range(2):
                    norm_stat_chunk(A, j, pend, bs2[j])
            pend = dc
        for j in range(2):
            norm_stat_chunk(A, j, pend, bs2[j])
        nbc["res"].difference_update(bs2)
        for j in range(2):
            norm_apply(A, 16, j, bs2[j])
        bs3 = ffn(A, w_gu2, w_d2, stat=True)
        for j in range(2):
            norm_apply(A, 24, j, bs3[j], final=True)
        if su + 1 < NSU:
            s3_xload(su + 1)
        for j in range(2):
            yst = A["gact"][:, 8 * j:8 * j + 8, :].bitcast(F32)
            P.dma("sp", yTv[:, :, t0 + j * 512:t0 + (j + 1) * 512], yst,
                  reads=[("gact", jj, 8 * j + c) for c in range(8) for jj in range(2)], writes=[("yT", su, j)])
    stats = P.analyze()
    print("PROG", stats, "arena peaks", ar.get("peaks"), ar["off"], "of", ARENA_W, flush=True)
    P.emit()
    return nc


_CACHE = {}


def _consts():
    et = np.zeros((6, 128, 8, 128), np.float32)
    p = np.arange(128)[:, None]
    q = np.arange(128)[None, :]
    for g in range(3):
        for kt in range(2):
            rel = p - q - 64 + 128 * kt
            valid = (np.abs(rel) <= 64)
            for h in range(8):
                et[g * 2 + kt, :, h, :] = np.where(valid, np.exp(-(SLOPES[h] * DILS[g]) * np.abs(rel).astype(np.float64)), 0.0)
    s = np.arange(64)[:, None]
    t = np.arange(64)[None, :]
    cm = np.zeros((2, 64, 8, 64), np.float32)
    cm[0] = (t >= s)[:, None, :]
    cm[1] = (t <= s)[:, None, :]
    rm = np.ones((128, 512), np.float32)
    rm[:, ::64] = 0.0
    ident = np.eye(128, dtype=np.float32).astype(ml_dtypes.bfloat16)
    return et.reshape(6, 128, 1024), cm.reshape(2, 64, 512), rm, ident


def kernel(x_prompt, x_sample, ffn1_norm, ffn1_w_gu, ffn1_w_down, mix_norm, w_in, hgrn_lb_fwd, hgrn_lb_bwd, hgrn_norm,
           w_branch_a, w_branch_b, w_out, ffn2_norm, ffn2_w_gu, ffn2_w_down, final_norm):
    x_prompt = np.asarray(x_prompt, np.float32)
    x_sample = np.asarray(x_sample, np.float32)
    S = x_sample.shape[1]
    SEG = S // 2
    assert x_prompt.shape[1] == SEG and x_prompt.shape[0] == 4 and x_sample.shape[0] == 4
    if S not in _CACHE:
        _CACHE[S] = build(S)
    nc = _CACHE[S]
    f = lambda a: np.ascontiguousarray(np.asarray(a, np.float32))
    col = lambda vec: np.asarray(vec, np.float32).reshape(8, 128).T
    et, cm, rm, ident = _consts()
    seqs = []
    for b in range(4):
        seqs.append((x_sample[b], 1.0))
    for i in range(2):
        seqs.append((np.concatenate([x_prompt[2 * i], x_prompt[2 * i + 1]], axis=0), 0.0))
    seqs.append(seqs[4]); seqs.append(seqs[5])
    shared = dict(w_gu1=f(ffn1_w_gu[0]), w_d1=f(ffn1_w_down[0]), w_in=f(w_in[0]), w_a=f(w_branch_a[0]), w_b=f(w_branch_b[0]),
                  w_o=f(w_out[0]), w_gu2=f(ffn2_w_gu[0]), w_d2=f(ffn2_w_down[0]), etab=et, cmask=cm, rmask=rm, ident=ident)
    in_maps = []
    for xs_, lk in seqs:
        prm = np.zeros((128, NPRM), np.float32)
        prm[:, 0:8] = col(ffn1_norm[0]); prm[:, 8:16] = col(mix_norm[0]); prm[:, 16:24] = col(ffn2_norm[0]); prm[:, 24:32] = col(final_norm)
        prm[:, 32:40] = col(hgrn_norm[0])
        prm[:, 40:48] = col(hgrn_lb_fwd[0]); prm[:, 48:56] = col(hgrn_lb_fwd[1])
        prm[:, 56:64] = col(hgrn_lb_bwd[0]); prm[:, 64:72] = col(hgrn_lb_bwd[1])
        prm[:, 72] = lk
        prm[:, 73] = 1.0; prm[:64, 73] = lk
        prm[:, 74] = 1.0; prm[64:, 74] = lk
        m = dict(shared)
        m["xT"] = np.ascontiguousarray(xs_.T)
        m["prm"] = prm
        in_maps.append(m)
    res = run_bass_kernel_spmd(nc, in_maps, core_ids=list(range(8)))
    outs = [np.ascontiguousarray(res.results[c]["yT"].T) for c in range(8)]
    y_sample = np.stack(outs[0:4], axis=0)
    y_prompt = np.stack([outs[4][:SEG], outs[4][SEG:], outs[5][:SEG], outs[5][SEG:]], axis=0)
    return (y_prompt.astype(np.float32), y_sample.astype(np.float32))
```

```python
import math
import os
import numpy as np
import ml_dtypes
import concourse.bass as bass
import concourse.mybir as mybir
from concourse.bass_utils import run_bass_kernel_spmd

F32 = mybir.dt.float32
BF16 = mybir.dt.bfloat16
ALU = mybir.AluOpType
AF = mybir.ActivationFunctionType

ENGS = ("pe", "act", "dve", "pool", "sp")
EPOCH = 12000
NDMASEM = 12


class Op:
    __slots__ = ("eng", "fn", "reads", "writes", "dma", "deps", "inc", "waits", "idx")

    def __init__(self, eng, fn, reads, writes, dma):
        self.eng, self.fn, self.reads, self.writes, self.dma = eng, fn, reads, writes, dma
        self.deps = set()
        self.inc = None
        self.waits = ()


class _Rec:
    def __getattr__(self, name):
        def f(*a, **kw):
            self.call = (name, a, kw)
            return self
        return f


class Prog:
    def __init__(self, nc):
        self.nc = nc
        self.ops = []
        self.barriers = []

    def op(self, eng, fn, reads=(), writes=(), dma=False):
        rec = _Rec()
        fn(rec)
        name, a, kw = rec.call
        o = Op(eng, (lambda e, name=name, a=a, kw=kw: getattr(e, name)(*a, **kw)), tuple(reads), tuple(writes), dma)
        o.idx = len(self.ops)
        self.ops.append(o)
        return o

    def dma(self, q, out, in_, reads=(), writes=()):
        return self.op(q, lambda e: e.dma_start(out=out, in_=in_), reads, writes, dma=True)

    def barrier(self):
        self.barriers.append(len(self.ops))

    def analyze(self):
        ops = self.ops
        last_writer = {}
        readers = {}
        for o in ops:
            deps = set()
            for t in o.reads:
                w = last_writer.get(t)
                if w is not None:
                    deps.add(w)
            for t in o.writes:
                w = last_writer.get(t)
                if w is not None:
                    deps.add(w)
                for r in readers.get(t, ()):
                    deps.add(r)
            for t in o.reads:
                readers.setdefault(t, []).append(o.idx)
            for t in o.writes:
                last_writer[t] = o.idx
                readers[t] = []
            deps.discard(o.idx)
            if o.eng == "pe" and not o.dma:
                deps = {d for d in deps if not (ops[d].eng == "pe" and not ops[d].dma)}
            o.deps = deps
        for bp in self.barriers:
            if bp == 0 or bp >= len(ops):
                continue
            bdeps = set()
            seen_c = set()
            dcnt = {e: 0 for e in ENGS}
            for i in range(bp - 1, -1, -1):
                o = ops[i]
                if o.dma:
                    if dcnt[o.eng] < NDMASEM:
                        dcnt[o.eng] += 1
                        bdeps.add(i)
                elif o.eng not in seen_c:
                    seen_c.add(o.eng)
                    bdeps.add(i)
            first = set()
            for i in range(bp, len(ops)):
                o = ops[i]
                if o.eng not in first:
                    first.add(o.eng)
                    o.deps |= bdeps
                    if len(first) == len(ENGS):
                        break
        has_dep = [False] * len(ops)
        dma_count = {e: 0 for e in ENGS}
        dma_hist = {e: [] for e in ENGS}
        for o in ops:
            if o.dma:
                j = dma_count[o.eng]
                dma_count[o.eng] += 1
                o.inc = (("dma", o.eng, j % NDMASEM), 16 * (j // NDMASEM + 1))
                if j >= NDMASEM:
                    o.deps.add(dma_hist[o.eng][j - NDMASEM])
                dma_hist[o.eng].append(o.idx)
                has_dep[o.idx] = True
        for o in ops:
            for d in o.deps:
                has_dep[d] = True
        cnt = {e: 0 for e in ENGS}
        for o in ops:
            if o.dma:
                continue
            if has_dep[o.idx]:
                c = cnt[o.eng]
                cnt[o.eng] += 1
                o.inc = (("eng", o.eng, c // EPOCH), c % EPOCH + 1)
        known = {e: {} for e in ENGS}
        nw = 0
        for o in ops:
            need = {}
            for d in o.deps:
                k, v = ops[d].inc
                if need.get(k, 0) < v:
                    need[k] = v
            kn = known[o.eng]
            w = []
            for k, v in need.items():
                if kn.get(k, 0) < v:
                    kn[k] = v
                    w.append((k, v))
            o.waits = tuple(w)
            nw += len(w)
        self.semkeys = sorted({o.inc[0] for o in ops if o.inc is not None})
        return dict(n_ops=len(ops), n_waits=nw, n_sems=len(self.semkeys),
                    per_eng={e: sum(1 for o in ops if o.eng == e) for e in ENGS})

    def emit(self):
        from contextlib import ExitStack
        nc = self.nc
        ops = self.ops
        final = {}
        for o in ops:
            if o.inc is not None:
                k, v = o.inc
                if final.get(k, 0) < v:
                    final[k] = v
        with ExitStack() as st:
            sems = {}
            for k in self.semkeys:
                sems[k] = st.enter_context(nc.semaphore("s_%s_%s_%d" % k))
            block = st.enter_context(nc.Block())
            per_eng = {e: [o for o in ops if o.eng == e] for e in ENGS}

            def run(handle, ename):
                for o in per_eng[ename]:
                    for k, v in o.waits:
                        handle.wait_ge(sems[k], v)
                    ins = o.fn(handle)
                    if o.inc is not None:
                        ins.then_inc(sems[o.inc[0]], 16 if o.dma else 1)
                if ename == "sp":
                    for k, v in final.items():
                        handle.wait_ge(sems[k], v)

            @block.tensor
            def _(e):
                run(e, "pe")

            @block.scalar
            def _(e):
                run(e, "act")

            @block.vector
            def _(e):
                run(e, "dve")

            @block.gpsimd
            def _(e):
                run(e, "pool")

            @block.sync
            def _(e):
                run(e, "sp")


D = 1024
DFF = 2816
PAD = 1024
NPRM = 80
EPS = 1e-6
DILS = (1, 4, 16)
SLOPES = tuple(2.0 ** (-8.0 * (j + 1) / 8) for j in range(8))
C_AQ, C_AK, C_AV = 0, 1536, 3072
C_HQ, C_HFF, C_HFB, C_HI, C_HG, C_GA, C_GB = 4608, 5632, 6656, 7680, 8704, 9728, 10752


def build(S):
    SEG = S // 2
    NSU = S // 1024
    NT = S // 512
    NCH = S // 64
    nc = bass.Bass("TRN2", target_bir_lowering=False)

    def din(name, shape, dt=F32):
        return nc.dram_tensor(name, list(shape), dt, kind="ExternalInput").ap()

    xT = din("xT", [D, S])
    w_gu1 = din("w_gu1", [D, 2 * DFF]); w_d1 = din("w_d1", [DFF, D])
    w_in = din("w_in", [D, 11776])
    w_a = din("w_a", [512, D]); w_b = din("w_b", [D, D]); w_o = din("w_o", [D, D])
    w_gu2 = din("w_gu2", [D, 2 * DFF]); w_d2 = din("w_d2", [DFF, D])
    prm_d = din("prm", [128, NPRM])
    etab_d = din("etab", [6, 128, 1024])
    cmask_d = din("cmask", [2, 64, 512])
    rmask_d = din("rmask", [128, 512])
    ident_d = din("ident", [128, 128], BF16)
    yT = nc.dram_tensor("yT", [D, S], F32, kind="ExternalOutput").ap()

    def dscr(name, shape, dt):
        return nc.dram_tensor(name, list(shape), dt).ap()

    x1T = dscr("x1T", [D, S], F32)
    QT = dscr("QT", [1536, S], BF16)
    KT = dscr("KT", [1536, S + 2 * PAD], BF16)
    VA = dscr("VA", [S + 2 * PAD, 24 * 65], BF16)
    HQ = dscr("HQ", [2, D, S], BF16)
    HK = dscr("HK", [2, D, S], BF16)
    HKT = dscr("HKT", [2, 8, S, 128], BF16)
    HV = dscr("HV", [S, D], BF16)
    HG = dscr("HG", [D, S], BF16)
    U = dscr("U", [3, S, 520], F32)
    OFB = dscr("OFB", [2, D, S], F32)

    P = Prog(nc)

    prm = nc.alloc_sbuf_tensor("prm_s", [128, NPRM], F32)
    lbs = nc.alloc_sbuf_tensor("lbs", [128, 16], F32)
    oml = nc.alloc_sbuf_tensor("oml", [128, 16], F32)
    noml = nc.alloc_sbuf_tensor("noml", [128, 16], F32)
    onesD = nc.alloc_sbuf_tensor("onesD", [128, 128], BF16)
    onesH = nc.alloc_sbuf_tensor("onesH", [128, 128], BF16)
    ident = nc.alloc_sbuf_tensor("ident_s", [128, 128], BF16)
    rmask = nc.alloc_sbuf_tensor("rmask_s", [128, 512], F32)
    dec_sb = nc.alloc_sbuf_tensor("dec_sb", [128, 2, 8, NCH], F32)
    zt_t = nc.alloc_sbuf_tensor("zt_t", [128, 1560], BF16)
    PB = [nc.alloc_psum_tensor("pb%d" % i, [128, 1024], F32) for i in range(4)]
    ARENA_W = (nc.sbuf_bytes_remaining - 2048) // 4
    arena = nc.alloc_sbuf_tensor("arena", [128, ARENA_W], F32)
    ar = {"off": 0}

    def a_f32(n):
        o = ar["off"]
        ar["off"] = o + n
        assert ar["off"] <= ARENA_W, ("arena overflow", ar["off"], ARENA_W)
        return arena[:, o:o + n]

    def a_bf(n):
        return a_f32((n + 1) // 2).bitcast(BF16)

    def a_reset():
        ar["off"] = 0

    def bank(i):
        return PB[i // 2][:, (i % 2) * 512:(i % 2 + 1) * 512]

    nbc = {"i": 0}

    def nb():
        i = nbc["i"]
        nbc["i"] = (i + 1) % 8
        return i

    def ts(j):
        return slice(j * 512, (j + 1) * 512)

    P.dma("sp", prm[:], prm_d, writes=["prm"])
    P.dma("sp", ident[:], ident_d, writes=["ident"])
    P.dma("sp", rmask[:], rmask_d, writes=["rmask"])
    P.op("dve", lambda e: e.memset(onesD[:], 1.0 / 1024.0), writes=["onesD"])
    P.op("dve", lambda e: e.memset(onesH[:], 1.0 / 128.0), writes=["onesH"])
    P.op("dve", lambda e: e.tensor_tensor(out=lbs[:, 0:8], in0=prm[:, 40:48], in1=prm[:, 48:56], op=ALU.subtract), reads=["prm"], writes=["lbs"])
    P.op("dve", lambda e: e.tensor_tensor(out=lbs[:, 8:16], in0=prm[:, 56:64], in1=prm[:, 64:72], op=ALU.subtract), reads=["prm", "lbs"], writes=["lbs"])
    P.op("act", lambda e: e.activation(out=lbs[:], in_=lbs[:], func=AF.Sigmoid), reads=["lbs"], writes=["lbs"])
    P.op("dve", lambda e: e.tensor_scalar(out=oml[:], in0=lbs[:], scalar1=-1.0, scalar2=1.0, op0=ALU.mult, op1=ALU.add), reads=["lbs"], writes=["oml"])
    P.op("dve", lambda e: e.tensor_scalar(out=noml[:], in0=lbs[:], scalar1=1.0, scalar2=-1.0, op0=ALU.mult, op1=ALU.add), reads=["lbs"], writes=["noml"])
    link = prm[:, 72:73]
    P.op("dve", lambda e: e.memset(zt_t[:], 0.0), writes=["zt"])

    def emit_pads():
        for c in range(12):
            for side in range(2):
                c0 = 0 if side == 0 else PAD + S
                P.dma("sp", KT[c * 128:(c + 1) * 128, c0:c0 + PAD], zt_t[:, 0:PAD], reads=["zt"], writes=[("KTpad", c, side)])
        for side in range(2):
            r0 = 0 if side == 0 else PAD + S
            for rb in range(PAD // 128):
                P.dma("sp", VA[r0 + rb * 128:r0 + (rb + 1) * 128, :], zt_t[:, 0:1560], reads=["zt"], writes=[("VApad", side, rb)])

    def alloc_common(nw=3):
        A = {}
        A["xs"] = a_f32(8 * 1024).rearrange("p (c t) -> p c t", c=8)
        A["hn"] = a_bf(8 * 1024).rearrange("p (c t) -> p c t", c=8)
        A["gact"] = a_bf(22 * 1024).rearrange("p (c t) -> p c t", c=22)
        A["w"] = [a_bf(4096) for _ in range(nw)]
        A["sq"] = [a_bf(512) for _ in range(3)]
        A["rs"] = [a_f32(512) for _ in range(2)]
        A["sil"] = [a_f32(512) for _ in range(2)]
        A["cnt"] = {"w": 0, "sq": 0, "rs": 0, "sil": 0}
        return A

    def rot(A, name, n):
        i = A["cnt"][name] % n
        A["cnt"][name] += 1
        return i

    def norm(A, gcol, j, final=False):
        xs, hn = A["xs"], A["hn"]
        b = nb()
        for c in range(8):
            si = rot(A, "sq", 3)
            P.op("act", lambda e, c=c, si=si: e.activation(out=A["sq"][si], in_=xs[:, c, ts(j)], func=AF.Square),
                 reads=[("xs", j, c)], writes=[("sq", si)])
            P.op("pe", lambda e, c=c, si=si, b=b: e.matmul(bank(b), lhsT=onesD[:], rhs=A["sq"][si], start=(c == 0), stop=(c == 7)),
                 reads=[("sq", si), "onesD"], writes=[("pb", b)])
        ri = rot(A, "rs", 2)
        P.op("act", lambda e, b=b, ri=ri: e.activation(out=A["rs"][ri], in_=bank(b), func=AF.Ln, bias=EPS),
             reads=[("pb", b)], writes=[("rs", ri)])
        P.op("act", lambda e, ri=ri: e.activation(out=A["rs"][ri], in_=A["rs"][ri], func=AF.Exp, scale=-0.5),
             reads=[("rs", ri)], writes=[("rs", ri)])
        for c in range(8):
            if final:
                yc = A["gact"][:, 8 * j + c, :].bitcast(F32)
                P.op("dve", lambda e, c=c, ri=ri, yc=yc: e.scalar_tensor_tensor(out=yc, in0=xs[:, c, ts(j)], scalar=prm[:, gcol + c:gcol + c + 1],
                                                                              in1=A["rs"][ri], op0=ALU.mult, op1=ALU.mult),
                     reads=[("xs", j, c), ("rs", ri), "prm"], writes=[("gact", 0, 8 * j + c), ("gact", 1, 8 * j + c)])
            else:
                P.op("dve", lambda e, c=c, ri=ri: e.scalar_tensor_tensor(out=hn[:, c, ts(j)], in0=xs[:, c, ts(j)], scalar=prm[:, gcol + c:gcol + c + 1],
                                                                       in1=A["rs"][ri], op0=ALU.mult, op1=ALU.mult),
                     reads=[("xs", j, c), ("rs", ri), "prm"], writes=[("hn", j, c)])

    def wload(A, pieces):
        s = rot(A, "w", len(A["w"]))
        slot = A["w"][s]
        for dv, src in pieces:
            P.dma("pool", dv(slot), src, writes=[("w", s)])
        return s, slot

    def ffn(A, w_gu, w_d, hook=None):
        xs, hn, gact = A["xs"], A["hn"], A["gact"]
        wguv = w_gu.rearrange("(k p) n -> p k n", p=128)
        wdv = w_d.rearrange("(k p) n -> p k n", p=128)
        for i in range(11):
            view = lambda sl: sl.rearrange("p (k a n) -> p k a n", k=8, a=2)
            s, slot = wload(A, [(lambda sl: view(sl)[:, :, 0, :], wguv[:, :, i * 256:(i + 1) * 256]),
                                (lambda sl: view(sl)[:, :, 1, :], wguv[:, :, DFF + i * 256:DFF + (i + 1) * 256])])
            v = view(slot)
            for j in range(2):
                for m2 in range(2):
                    m = 2 * i + m2
                    ba, bb = nb(), nb()
                    for k in range(8):
                        P.op("pe", lambda e, k=k, ba=ba, m2=m2, j=j, v=v: e.matmul(bank(ba), lhsT=v[:, k, 0, m2 * 128:(m2 + 1) * 128], rhs=hn[:, k, ts(j)],
                                                                              start=(k == 0), stop=(k == 7)),
                             reads=[("w", s), ("hn", j, k)], writes=[("pb", ba)])
                    for k in range(8):
                        P.op("pe", lambda e, k=k, bb=bb, m2=m2, j=j, v=v: e.matmul(bank(bb), lhsT=v[:, k, 1, m2 * 128:(m2 + 1) * 128], rhs=hn[:, k, ts(j)],
                                                                              start=(k == 0), stop=(k == 7)),
                             reads=[("w", s), ("hn", j, k)], writes=[("pb", bb)])
                    ti = rot(A, "sil", 2)
                    P.op("act", lambda e, ba=ba, ti=ti: e.activation(out=A["sil"][ti], in_=bank(ba), func=AF.Silu),
                         reads=[("pb", ba)], writes=[("sil", ti)])
                    P.op("dve", lambda e, bb=bb, ti=ti, m=m, j=j: e.tensor_tensor(out=gact[:, m, ts(j)], in0=A["sil"][ti], in1=bank(bb), op=ALU.mult),
                         reads=[("sil", ti), ("pb", bb)], writes=[("gact", j, m)])
            if hook is not None:
                hook()
        for mo in range(8):
            view = lambda sl: sl[:, 0:2816].rearrange("p (k n) -> p k n", k=22)
            s, slot = wload(A, [(view, wdv[:, :, mo * 128:(mo + 1) * 128])])
            v = view(slot)
            for j in range(2):
                b = nb()
                for k in range(22):
                    P.op("pe", lambda e, k=k, b=b, j=j, v=v: e.matmul(bank(b), lhsT=v[:, k, :], rhs=gact[:, k, ts(j)], start=(k == 0), stop=(k == 21)),
                         reads=[("w", s), ("gact", j, k)], writes=[("pb", b)])
                P.op("dve", lambda e, b=b, j=j, mo=mo: e.scalar_tensor_tensor(out=xs[:, mo, ts(j)], in0=bank(b), scalar=0.5, in1=xs[:, mo, ts(j)],
                                                                          op0=ALU.mult, op1=ALU.add),
                     reads=[("pb", b), ("xs", j, mo)], writes=[("xs", j, mo)])
            if hook is not None:
                hook()

    winv = w_in.rearrange("(k p) n -> p k n", p=128)
    xTv = xT.rearrange("(c p) t -> p c t", p=128)
    x1Tv = x1T.rearrange("(c p) t -> p c t", p=128)
    yTv = yT.rearrange("(c p) t -> p c t", p=128)

    if os.environ.get("KSTOP", "") == "0":
        print("PROG", P.analyze(), flush=True)
        P.emit()
        return nc
    A = alloc_common()
    kendT = [a_bf(512) for _ in range(3)]
    X = [[a_f32(512) for _ in range(6)] for _ in range(2)]
    sgq = a_f32(512)
    qsb = a_f32(512)
    qst = [a_bf(512) for _ in range(3)]
    hqst = [a_bf(512) for _ in range(4)]
    hkst = [a_bf(512) for _ in range(4)]
    vst = [a_bf(520).rearrange("p (h e) -> p h e", h=8) for _ in range(3)]
    hvst = [a_bf(512) for _ in range(2)]
    hgst = [a_bf(512) for _ in range(2)]
    kst = [a_bf(512) for _ in range(2)]
    A["cnt"].update(qst=0, hqst=0, hkst=0, vst=0, hvst=0, kst=0, kendT=0, hgst=0)
    _g = A["gact"]
    _gt = lambda t: (_g[:, t, :].bitcast(F32), [("gact", 0, t), ("gact", 1, t)])
    TS = [dict(sgq=(sgq, ["sgq"]), qsb=(qsb, ["qsb"]), X=[[(X[dr][i], [("X", dr, i)]) for i in range(6)] for dr in range(2)]),
          dict(sgq=_gt(12), qsb=_gt(13), X=[[_gt(dr * 6 + i) for i in range(6)] for dr in range(2)])]
    KEND = [(_g[:, 14 + b_ // 2, ts(b_ % 2)], ("gact", b_ % 2, 14 + b_ // 2)) for b_ in range(16)]
    for i in range(3):
        P.op("dve", lambda e, i=i: e.memset(vst[i][:, :, 64:65], 1.0), writes=[("vst", i)])

    def s1_xload(su_):
        for j in range(2):
            P.dma("sp", A["xs"][:, :, ts(j)], xTv[:, :, su_ * 1024 + j * 512:su_ * 1024 + (j + 1) * 512], writes=[("xs", j, c) for c in range(8)])
    _nsu1 = 0 if os.environ.get('KSKIP1') else NSU
    if _nsu1:
        s1_xload(0)
    emit_pads()
    for su in range(_nsu1):
        t0 = su * 1024
        for j in range(2):
            norm(A, 0, j)
        ffn(A, w_gu1, w_d1)
        for j in range(2):
            P.dma("sp", x1Tv[:, :, t0 + j * 512:t0 + (j + 1) * 512], A["xs"][:, :, ts(j)], reads=[("xs", j, c) for c in range(8)], writes=[("x1T", su, j)])
        for j in range(2):
            norm(A, 8, j)
        if su + 1 < _nsu1:
            s1_xload(su + 1)
        hn = A["hn"]
        def aqk_block(which, blk):
            cbase = C_AQ if which == 0 else C_AK
            view = lambda sl: sl[:, 0:2048].rearrange("p (k n) -> p k n", k=8)
            s, slot = wload(A, [(view, winv[:, :, cbase + blk * 256:cbase + (blk + 1) * 256])])
            v = view(slot)
            for m2 in range(2):
                ch = blk * 2 + m2
                for j in range(2):
                    b = nb()
                    for k in range(8):
                        P.op("pe", lambda e: e.matmul(bank(b), lhsT=v[:, k, m2 * 128:(m2 + 1) * 128], rhs=hn[:, k, ts(j)], start=(k == 0), stop=(k == 7)),
                             reads=[("w", s), ("hn", j, k)], writes=[("pb", b)])
                    qi = rot(A, "qst", 3)
                    if which == 0:
                        P.op("act", lambda e: e.activation(out=qst[qi], in_=bank(b), func=AF.Copy, scale=0.125), reads=[("pb", b)], writes=[("qst", qi)])
                        P.dma("sp", QT[ch * 128:(ch + 1) * 128, t0 + j * 512:t0 + (j + 1) * 512], qst[qi], reads=[("qst", qi)], writes=[("QT", ch, su, j)])
                    else:
                        P.op("dve", lambda e: e.tensor_copy(out=qst[qi], in_=bank(b)), reads=[("pb", b)], writes=[("qst", qi)])
                        P.dma("sp", KT[ch * 128:(ch + 1) * 128, PAD + t0 + j * 512:PAD + t0 + (j + 1) * 512], qst[qi], reads=[("qst", qi)], writes=[("KT", ch, su, j)])

        def tok_block(i):
            cb = C_AV + i * 512 if i < 3 else C_HI + (i - 3) * 512
            view = lambda sl: sl.rearrange("p (k n) -> p k n", k=8)
            s, slot = wload(A, [(view, winv[:, :, cb:cb + 512])])
            v = view(slot)
            for tb in range(8):
                b = nb()
                j = tb // 4
                for k in range(8):
                    P.op("pe", lambda e: e.matmul(bank(b), lhsT=hn[:, k, tb * 128:(tb + 1) * 128], rhs=v[:, k, :], start=(k == 0), stop=(k == 7)),
                         reads=[("w", s), ("hn", j, k)], writes=[("pb", b)])
                if i < 3:
                    vi = rot(A, "vst", 3)
                    P.op("act", lambda e: e.activation(out=vst[vi][:, :, 0:64], in_=bank(b).rearrange("p (h e) -> p h e", h=8), func=AF.Copy),
                         reads=[("pb", b)], writes=[("vst", vi)])
                    P.dma("sp", VA[PAD + t0 + tb * 128:PAD + t0 + (tb + 1) * 128, i * 520:(i + 1) * 520], vst[vi].rearrange("p h e -> p (h e)"),
                          reads=[("vst", vi)], writes=[("VA", i, su, tb)])
                else:
                    vi = rot(A, "hvst", 2)
                    P.op("dve", lambda e: e.tensor_copy(out=hvst[vi], in_=bank(b)), reads=[("pb", b)], writes=[("hvst", vi)])
                    P.dma("sp", HV[t0 + tb * 128:t0 + (tb + 1) * 128, (i - 3) * 512:(i - 2) * 512], hvst[vi], reads=[("hvst", vi)], writes=[("HV", i, su, tb)])

        def hgrn_proj(h):
            view = lambda sl: sl[:, 0:4096].rearrange("p (k a n) -> p k a n", k=8, a=4)
            s, slot = wload(A, [(lambda sl, a=a: view(sl)[:, :, a, :], winv[:, :, cb + h * 128:cb + (h + 1) * 128])
                                for a, cb in enumerate((C_HQ, C_HFF, C_HFB, C_HG))])
            v = view(slot)
            for j in range(2):
                bg = nb()
                for k in range(8):
                    P.op("pe", lambda e: e.matmul(bank(bg), lhsT=v[:, k, 3, :], rhs=hn[:, k, ts(j)], start=(k == 0), stop=(k == 7)),
                         reads=[("w", s), ("hn", j, k)], writes=[("pb", bg)])
                gi_ = rot(A, "hgst", 2)
                P.op("act", lambda e: e.activation(out=hgst[gi_], in_=bank(bg), func=AF.Silu), reads=[("pb", bg)], writes=[("hgst", gi_)])
                P.dma("sp", HG[h * 128:(h + 1) * 128, t0 + j * 512:t0 + (j + 1) * 512], hgst[gi_], reads=[("hgst", gi_)], writes=[("HGs", h, su * 2 + j)])
            out = []
            for j in range(2):
                bks = [nb(), nb(), nb()]
                for a in range(3):
                    for k in range(8):
                        P.op("pe", lambda e: e.matmul(bank(bks[a]), lhsT=v[:, k, a, :], rhs=hn[:, k, ts(j)], start=(k == 0), stop=(k == 7)),
                             reads=[("w", s), ("hn", j, k)], writes=[("pb", bks[a])])
                out.append(bks)
            return out

        def q_ops(h, j, bq):
            T = TS[j]
            sgq_, kq1 = T["sgq"]
            qsb_, kq2 = T["qsb"]
            return [lambda: P.op("act", lambda e: e.activation(out=sgq_, in_=bank(bq), func=AF.Sigmoid), reads=[("pb", bq)], writes=kq1),
                    lambda: P.op("dve", lambda e: e.tensor_tensor(out=qsb_, in0=bank(bq), in1=sgq_, op=ALU.mult), reads=[("pb", bq)] + kq1, writes=kq2)]

        def chain_ops(h, j, dr, bk, kei):
            T = TS[j]
            tg = su * 2 + j
            x = [T["X"][dr][i][0] for i in range(6)]
            xk = [T["X"][dr][i][1] for i in range(6)]
            qsb_, kq2 = T["qsb"]
            col = dr * 8 + h
            kb, kbkey = KEND[kei]
            ops = []
            ops.append(lambda: P.op("act", lambda e: e.activation(out=x[0], in_=bank(bk), func=AF.Sigmoid), reads=[("pb", bk)], writes=xk[0]))
            ops.append(lambda: P.op("act", lambda e: e.activation(out=x[1], in_=x[0], func=AF.Ln, scale=oml[:, col:col + 1], bias=lbs[:, col:col + 1]),
                                    reads=xk[0] + ["oml", "lbs"], writes=xk[1]))
            ops.append(lambda: P.op("dve", lambda e: e.tensor_scalar(out=x[2], in0=x[0], scalar1=noml[:, col:col + 1], scalar2=oml[:, col:col + 1], op0=ALU.mult, op1=ALU.add),
                                    reads=xk[0] + ["oml", "noml"], writes=xk[2]))
            ops.append(lambda: P.op("dve", lambda e: e.tensor_tensor_scan(out=x[3], data0=rmask[:], data1=x[1], initial=0.0, op0=ALU.mult, op1=ALU.add),
                                    reads=xk[1] + ["rmask"], writes=xk[3]))
            if dr == 0:
                bfin, bkey = x[3], xk[3]
            else:
                x33 = x[3].rearrange("p (c t) -> p c t", t=64)
                x43 = x[4].rearrange("p (c t) -> p c t", t=64)
                ops.append(lambda: P.op("dve", lambda e: e.scalar_tensor_tensor(out=x[4], in0=x[3], scalar=-1.0, in1=x[1], op0=ALU.mult, op1=ALU.add),
                                        reads=xk[3] + xk[1], writes=xk[4]))
                ops.append(lambda: P.op("dve", lambda e: e.tensor_tensor(out=x43, in0=x43, in1=x33[:, :, 63:64].broadcast_to([128, 8, 64]), op=ALU.add),
                                        reads=xk[3] + xk[4], writes=xk[4]))
                bfin, bkey = x[4], xk[4]
            ops.append(lambda: P.op("act", lambda e: e.activation(out=x[5], in_=bfin, func=AF.Exp), reads=bkey, writes=xk[5]))
            ops.append(lambda: P.op("act", lambda e: e.activation(out=x[0], in_=bfin, func=AF.Exp, scale=-1.0), reads=bkey, writes=xk[0]))
            eb3 = x[5].rearrange("p (c t) -> p c t", t=64)
            dsel = eb3[:, :, 63:64] if dr == 0 else eb3[:, :, 0:1]
            ops.append(lambda: P.op("dve", lambda e: e.tensor_copy(out=dec_sb[:, dr, h, tg * 8:(tg + 1) * 8].rearrange("p (c o) -> p c o", o=1), in_=dsel),
                                    reads=xk[5], writes=[("dec", dr, h, tg)]))

            def qtil():
                qi = rot(A, "hqst", 4)
                P.op("dve", lambda e: e.tensor_tensor(out=hqst[qi], in0=qsb_, in1=x[5], op=ALU.mult), reads=kq2 + xk[5], writes=[("hqst", qi)])
                P.dma("sp", HQ[dr, h * 128:(h + 1) * 128, t0 + j * 512:t0 + (j + 1) * 512], hqst[qi], reads=[("hqst", qi)], writes=[("HQ", dr, h, tg)])
            ops.append(qtil)
            ops.append(lambda: P.op("dve", lambda e: e.tensor_tensor(out=x[1], in0=x[2], in1=x[0], op=ALU.mult), reads=xk[2] + xk[0], writes=xk[1]))

            def ktil():
                ki = rot(A, "hkst", 4)
                P.op("act", lambda e: e.activation(out=hkst[ki], in_=x[1], func=AF.Copy), reads=xk[1], writes=[("hkst", ki)])
                P.dma("sp", HK[dr, h * 128:(h + 1) * 128, t0 + j * 512:t0 + (j + 1) * 512], hkst[ki], reads=[("hkst", ki)], writes=[("HK", dr, h, tg)])
            ops.append(ktil)
            x13 = x[1].rearrange("p (c t) -> p c t", t=64)
            ops.append(lambda: P.op("dve", lambda e: e.tensor_tensor(out=kb.rearrange("p (c t) -> p c t", t=64), in0=x13, in1=dsel.broadcast_to([128, 8, 64]), op=ALU.mult),
                                    reads=xk[1] + xk[5], writes=[kbkey]))
            return ops

        def hgrn_tr(h, info):
            for (j, dr, kei) in info:
                tg = su * 2 + j
                kb, kbkey = KEND[kei]
                bt = nb()
                pbf = bank(bt).bitcast(BF16)
                for jb in range(4):
                    P.op("pe", lambda e: e.transpose(out=pbf[:, jb * 128:(jb + 1) * 128], in_=kb[:, jb * 128:(jb + 1) * 128], identity=ident[:]),
                         reads=[kbkey, "ident"], writes=[("pb", bt)])
                ksi = rot(A, "kst", 2)
                P.op("act", lambda e: e.activation(out=kst[ksi], in_=pbf[:, 0:512], func=AF.Copy), reads=[("pb", bt)], writes=[("kst", ksi)])
                P.dma("sp", HKT[dr, h, t0 + j * 512:t0 + (j + 1) * 512, :].rearrange("(jb p) k -> p jb k", p=128), kst[ksi].rearrange("p (jb k) -> p jb k", jb=4),
                      reads=[("kst", ksi)], writes=[("HKT", dr, h, tg)])

        other = [("aqk", 0, b_) for b_ in range(6)] + [("aqk", 1, b_) for b_ in range(6)] + [("tok", i_) for i_ in range(5)]

        def emit_other(n):
            for _ in range(n):
                if other:
                    it = other.pop(0)
                    if it[0] == "aqk":
                        aqk_block(it[1], it[2])
                    else:
                        tok_block(it[1])
        pend = {}
        for h in range(8):
            bks = hgrn_proj(h)
            lists = []
            info = []
            qlists = []
            for j in range(2):
                qlists.append(q_ops(h, j, bks[j][0]))
                for dr in range(2):
                    kei = A["cnt"]["kendT"] % len(KEND)
                    A["cnt"]["kendT"] += 1
                    lists.append(chain_ops(h, j, dr, bks[j][1 + dr], kei))
                    info.append((j, dr, kei))
            for i_ in range(2):
                for l_ in qlists:
                    l_[i_]()
            for l_ in lists:
                l_[0]()
            emit_other(2)
            mx = max(len(l_) for l_ in lists)
            for i_ in range(1, mx):
                for l_ in lists:
                    if i_ < len(l_):
                        l_[i_]()
            pend[h] = info
            if h - 2 in pend:
                hgrn_tr(h - 2, pend.pop(h - 2))
        emit_other(len(other))
        for h in sorted(pend):
            hgrn_tr(h, pend[h])
    P.barrier()
    a_reset()

    if os.environ.get("KSTOP", "") == "1":
        print("PROG", P.analyze(), flush=True)
        P.emit()
        return nc
    Qg = a_bf(4 * S).rearrange("p (c t) -> p c t", c=4)
    Kg = a_bf(4 * (S + 2 * PAD)).rearrange("p (c t) -> p c t", c=4)
    Eg = a_f32(2 * 1024).rearrange("p (k n) -> p k n", k=2)
    pex = [a_f32(1024) for _ in range(4)]
    pT = [a_bf(1024) for _ in range(4)]
    vt = [a_bf(520).rearrange("p (h e) -> p h e", h=8) for _ in range(4)]
    ust = [a_f32(520) for _ in range(2)]
    Qz = [[a_bf(512).rearrange("p (c q) -> p c q", c=4) for _ in range(2)] for _ in range(2)]
    for bq in range(2):
        for hp in range(2):
            P.op("dve", lambda e: e.memset(Qz[bq][hp], 0.0), writes=[("Qz", bq, hp)])
    cnt2 = {"pex": 0, "pT": 0, "vt": 0, "ust": 0, "po": 0, "qz": 0}
    _katt = [int(v) for v in os.environ.get('KATT', '3,99,9999').split(',')]
    for g in range(min(3, _katt[0])):
        d = DILS[g]
        L = S // d
        PADd = PAD // d
        Bs = SEG // d
        for c in range(4):
            P.dma("sp", Qg[:, c, :], QT[g * 512 + c * 128:g * 512 + (c + 1) * 128, :], writes=[("Qg", c)])
            P.dma("sp", Kg[:, c, :], KT[g * 512 + c * 128:g * 512 + (c + 1) * 128, :], writes=[("Kg", c)])
        for kt in range(2):
            P.dma("sp", Eg[:, kt, :], etab_d[g * 2 + kt], writes=[("Eg", kt)])
        Qv = Qg.rearrange("p c (i r) -> p c r i", r=d)
        Kv = Kg.rearrange("p c (i r) -> p c r i", r=d)
        VAv = VA.rearrange("(i r) f -> r i f", r=d)
        Uv = U[g].rearrange("(i r) f -> r i f", r=d)
        blocks = [(r_, m_) for r_ in range(min(d, _katt[1])) for m_ in range(min(L // 128, _katt[2]))]
        vslots = {}

        qzof = {}

        def att_qz(r, m):
            bq = cnt2["qz"] % 2
            cnt2["qz"] += 1
            qzof[(r, m)] = bq
            for hp in range(2):
                P.op("dve",
                     lambda e: e.tensor_copy(out=Qz[bq][hp][hp * 64:(hp + 1) * 64, :, :], in_=Qv[hp * 64:(hp + 1) * 64, :, r, 128 * m:128 * m + 128]),
                     reads=[("Qg", c_) for c_ in range(4)], writes=[("Qz", bq, hp)])

        def att_front(r, m, nb_=None):
            pts = []
            if (r, m) not in qzof:
                att_qz(r, m)
            bq = qzof[(r, m)]
            if nb_ is not None:
                att_qz(*nb_)
            for kt in range(2):
                jt = m + kt
                if (r, jt) not in vslots:
                    vi = cnt2["vt"] % 4
                    cnt2["vt"] += 1
                    vslots[(r, jt)] = vi
                    i0 = PADd + 128 * jt - 64
                    P.dma("sp", vt[vi].rearrange("p h e -> p (h e)"), VAv[r, i0:i0 + 128, g * 520:(g + 1) * 520], writes=[("vt", vi)])
                k0 = PADd + 128 * jt - 64
                for h in (0, 2, 4, 6, 1, 3, 5, 7):
                    c, hp = h // 2, h % 2
                    P.op("pe", lambda e: e.matmul(PB[kt][:, h * 128:(h + 1) * 128], lhsT=Kv[:, c, r, k0:k0 + 128],
                                                  rhs=Qz[bq][hp][:, c, :], start=True, stop=True),
                         reads=[("Kg", c), ("Qz", bq, hp)], writes=[("pS", kt, hp)])
                pi = cnt2["pex"] % 4
                cnt2["pex"] += 1
                for hb in range(2):
                    P.op("act", lambda e: e.activation(out=pex[pi][:, hb * 512:(hb + 1) * 512], in_=PB[kt][:, hb * 512:(hb + 1) * 512], func=AF.Exp),
                         reads=[("pS", kt, 0), ("pS", kt, 1)], writes=[("pex", pi)])
                ti = cnt2["pT"] % 4
                cnt2["pT"] += 1
                if 128 * jt == Bs:
                    lc = 74 if kt == 1 else 73
                    P.op("dve", lambda e: e.scalar_tensor_tensor(out=pT[ti], in0=pex[pi], scalar=prm[:, lc:lc + 1], in1=Eg[:, kt, :], op0=ALU.mult, op1=ALU.mult),
                         reads=[("pex", pi), ("Eg", kt), "prm"], writes=[("pT", ti)])
                else:
                    P.op("dve", lambda e: e.tensor_tensor(out=pT[ti], in0=pex[pi], in1=Eg[:, kt, :], op=ALU.mult), reads=[("pex", pi), ("Eg", kt)], writes=[("pT", ti)])
                pts.append(ti)
            return pts

        def att_back(r, m, pts):
            po = cnt2["po"] % 2
            cnt2["po"] += 1
            pO = PB[2 + po]
            for h in range(8):
                col = (h // 4) * 512 + (h % 4) * 65
                for kt in range(2):
                    vsl = vslots[(r, m + kt)]
                    P.op("pe", lambda e: e.matmul(pO[:, col:col + 65], lhsT=pT[pts[kt]][:, h * 128:(h + 1) * 128], rhs=vt[vsl][:, h, :],
                                                  start=(kt == 0), stop=(kt == 1)),
                         reads=[("pT", pts[kt]), ("vt", vsl)], writes=[("pO", po)])
            ui = cnt2["ust"] % 2
            cnt2["ust"] += 1
            for hb in range(2):
                P.op("act", lambda e: e.activation(out=ust[ui][:, hb * 260:(hb + 1) * 260], in_=pO[:, hb * 512:hb * 512 + 260], func=AF.Copy),
                     reads=[("pO", po)], writes=[("ust", ui)])
            P.dma("sp", Uv[r, 128 * m:128 * m + 128, :], ust[ui], reads=[("ust", ui)], writes=[("U", g, r, m)])

        nxt = att_front(*blocks[0], nb_=(blocks[1] if len(blocks) > 1 else None))
        for bi, (r, m) in enumerate(blocks):
            cur = nxt
            if bi + 1 < len(blocks):
                nxt = att_front(*blocks[bi + 1], nb_=(blocks[bi + 2] if bi + 2 < len(blocks) else None))
            att_back(r, m, cur)
    P.barrier()
    a_reset()

    if os.environ.get("KSTOP", "") == "2":
        print("PROG", P.analyze(), flush=True)
        P.emit()
        return nc
    qT_s = [a_bf(8 * 512).rearrange("p (h t) -> p h t", h=8) for _ in range(2)]
    kT_s = [a_bf(8 * 512).rearrange("p (h t) -> p h t", h=8) for _ in range(2)]
    ktok_s = [a_bf(8 * 1024).rearrange("p (c f) -> p c f", c=8) for _ in range(2)]
    v_s = [a_bf(8 * 1024).rearrange("p (c f) -> p c f", c=8) for _ in range(2)]
    st_f = a_f32(1024).rearrange("p (h v) -> p h v", h=8)
    st_b = a_bf(1024).rearrange("p (h v) -> p h v", h=8)
    atm = [a_bf(512).rearrange("p (h t) -> p h t", h=8) for _ in range(2)]
    o_sb = [a_f32(8 * 512).rearrange("p (h t) -> p h t", h=8) for _ in range(2)]
    cm = a_f32(2 * 512).rearrange("p (d n) -> p d n", d=2)
    for dr in range(2):
        P.dma("sp", cm[0:64, dr, :], cmask_d[dr], writes=["cm"])
    st_b2 = [st_b, a_bf(1024).rearrange("p (h v) -> p h v", h=8)]
    gcount = 0
    nseq = 0
    for dr in (1, 0):
        for h in range(8):
            P.op("dve", lambda e: e.memset(st_f[:, h, :], 0.0), writes=[("st_f", h)])
        P.op("dve", lambda e: e.memset(st_b2[nseq % 2], 0.0), writes=[("st_b", nseq % 2)])
        glist = list(range(NT)) if dr == 0 else list(range(NT - 1, -1, -1))
        steps = []
        for gi in glist:
            sl = gcount % 2
            gcount += 1
            clist = list(range(8)) if dr == 0 else list(range(7, -1, -1))
            for ci, c in enumerate(clist):
                steps.append(dict(gi=gi, sl=sl, c=c, first=(ci == 0), last=(ci == 7), n=nseq, gidx=len(steps) // 8))
                nseq += 1

        def emit_loads(stp):
            gi, sl = stp["gi"], stp["sl"]
            t0 = gi * 512
            P.dma("sp", qT_s[sl], HQ[dr].rearrange("(h k) t -> k h t", k=128)[:, :, t0:t0 + 512], reads=[("HQ", dr, h, gi) for h in range(8)], writes=[("qT_s", sl)])
            P.dma("sp", kT_s[sl], HK[dr].rearrange("(h k) t -> k h t", k=128)[:, :, t0:t0 + 512], reads=[("HK", dr, h, gi) for h in range(8)], writes=[("kT_s", sl)])
            for h in range(8):
                P.dma("sp", ktok_s[sl][0:64, :, h * 128:(h + 1) * 128], HKT[dr, h, t0:t0 + 512, :].rearrange("(c s) k -> s c k", s=64),
                      reads=[("HKT", dr, h, gi)], writes=[("ktok_s", sl)])
            P.dma("sp", v_s[sl][0:64, :, :], HV[t0:t0 + 512, :].rearrange("(c s) f -> s c f", s=64), writes=[("v_s", sl)])

        def emit_front(stp):
            sl, c, n = stp["sl"], stp["c"], stp["n"]
            par = n % 2
            pAT = PB[0][:, par * 512:(par + 1) * 512]
            pKV = PB[2 + par]
            cs = slice(c * 64, (c + 1) * 64)
            for h in range(8):
                P.op("pe", lambda e: e.matmul(pAT[0:64, h * 64:(h + 1) * 64], lhsT=kT_s[sl][:, h, cs], rhs=qT_s[sl][:, h, cs], start=True, stop=True),
                     reads=[("kT_s", sl), ("qT_s", sl)], writes=[("pAT", par)])
            for h in range(8):
                P.op("pe", lambda e: e.matmul(pKV[:, h * 128:(h + 1) * 128], lhsT=ktok_s[sl][0:64, c, h * 128:(h + 1) * 128], rhs=v_s[sl][0:64, c, h * 128:(h + 1) * 128],
                                              start=True, stop=True),
                     reads=[("ktok_s", sl), ("v_s", sl)], writes=[("pKV", par, h // 4)])
            P.op("dve", lambda e: e.tensor_tensor(out=atm[par][0:64].rearrange("p h t -> p (h t)"), in0=pAT[0:64, :], in1=cm[0:64, dr, :], op=ALU.mult),
                 reads=[("pAT", par), "cm"], writes=[("atm", par)])

        def emit_back(stp):
            gi, sl, c, n = stp["gi"], stp["sl"], stp["c"], stp["n"]
            par = n % 2
            cg = gi * 8 + c
            pOo = PB[1][:, par * 512:(par + 1) * 512]
            pKV = PB[2 + par]
            cs = slice(c * 64, (c + 1) * 64)
            sb_in, sb_out = st_b2[n % 2], st_b2[(n + 1) % 2]
            for h in range(8):
                P.op("pe", lambda e: e.matmul(pOo[:, h * 64:(h + 1) * 64], lhsT=v_s[sl][0:64, c, h * 128:(h + 1) * 128], rhs=atm[par][0:64, h, :], start=True, stop=False),
                     reads=[("v_s", sl), ("atm", par)], writes=[("pOo", par)])
                P.op("pe", lambda e: e.matmul(pOo[:, h * 64:(h + 1) * 64], lhsT=sb_in[:, h, :], rhs=qT_s[sl][:, h, cs], start=False, stop=True),
                     reads=[("qT_s", sl), ("st_b", n % 2)], writes=[("pOo", par)])
            for h in range(8):
                P.op("dve", lambda e: e.scalar_tensor_tensor(out=st_f[:, h, :], in0=st_f[:, h, :], scalar=dec_sb[:, dr, h, cg:cg + 1], in1=pKV[:, h * 128:(h + 1) * 128],
                                                             op0=ALU.mult, op1=ALU.add),
                     reads=[("st_f", h), ("pKV", par, h // 4), ("dec", dr, h, gi)], writes=[("st_f", h)])
            bnd = (dr == 0 and cg == SEG // 64 - 1) or (dr == 1 and cg == SEG // 64)
            if bnd:
                P.op("dve", lambda e: e.tensor_scalar(out=st_f.rearrange("p h v -> p (h v)"), in0=st_f.rearrange("p h v -> p (h v)"), scalar1=prm[:, 72:73], scalar2=None, op0=ALU.mult),
                     reads=[("st_f", h) for h in range(8)] + ["prm"], writes=[("st_f", h) for h in range(8)])
            P.op("act", lambda e: e.activation(out=sb_out.rearrange("p h v -> p (h v)"), in_=st_f.rearrange("p h v -> p (h v)"), func=AF.Copy),
                 reads=[("st_f", h) for h in range(8)], writes=[("st_b", (n + 1) % 2)])
            P.op("act", lambda e: e.activation(out=o_sb[sl][:, :, cs], in_=pOo.rearrange("p (h t) -> p h t", h=8), func=AF.Copy), reads=[("pOo", par)], writes=[("o_sb", sl)])
            if stp["last"]:
                t0 = gi * 512
                P.dma("sp", OFB[dr].rearrange("(h v) t -> v h t", v=128)[:, :, t0:t0 + 512], o_sb[sl], reads=[("o_sb", sl)], writes=[("OFB", dr, gi)])

        emit_loads(steps[0])
        emit_front(steps[0])
        for i, stp in enumerate(steps):
            if stp["first"] and i + 8 < len(steps):
                emit_loads(steps[i + 8])
            if i + 1 < len(steps):
                emit_front(steps[i + 1])
            emit_back(stp)
    P.barrier()
    a_reset()

    if os.environ.get("KSTOP", "") == "3":
        print("PROG", P.analyze(), flush=True)
        P.emit()
        return nc
    A = alloc_common(2)
    xs, hn = A["xs"], A["hn"]
    yaT = a_bf(4 * 1024).rearrange("p (c t) -> p c t", c=4)
    ybT = a_bf(8 * 1024).rearrange("p (c t) -> p c t", c=8)
    mg = a_bf(8 * 1024).rearrange("p (c t) -> p c t", c=8)
    ul = [a_f32(3 * 520).rearrange("p (g f) -> p g f", g=3) for _ in range(2)]
    rden = a_f32(8)
    yatok = [a_bf(512) for _ in range(2)]
    ofl = [a_f32(2 * 512).rearrange("p (d t) -> p d t", d=2) for _ in range(2)]
    T3 = [a_f32(512) for _ in range(4)]
    c3 = {"ul": 0, "yatok": 0, "ofl": 0}
    rden2 = [rden, a_f32(8)]
    _g3 = A["gact"]
    _gk = lambda t: [("gact", 0, t), ("gact", 1, t)]
    OFL = [(_g3[:, 2 * i:2 * i + 2, :].bitcast(F32).rearrange("p d t -> p d t"), _gk(2 * i) + _gk(2 * i + 1)) for i in range(4)]
    RS4 = [(_g3[:, 8 + i, :].bitcast(F32), _gk(8 + i)) for i in range(4)]
    SIL4 = [(_g3[:, 12 + i, :].bitcast(F32), _gk(12 + i)) for i in range(4)]
    SQ4 = [(_g3[:, 16 + i // 2, ts(i % 2)], [("gact", i % 2, 16 + i // 2)]) for i in range(4)]
    Ut = U.rearrange("g t f -> t g f")
    wav = w_a.rearrange("(k p) n -> p k n", p=128)
    wbv = w_b.rearrange("(k p) n -> p k n", p=128)
    wov = w_o.rearrange("(k p) n -> p k n", p=128)
    OFBv = OFB.rearrange("d (h v) t -> v d h t", v=128)
    HGv = HG.rearrange("(h v) t -> v h t", v=128)
    _mgf = mg.rearrange("p c t -> p (c t)").bitcast(F32)
    USETS = []
    for k_ in range(2):
        base = k_ * 2048
        keys = [("mg", j_, dc_) for dc_ in range(4 * k_, 4 * k_ + 4) for j_ in range(2)]
        USETS.append(dict(of=_mgf[:, base:base + 1024].rearrange("p (d t) -> p d t", d=2), rs=_mgf[:, base + 1024:base + 1536],
                          sq=_mgf[:, base + 1536:base + 1792].bitcast(BF16), hg=_mgf[:, base + 1792:base + 2048].bitcast(BF16), keys=keys))
    for k_ in range(2):
        USETS.append(dict(of=ofl[k_], rs=T3[2 * k_], sq=T3[2 * k_ + 1][:, 0:256].bitcast(BF16), hg=T3[2 * k_ + 1][:, 256:512].bitcast(BF16),
                          keys=[("ofl", k_), ("T3", 2 * k_), ("T3", 2 * k_ + 1)]))
    ucnt = {"yb": 0, "ya": 0}

    def yb_unit(su_, h, j):
        U_ = USETS[ucnt["yb"] % 4]
        ucnt["yb"] += 1
        of_, rs_, sq_, hg_, uk = U_["of"], U_["rs"], U_["sq"], U_["hg"], U_["keys"]
        tg = su_ * 2 + j
        tt0 = su_ * 1024
        st = {}

        def F1():
            P.dma("sp", of_, OFBv[:, :, h, tt0 + j * 512:tt0 + (j + 1) * 512], reads=[("OFB", 0, tg), ("OFB", 1, tg)], writes=uk)
            P.dma("sp", hg_, HGv[:, h, tt0 + j * 512:tt0 + (j + 1) * 512], writes=uk)

        def F():
            P.op("dve", lambda e: e.tensor_tensor(out=of_[:, 0, :], in0=of_[:, 0, :], in1=of_[:, 1, :], op=ALU.add), reads=uk, writes=uk)
            P.op("act", lambda e: e.activation(out=sq_, in_=of_[:, 0, :], func=AF.Square), reads=uk, writes=uk)

        def M():
            st["bn"] = nb()
            P.op("pe", lambda e: e.matmul(bank(st["bn"]), lhsT=onesH[:], rhs=sq_, start=True, stop=True), reads=uk + ["onesH"], writes=[("pb", st["bn"])])

        def B():
            bn = st["bn"]
            P.op("act", lambda e: e.activation(out=rs_, in_=bank(bn), func=AF.Ln, bias=EPS), reads=[("pb", bn)] + uk, writes=uk)
            P.op("act", lambda e: e.activation(out=rs_, in_=rs_, func=AF.Exp, scale=-0.5), reads=uk, writes=uk)
            P.op("dve", lambda e: e.scalar_tensor_tensor(out=of_[:, 1, :], in0=of_[:, 0, :], scalar=prm[:, 32 + h:33 + h], in1=rs_, op0=ALU.mult, op1=ALU.mult),
                 reads=uk + ["prm"], writes=uk)
            P.op("dve", lambda e: e.tensor_tensor(out=ybT[:, h, ts(j)], in0=of_[:, 1, :], in1=hg_, op=ALU.mult), reads=uk, writes=[("ybT", j, h)])
        return [F1, F, M, B]

    def ya_unit(su_, tb):
        j = tb // 4
        li = ucnt["ya"] % 2
        ucnt["ya"] += 1
        rd = rden2[li]
        u3 = ul[li][:, 0, :].rearrange("p (h e) -> p h e", h=8)
        yi = li
        tt0 = su_ * 1024
        st = {}

        def F1():
            P.dma("sp", ul[li], Ut[tt0 + tb * 128:tt0 + (tb + 1) * 128, :, :], writes=[("ul", li)])

        def F():
            P.op("dve", lambda e: e.tensor_tensor(out=ul[li][:, 0, :], in0=ul[li][:, 0, :], in1=ul[li][:, 1, :], op=ALU.add), reads=[("ul", li)], writes=[("ul", li)])
            P.op("dve", lambda e: e.tensor_tensor(out=ul[li][:, 0, :], in0=ul[li][:, 0, :], in1=ul[li][:, 2, :], op=ALU.add), reads=[("ul", li)], writes=[("ul", li)])
            P.op("dve", lambda e: e.reciprocal(out=rd.rearrange("p (h o) -> p h o", o=1), in_=u3[:, :, 64:65]), reads=[("ul", li)], writes=[("rden", li)])
            P.op("dve", lambda e: e.tensor_tensor(out=yatok[yi].rearrange("p (h e) -> p h e", h=8), in0=u3[:, :, 0:64],
                                                  in1=rd.rearrange("p (h o) -> p h o", o=1).broadcast_to([128, 8, 64]), op=ALU.mult),
                 reads=[("ul", li), ("rden", li)], writes=[("yatok", yi)])

        def M():
            st["bt"] = nb()
            pbf = bank(st["bt"]).bitcast(BF16)
            for fc in range(4):
                P.op("pe", lambda e: e.transpose(out=pbf[:, fc * 128:(fc + 1) * 128], in_=yatok[yi][:, fc * 128:(fc + 1) * 128], identity=ident[:]),
                     reads=[("yatok", yi), "ident"], writes=[("pb", st["bt"])])

        def B():
            pbf = bank(st["bt"]).bitcast(BF16)
            P.op("act", lambda e: e.activation(out=yaT[:, :, tb * 128:(tb + 1) * 128], in_=pbf[:, 0:512].rearrange("p (c t) -> p c t", c=4), func=AF.Copy),
                 reads=[("pb", st["bt"])], writes=[("yaT", j, tb)])
        return [F1, F, M, B]

    def make_sched(su_):
        sched = [[] for _ in range(24)]
        ybu = [(h, j) for h in range(8) for j in range(2)]
        for i_, (h, j) in enumerate(ybu):
            F1, F, M, B = yb_unit(su_, h, j)
            sched[i_].append(F1); sched[i_ + 1].append(F); sched[i_ + 3].append(M); sched[i_ + 3].append(B)
            if i_ % 2 == 0:
                F1, F, M, B = ya_unit(su_, i_ // 2)
                sched[i_].append(F1); sched[i_ + 1].append(F); sched[i_ + 3].append(M); sched[i_ + 3].append(B)
        return sched

    def run_sched_all(sched):
        for lst in sched:
            for f_ in lst:
                f_()

    def s3_xload(su_):
        for j in range(2):
            P.dma("sp", xs[:, :, ts(j)], x1Tv[:, :, su_ * 1024 + j * 512:su_ * 1024 + (j + 1) * 512], reads=[("x1T", su_, j)], writes=[("xs", j, c) for c in range(8)])
    s3_xload(0)
    run_sched_all(make_sched(0))
    for su in range(NSU):
        t0 = su * 1024
        for j in range(2):
            norm(A, 8, j)
        for dc in range(8):
            def view(sl_):
                return (sl_[:, 0:1024].rearrange("p (k n) -> p k n", k=8), sl_[:, 1024:2048].rearrange("p (k n) -> p k n", k=8),
                        sl_[:, 2048:2560].rearrange("p (k n) -> p k n", k=4), sl_[:, 2560:3584].rearrange("p (k n) -> p k n", k=8))
            s, slot = wload(A, [(lambda sl_: view(sl_)[0], winv[:, :, C_GA + dc * 128:C_GA + (dc + 1) * 128]),
                                (lambda sl_: view(sl_)[1], winv[:, :, C_GB + dc * 128:C_GB + (dc + 1) * 128]),
                                (lambda sl_: view(sl_)[2], wav[:, :, dc * 128:(dc + 1) * 128]),
                                (lambda sl_: view(sl_)[3], wbv[:, :, dc * 128:(dc + 1) * 128])])
            vga, vgb, va, vb = view(slot)
            for j in range(2):
                b1, b2, b3, b4 = nb(), nb(), nb(), nb()
                for k in range(8):
                    P.op("pe", lambda e: e.matmul(bank(b1), lhsT=vga[:, k, :], rhs=hn[:, k, ts(j)], start=(k == 0), stop=(k == 7)), reads=[("w", s), ("hn", j, k)], writes=[("pb", b1)])
                for k in range(8):
                    P.op("pe", lambda e: e.matmul(bank(b2), lhsT=vgb[:, k, :], rhs=hn[:, k, ts(j)], start=(k == 0), stop=(k == 7)), reads=[("w", s), ("hn", j, k)], writes=[("pb", b2)])
                for k in range(4):
                    P.op("pe", lambda e: e.matmul(bank(b3), lhsT=va[:, k, :], rhs=yaT[:, k, ts(j)], start=(k == 0), stop=(k == 3)),
                         reads=[("w", s)] + [("yaT", j, tb) for tb in range(4 * j, 4 * j + 4)], writes=[("pb", b3)])
                for k in range(8):
                    P.op("pe", lambda e: e.matmul(bank(b4), lhsT=vb[:, k, :], rhs=ybT[:, k, ts(j)], start=(k == 0), stop=(k == 7)), reads=[("w", s), ("ybT", j, k)], writes=[("pb", b4)])
                P.op("act", lambda e: e.activation(out=T3[0], in_=bank(b1), func=AF.Sigmoid), reads=[("pb", b1)], writes=[("T3", 0)])
                P.op("act", lambda e: e.activation(out=T3[1], in_=bank(b2), func=AF.Sigmoid), reads=[("pb", b2)], writes=[("T3", 1)])
                P.op("dve", lambda e: e.tensor_tensor(out=T3[2], in0=T3[0], in1=bank(b3), op=ALU.mult), reads=[("T3", 0), ("pb", b3)], writes=[("T3", 2)])
                P.op("dve", lambda e: e.tensor_tensor(out=T3[3], in0=T3[1], in1=bank(b4), op=ALU.mult), reads=[("T3", 1), ("pb", b4)], writes=[("T3", 3)])
                P.op("dve", lambda e: e.tensor_tensor(out=mg[:, dc, ts(j)], in0=T3[2], in1=T3[3], op=ALU.add), reads=[("T3", 2), ("T3", 3)], writes=[("mg", j, dc)])
        for dc in range(8):
            view = lambda sl_: sl_[:, 0:1024].rearrange("p (k n) -> p k n", k=8)
            s, slot = wload(A, [(view, wov[:, :, dc * 128:(dc + 1) * 128])])
            v = view(slot)
            for j in range(2):
                b = nb()
                for k in range(8):
                    P.op("pe", lambda e: e.matmul(bank(b), lhsT=v[:, k, :], rhs=mg[:, k, ts(j)], start=(k == 0), stop=(k == 7)), reads=[("w", s), ("mg", j, k)], writes=[("pb", b)])
                P.op("dve", lambda e: e.tensor_tensor(out=xs[:, dc, ts(j)], in0=xs[:, dc, ts(j)], in1=bank(b), op=ALU.add), reads=[("pb", b), ("xs", j, dc)], writes=[("xs", j, dc)])
        for j in range(2):
            norm(A, 16, j)
        if su + 1 < NSU:
            sched = make_sched(su + 1)
            hk = {"i": 0}

            def hook():
                if hk["i"] < len(sched):
                    for f_ in sched[hk["i"]]:
                        f_()
                hk["i"] += 1
            ffn(A, w_gu2, w_d2, hook=hook)
            while hk["i"] < len(sched):
                hook()
        else:
            ffn(A, w_gu2, w_d2)
        for j in range(2):
            norm(A, 24, j, final=True)
        if su + 1 < NSU:
            s3_xload(su + 1)
        for j in range(2):
            yst = A["gact"][:, 8 * j:8 * j + 8, :].bitcast(F32)
            P.dma("sp", yTv[:, :, t0 + j * 512:t0 + (j + 1) * 512], yst,
                  reads=[("gact", jj, 8 * j + c) for c in range(8) for jj in range(2)], writes=[("yT", su, j)])
    stats = P.analyze()
    print("PROG", stats, flush=True)
    P.emit()
    return nc


_CACHE = {}


def _consts():
    et = np.zeros((6, 128, 8, 128), np.float32)
    p = np.arange(128)[:, None]
    q = np.arange(128)[None, :]
    for g in range(3):
        for kt in range(2):
            rel = p - q - 64 + 128 * kt
            valid = (np.abs(rel) <= 64)
            for h in range(8):
                et[g * 2 + kt, :, h, :] = np.where(valid, np.exp(-(SLOPES[h] * DILS[g]) * np.abs(rel).astype(np.float64)), 0.0)
    s = np.arange(64)[:, None]
    t = np.arange(64)[None, :]
    cm = np.zeros((2, 64, 8, 64), np.float32)
    cm[0] = (t >= s)[:, None, :]
    cm[1] = (t <= s)[:, None, :]
    rm = np.ones((128, 512), np.float32)
    rm[:, ::64] = 0.0
    ident = np.eye(128, dtype=np.float32).astype(ml_dtypes.bfloat16)
    return et.reshape(6, 128, 1024), cm.reshape(2, 64, 512), rm, ident


def kernel(x_prompt, x_sample, ffn1_norm, ffn1_w_gu, ffn1_w_down, mix_norm, w_in, hgrn_lb_fwd, hgrn_lb_bwd, hgrn_norm,
           w_branch_a, w_branch_b, w_out, ffn2_norm, ffn2_w_gu, ffn2_w_down, final_norm):
    x_prompt = np.asarray(x_prompt, np.float32)
    x_sample = np.asarray(x_sample, np.float32)
    S = x_sample.shape[1]
    SEG = S // 2
    assert x_prompt.shape[1] == SEG and x_prompt.shape[0] == 4 and x_sample.shape[0] == 4
    if S not in _CACHE:
        _CACHE[S] = build(S)
    nc = _CACHE[S]
    f = lambda a: np.ascontiguousarray(np.asarray(a, np.float32))
    col = lambda vec: np.asarray(vec, np.float32).reshape(8, 128).T
    et, cm, rm, ident = _consts()
    seqs = []
    for b in range(4):
        seqs.append((x_sample[b], 1.0))
    for i in range(2):
        seqs.append((np.concatenate([x_prompt[2 * i], x_prompt[2 * i + 1]], axis=0), 0.0))
    seqs.append(seqs[4]); seqs.append(seqs[5])
    shared = dict(w_gu1=f(ffn1_w_gu[0]), w_d1=f(ffn1_w_down[0]), w_in=f(w_in[0]), w_a=f(w_branch_a[0]), w_b=f(w_branch_b[0]),
                  w_o=f(w_out[0]), w_gu2=f(ffn2_w_gu[0]), w_d2=f(ffn2_w_down[0]), etab=et, cmask=cm, rmask=rm, ident=ident)
    in_maps = []
    for xs_, lk in seqs:
        prm = np.zeros((128, NPRM), np.float32)
        prm[:, 0:8] = col(ffn1_norm[0]); prm[:, 8:16] = col(mix_norm[0]); prm[:, 16:24] = col(ffn2_norm[0]); prm[:, 24:32] = col(final_norm)
        prm[:, 32:40] = col(hgrn_norm[0])
        prm[:, 40:48] = col(hgrn_lb_fwd[0]); prm[:, 48:56] = col(hgrn_lb_fwd[1])
        prm[:, 56:64] = col(hgrn_lb_bwd[0]); prm[:, 64:72] = col(hgrn_lb_bwd[1])
        prm[:, 72] = lk
        prm[:, 73] = 1.0; prm[:64, 73] = lk
        prm[:, 74] = 1.0; prm[64:, 74] = lk
        m = dict(shared)
        m["xT"] = np.ascontiguousarray(xs_.T)
        m["prm"] = prm
        in_maps.append(m)
    res = run_bass_kernel_spmd(nc, in_maps, core_ids=list(range(8)))
    outs = [np.ascontiguousarray(res.results[c]["yT"].T) for c in range(8)]
    y_sample = np.stack(outs[0:4], axis=0)
    y_prompt = np.stack([outs[4][:SEG], outs[4][SEG:], outs[5][:SEG], outs[5][SEG:]], axis=0)
    return (y_prompt.astype(np.float32), y_sample.astype(np.float32))
```

```python
import math
import os
import numpy as np
import ml_dtypes
import concourse.bass as bass
import concourse.mybir as mybir
from concourse.bass_utils import run_bass_kernel_spmd

F32 = mybir.dt.float32
BF16 = mybir.dt.bfloat16
ALU = mybir.AluOpType
AF = mybir.ActivationFunctionType

ENGS = ("pe", "act", "dve", "pool", "sp")
EPOCH = 12000
NDMASEM = 12


class Op:
    __slots__ = ("eng", "fn", "reads", "writes", "dma", "deps", "inc", "waits", "idx")

    def __init__(self, eng, fn, reads, writes, dma):
        self.eng, self.fn, self.reads, self.writes, self.dma = eng, fn, reads, writes, dma
        self.deps = set()
        self.inc = None
        self.waits = ()


class _Rec:
    def __getattr__(self, name):
        def f(*a, **kw):
            self.call = (name, a, kw)
            return self
        return f


class Prog:
    def __init__(self, nc):
        self.nc = nc
        self.ops = []
        self.barriers = []

    def op(self, eng, fn, reads=(), writes=(), dma=False):
        rec = _Rec()
        fn(rec)
        name, a, kw = rec.call
        o = Op(eng, (lambda e, name=name, a=a, kw=kw: getattr(e, name)(*a, **kw)), tuple(reads), tuple(writes), dma)
        o.idx = len(self.ops)
        self.ops.append(o)
        return o

    def dma(self, q, out, in_, reads=(), writes=()):
        return self.op(q, lambda e: e.dma_start(out=out, in_=in_), reads, writes, dma=True)

    def barrier(self):
        self.barriers.append(len(self.ops))

    def analyze(self):
        ops = self.ops
        last_writer = {}
        readers = {}
        for o in ops:
            deps = set()
            for t in o.reads:
                w = last_writer.get(t)
                if w is not None:
                    deps.add(w)
            for t in o.writes:
                w = last_writer.get(t)
                if w is not None:
                    deps.add(w)
                for r in readers.get(t, ()):
                    deps.add(r)
            for t in o.reads:
                readers.setdefault(t, []).append(o.idx)
            for t in o.writes:
                last_writer[t] = o.idx
                readers[t] = []
            deps.discard(o.idx)
            if o.eng == "pe" and not o.dma:
                deps = {d for d in deps if not (ops[d].eng == "pe" and not ops[d].dma)}
            o.deps = deps
        for bp in self.barriers:
            if bp == 0 or bp >= len(ops):
                continue
            bdeps = set()
            seen_c = set()
            dcnt = {e: 0 for e in ENGS}
            for i in range(bp - 1, -1, -1):
                o = ops[i]
                if o.dma:
                    if dcnt[o.eng] < NDMASEM:
                        dcnt[o.eng] += 1
                        bdeps.add(i)
                elif o.eng not in seen_c:
                    seen_c.add(o.eng)
                    bdeps.add(i)
            first = set()
            for i in range(bp, len(ops)):
                o = ops[i]
                if o.eng not in first:
                    first.add(o.eng)
                    o.deps |= bdeps
                    if len(first) == len(ENGS):
                        break
        has_dep = [False] * len(ops)
        dma_count = {e: 0 for e in ENGS}
        dma_hist = {e: [] for e in ENGS}
        for o in ops:
            if o.dma:
                j = dma_count[o.eng]
                dma_count[o.eng] += 1
                o.inc = (("dma", o.eng, j % NDMASEM), 16 * (j // NDMASEM + 1))
                if j >= NDMASEM:
                    o.deps.add(dma_hist[o.eng][j - NDMASEM])
                dma_hist[o.eng].append(o.idx)
                has_dep[o.idx] = True
        for o in ops:
            for d in o.deps:
                has_dep[d] = True
        cnt = {e: 0 for e in ENGS}
        for o in ops:
            if o.dma:
                continue
            if has_dep[o.idx]:
                c = cnt[o.eng]
                cnt[o.eng] += 1
                o.inc = (("eng", o.eng, c // EPOCH), c % EPOCH + 1)
        known = {e: {} for e in ENGS}
        nw = 0
        for o in ops:
            need = {}
            for d in o.deps:
                k, v = ops[d].inc
                if need.get(k, 0) < v:
                    need[k] = v
            kn = known[o.eng]
            w = []
            for k, v in need.items():
                if kn.get(k, 0) < v:
                    kn[k] = v
                    w.append((k, v))
            o.waits = tuple(w)
            nw += len(w)
        self.semkeys = sorted({o.inc[0] for o in ops if o.inc is not None})
        return dict(n_ops=len(ops), n_waits=nw, n_sems=len(self.semkeys),
                    per_eng={e: sum(1 for o in ops if o.eng == e) for e in ENGS})

    def emit(self):
        from contextlib import ExitStack
        nc = self.nc
        ops = self.ops
        final = {}
        for o in ops:
            if o.inc is not None:
                k, v = o.inc
                if final.get(k, 0) < v:
                    final[k] = v
        with ExitStack() as st:
            sems = {}
            for k in self.semkeys:
                sems[k] = st.enter_context(nc.semaphore("s_%s_%s_%d" % k))
            block = st.enter_context(nc.Block())
            per_eng = {e: [o for o in ops if o.eng == e] for e in ENGS}

            def run(handle, ename):
                for o in per_eng[ename]:
                    for k, v in o.waits:
                        handle.wait_ge(sems[k], v)
                    ins = o.fn(handle)
                    if o.inc is not None:
                        ins.then_inc(sems[o.inc[0]], 16 if o.dma else 1)
                if ename == "sp":
                    for k, v in final.items():
                        handle.wait_ge(sems[k], v)

            @block.tensor
            def _(e):
                run(e, "pe")

            @block.scalar
            def _(e):
                run(e, "act")

            @block.vector
            def _(e):
                run(e, "dve")

            @block.gpsimd
            def _(e):
                run(e, "pool")

            @block.sync
            def _(e):
                run(e, "sp")


D = 1024
DFF = 2816
PAD = 1024
NPRM = 80
EPS = 1e-6
DILS = (1, 4, 16)
SLOPES = tuple(2.0 ** (-8.0 * (j + 1) / 8) for j in range(8))
C_AQ, C_AK, C_AV = 0, 1536, 3072
C_HQ, C_HFF, C_HFB, C_HI, C_HG, C_GA, C_GB = 4608, 5632, 6656, 7680, 8704, 9728, 10752


def build(S):
    SEG = S // 2
    NSU = S // 1024
    NT = S // 512
    NCH = S // 64
    nc = bass.Bass("TRN2", target_bir_lowering=False)

    def din(name, shape, dt=F32):
        return nc.dram_tensor(name, list(shape), dt, kind="ExternalInput").ap()

    xT = din("xT", [D, S])
    gu1b = din("gu1b", [11, 128, 4096]); d1b = din("d1b", [8, 128, 2816])
    aqkb = din("aqkb", [12, 128, 2048]); hgb = din("hgb", [8, 128, 4096]); tokb = din("tokb", [5, 128, 4096])
    gtb = din("gtb", [8, 128, 3584]); outb = din("outb", [8, 128, 1024])
    gu2b = din("gu2b", [11, 128, 4096]); d2b = din("d2b", [8, 128, 2816])
    prm_d = din("prm", [128, NPRM])
    etab_d = din("etab", [6, 128, 1024])
    cmask_d = din("cmask", [2, 64, 512])
    rmask_d = din("rmask", [128, 512])
    ident_d = din("ident", [128, 128], BF16)
    yT = nc.dram_tensor("yT", [D, S], F32, kind="ExternalOutput").ap()

    def dscr(name, shape, dt):
        return nc.dram_tensor(name, list(shape), dt).ap()

    x1T = dscr("x1T", [D, S], F32)
    QT = dscr("QT", [1536, S], BF16)
    KT = dscr("KT", [1536, S + 2 * PAD], BF16)
    VA = dscr("VA", [S + 2 * PAD, 24 * 65], BF16)
    HQ = dscr("HQ", [2, D, S], BF16)
    HK = dscr("HK", [2, D, S], BF16)
    HKT = dscr("HKT", [2, 8, S, 128], BF16)
    HV = dscr("HV", [S, D], BF16)
    HG = dscr("HG", [D, S], BF16)
    U = dscr("U", [3, S, 520], F32)
    OFB = dscr("OFB", [2, D, S], F32)

    P = Prog(nc)

    prm = nc.alloc_sbuf_tensor("prm_s", [128, NPRM], F32)
    lbs = nc.alloc_sbuf_tensor("lbs", [128, 16], F32)
    oml = nc.alloc_sbuf_tensor("oml", [128, 16], F32)
    noml = nc.alloc_sbuf_tensor("noml", [128, 16], F32)
    onesD = nc.alloc_sbuf_tensor("onesD", [128, 128], BF16)
    onesH = nc.alloc_sbuf_tensor("onesH", [128, 128], BF16)
    ident = nc.alloc_sbuf_tensor("ident_s", [128, 128], BF16)
    rmask = nc.alloc_sbuf_tensor("rmask_s", [128, 512], F32)
    dec_sb = nc.alloc_sbuf_tensor("dec_sb", [128, 2, 8, NCH], F32)
    zt_t = nc.alloc_sbuf_tensor("zt_t", [128, 1560], BF16)
    PB = [nc.alloc_psum_tensor("pb%d" % i, [128, 1024], F32) for i in range(4)]
    ARENA_W = (nc.sbuf_bytes_remaining - 2048) // 4
    arena = nc.alloc_sbuf_tensor("arena", [128, ARENA_W], F32)
    ar = {"off": 0}

    def a_f32(n):
        o = ar["off"]
        ar["off"] = o + n
        assert ar["off"] <= ARENA_W, ("arena overflow", ar["off"], ARENA_W)
        return arena[:, o:o + n]

    def a_bf(n):
        return a_f32((n + 1) // 2).bitcast(BF16)

    def a_reset():
        ar["off"] = 0

    def bank(i):
        return PB[i // 2][:, (i % 2) * 512:(i % 2 + 1) * 512]

    nbc = {"i": 0}

    def nb():
        i = nbc["i"]
        nbc["i"] = (i + 1) % 8
        return i

    def ts(j):
        return slice(j * 512, (j + 1) * 512)

    P.dma("sp", prm[:], prm_d, writes=["prm"])
    P.dma("sp", ident[:], ident_d, writes=["ident"])
    P.dma("sp", rmask[:], rmask_d, writes=["rmask"])
    P.op("dve", lambda e: e.memset(onesD[:], 1.0 / 1024.0), writes=["onesD"])
    P.op("dve", lambda e: e.memset(onesH[:], 1.0 / 128.0), writes=["onesH"])
    P.op("dve", lambda e: e.tensor_tensor(out=lbs[:, 0:8], in0=prm[:, 40:48], in1=prm[:, 48:56], op=ALU.subtract), reads=["prm"], writes=["lbs"])
    P.op("dve", lambda e: e.tensor_tensor(out=lbs[:, 8:16], in0=prm[:, 56:64], in1=prm[:, 64:72], op=ALU.subtract), reads=["prm", "lbs"], writes=["lbs"])
    P.op("act", lambda e: e.activation(out=lbs[:], in_=lbs[:], func=AF.Sigmoid), reads=["lbs"], writes=["lbs"])
    P.op("dve", lambda e: e.tensor_scalar(out=oml[:], in0=lbs[:], scalar1=-1.0, scalar2=1.0, op0=ALU.mult, op1=ALU.add), reads=["lbs"], writes=["oml"])
    P.op("dve", lambda e: e.tensor_scalar(out=noml[:], in0=lbs[:], scalar1=1.0, scalar2=-1.0, op0=ALU.mult, op1=ALU.add), reads=["lbs"], writes=["noml"])
    link = prm[:, 72:73]
    P.op("dve", lambda e: e.memset(zt_t[:], 0.0), writes=["zt"])

    def emit_pads():
        for c in range(12):
            for side in range(2):
                c0 = 0 if side == 0 else PAD + S
                P.dma("sp", KT[c * 128:(c + 1) * 128, c0:c0 + PAD], zt_t[:, 0:PAD], reads=["zt"], writes=[("KTpad", c, side)])
        for side in range(2):
            r0 = 0 if side == 0 else PAD + S
            for rb in range(PAD // 128):
                P.dma("sp", VA[r0 + rb * 128:r0 + (rb + 1) * 128, :], zt_t[:, 0:1560], reads=["zt"], writes=[("VApad", side, rb)])

    def alloc_common(nw=3):
        A = {}
        A["xs"] = a_f32(8 * 1024).rearrange("p (c t) -> p c t", c=8)
        A["hn"] = a_bf(8 * 1024).rearrange("p (c t) -> p c t", c=8)
        A["gact"] = a_bf(22 * 1024).rearrange("p (c t) -> p c t", c=22)
        A["w"] = [a_bf(4096) for _ in range(nw)]
        A["sq"] = [a_bf(512) for _ in range(3)]
        A["rs"] = [a_f32(512) for _ in range(2)]
        A["sil"] = [a_f32(512) for _ in range(2)]
        A["cnt"] = {"w": 0, "sq": 0, "rs": 0, "sil": 0}
        return A

    def rot(A, name, n):
        i = A["cnt"][name] % n
        A["cnt"][name] += 1
        return i

    def norm(A, gcol, j, final=False):
        xs, hn = A["xs"], A["hn"]
        b = nb()
        for c in range(8):
            si = rot(A, "sq", 3)
            P.op("act", lambda e, c=c, si=si: e.activation(out=A["sq"][si], in_=xs[:, c, ts(j)], func=AF.Square),
                 reads=[("xs", j, c)], writes=[("sq", si)])
            P.op("pe", lambda e, c=c, si=si, b=b: e.matmul(bank(b), lhsT=onesD[:], rhs=A["sq"][si], start=(c == 0), stop=(c == 7)),
                 reads=[("sq", si), "onesD"], writes=[("pb", b)])
        ri = rot(A, "rs", 2)
        P.op("act", lambda e, b=b, ri=ri: e.activation(out=A["rs"][ri], in_=bank(b), func=AF.Ln, bias=EPS),
             reads=[("pb", b)], writes=[("rs", ri)])
        P.op("act", lambda e, ri=ri: e.activation(out=A["rs"][ri], in_=A["rs"][ri], func=AF.Exp, scale=-0.5),
             reads=[("rs", ri)], writes=[("rs", ri)])
        for c in range(8):
            if final:
                yc = A["gact"][:, 8 * j + c, :].bitcast(F32)
                P.op("dve", lambda e, c=c, ri=ri, yc=yc: e.scalar_tensor_tensor(out=yc, in0=xs[:, c, ts(j)], scalar=prm[:, gcol + c:gcol + c + 1],
                                                                              in1=A["rs"][ri], op0=ALU.mult, op1=ALU.mult),
                     reads=[("xs", j, c), ("rs", ri), "prm"], writes=[("gact", 0, 8 * j + c), ("gact", 1, 8 * j + c)])
            else:
                P.op("dve", lambda e, c=c, ri=ri: e.scalar_tensor_tensor(out=hn[:, c, ts(j)], in0=xs[:, c, ts(j)], scalar=prm[:, gcol + c:gcol + c + 1],
                                                                       in1=A["rs"][ri], op0=ALU.mult, op1=ALU.mult),
                     reads=[("xs", j, c), ("rs", ri), "prm"], writes=[("hn", j, c)])

    def wload(A, pieces):
        s = rot(A, "w", len(A["w"]))
        slot = A["w"][s]
        for dv, src in pieces:
            P.dma("pool", dv(slot), src, writes=[("w", s)])
        return s, slot

    def ffn(A, w_gu, w_d, hook=None):
        xs, hn, gact = A["xs"], A["hn"], A["gact"]
        for i in range(11):
            view = lambda sl: sl.rearrange("p (k a n) -> p k a n", k=8, a=2)
            s, slot = wload(A, [(lambda sl: sl[:, 0:4096], w_gu[i])])
            v = view(slot)
            for j in range(2):
                for m2 in range(2):
                    m = 2 * i + m2
                    ba, bb = nb(), nb()
                    for k in range(8):
                        P.op("pe", lambda e, k=k, ba=ba, m2=m2, j=j, v=v: e.matmul(bank(ba), lhsT=v[:, k, 0, m2 * 128:(m2 + 1) * 128], rhs=hn[:, k, ts(j)],
                                                                              start=(k == 0), stop=(k == 7)),
                             reads=[("w", s), ("hn", j, k)], writes=[("pb", ba)])
                    for k in range(8):
                        P.op("pe", lambda e, k=k, bb=bb, m2=m2, j=j, v=v: e.matmul(bank(bb), lhsT=v[:, k, 1, m2 * 128:(m2 + 1) * 128], rhs=hn[:, k, ts(j)],
                                                                              start=(k == 0), stop=(k == 7)),
                             reads=[("w", s), ("hn", j, k)], writes=[("pb", bb)])
                    ti = rot(A, "sil", 2)
                    P.op("act", lambda e, ba=ba, ti=ti: e.activation(out=A["sil"][ti], in_=bank(ba), func=AF.Silu),
                         reads=[("pb", ba)], writes=[("sil", ti)])
                    P.op("dve", lambda e, bb=bb, ti=ti, m=m, j=j: e.tensor_tensor(out=gact[:, m, ts(j)], in0=A["sil"][ti], in1=bank(bb), op=ALU.mult),
                         reads=[("sil", ti), ("pb", bb)], writes=[("gact", j, m)])
            if hook is not None:
                hook()
        for mo in range(8):
            view = lambda sl: sl[:, 0:2816].rearrange("p (k n) -> p k n", k=22)
            s, slot = wload(A, [(lambda sl: sl[:, 0:2816], w_d[mo])])
            v = view(slot)
            for j in range(2):
                b = nb()
                for k in range(22):
                    P.op("pe", lambda e, k=k, b=b, j=j, v=v: e.matmul(bank(b), lhsT=v[:, k, :], rhs=gact[:, k, ts(j)], start=(k == 0), stop=(k == 21)),
                         reads=[("w", s), ("gact", j, k)], writes=[("pb", b)])
                P.op("dve", lambda e, b=b, j=j, mo=mo: e.scalar_tensor_tensor(out=xs[:, mo, ts(j)], in0=bank(b), scalar=0.5, in1=xs[:, mo, ts(j)],
                                                                          op0=ALU.mult, op1=ALU.add),
                     reads=[("pb", b), ("xs", j, mo)], writes=[("xs", j, mo)])
            if hook is not None:
                hook()

    xTv = xT.rearrange("(c p) t -> p c t", p=128)
    x1Tv = x1T.rearrange("(c p) t -> p c t", p=128)
    yTv = yT.rearrange("(c p) t -> p c t", p=128)

    if os.environ.get("KSTOP", "") == "0":
        print("PROG", P.analyze(), flush=True)
        P.emit()
        return nc
    A = alloc_common()
    kendT = [a_bf(512) for _ in range(3)]
    X = [[a_f32(512) for _ in range(6)] for _ in range(2)]
    sgq = a_f32(512)
    qsb = a_f32(512)
    qst = [a_bf(512) for _ in range(3)]
    hqst = [a_bf(512) for _ in range(4)]
    hkst = [a_bf(512) for _ in range(4)]
    vst = [a_bf(520).rearrange("p (h e) -> p h e", h=8) for _ in range(3)]
    hvst = [a_bf(512) for _ in range(2)]
    hgst = [a_bf(512) for _ in range(2)]
    kst = [a_bf(512) for _ in range(2)]
    A["cnt"].update(qst=0, hqst=0, hkst=0, vst=0, hvst=0, kst=0, kendT=0, hgst=0)
    _g = A["gact"]
    _gt = lambda t: (_g[:, t, :].bitcast(F32), [("gact", 0, t), ("gact", 1, t)])
    TS = [dict(sgq=(sgq, ["sgq"]), qsb=(qsb, ["qsb"]), X=[[(X[dr][i], [("X", dr, i)]) for i in range(6)] for dr in range(2)]),
          dict(sgq=_gt(12), qsb=_gt(13), X=[[_gt(dr * 6 + i) for i in range(6)] for dr in range(2)])]
    KEND = [(_g[:, 14 + b_ // 2, ts(b_ % 2)], ("gact", b_ % 2, 14 + b_ // 2)) for b_ in range(16)]
    for i in range(3):
        P.op("dve", lambda e, i=i: e.memset(vst[i][:, :, 64:65], 1.0), writes=[("vst", i)])

    def s1_xload(su_):
        for j in range(2):
            P.dma("sp", A["xs"][:, :, ts(j)], xTv[:, :, su_ * 1024 + j * 512:su_ * 1024 + (j + 1) * 512], writes=[("xs", j, c) for c in range(8)])
    _nsu1 = 0 if os.environ.get('KSKIP1') else NSU
    if _nsu1:
        s1_xload(0)
    emit_pads()
    for su in range(_nsu1):
        t0 = su * 1024
        for j in range(2):
            norm(A, 0, j)
        ffn(A, gu1b, d1b)
        for j in range(2):
            P.dma("sp", x1Tv[:, :, t0 + j * 512:t0 + (j + 1) * 512], A["xs"][:, :, ts(j)], reads=[("xs", j, c) for c in range(8)], writes=[("x1T", su, j)])
        for j in range(2):
            norm(A, 8, j)
        if su + 1 < _nsu1:
            s1_xload(su + 1)
        hn = A["hn"]
        def aqk_block(which, blk):
            cbase = C_AQ if which == 0 else C_AK
            view = lambda sl: sl[:, 0:2048].rearrange("p (k n) -> p k n", k=8)
            s, slot = wload(A, [(lambda sl: sl[:, 0:2048], aqkb[which * 6 + blk])])
            v = view(slot)
            for m2 in range(2):
                ch = blk * 2 + m2
                for j in range(2):
                    b = nb()
                    for k in range(8):
                        P.op("pe", lambda e: e.matmul(bank(b), lhsT=v[:, k, m2 * 128:(m2 + 1) * 128], rhs=hn[:, k, ts(j)], start=(k == 0), stop=(k == 7)),
                             reads=[("w", s), ("hn", j, k)], writes=[("pb", b)])
                    qi = rot(A, "qst", 3)
                    if which == 0:
                        P.op("act", lambda e: e.activation(out=qst[qi], in_=bank(b), func=AF.Copy, scale=0.125), reads=[("pb", b)], writes=[("qst", qi)])
                        P.dma("sp", QT[ch * 128:(ch + 1) * 128, t0 + j * 512:t0 + (j + 1) * 512], qst[qi], reads=[("qst", qi)], writes=[("QT", ch, su, j)])
                    else:
                        P.op("dve", lambda e: e.tensor_copy(out=qst[qi], in_=bank(b)), reads=[("pb", b)], writes=[("qst", qi)])
                        P.dma("sp", KT[ch * 128:(ch + 1) * 128, PAD + t0 + j * 512:PAD + t0 + (j + 1) * 512], qst[qi], reads=[("qst", qi)], writes=[("KT", ch, su, j)])

        def tok_block(i):
            cb = C_AV + i * 512 if i < 3 else C_HI + (i - 3) * 512
            view = lambda sl: sl.rearrange("p (k n) -> p k n", k=8)
            s, slot = wload(A, [(lambda sl: sl[:, 0:4096], tokb[i])])
            v = view(slot)
            for tb in range(8):
                b = nb()
                j = tb // 4
                for k in range(8):
                    P.op("pe", lambda e: e.matmul(bank(b), lhsT=hn[:, k, tb * 128:(tb + 1) * 128], rhs=v[:, k, :], start=(k == 0), stop=(k == 7)),
                         reads=[("w", s), ("hn", j, k)], writes=[("pb", b)])
                if i < 3:
                    vi = rot(A, "vst", 3)
                    P.op("act", lambda e: e.activation(out=vst[vi][:, :, 0:64], in_=bank(b).rearrange("p (h e) -> p h e", h=8), func=AF.Copy),
                         reads=[("pb", b)], writes=[("vst", vi)])
                    P.dma("sp", VA[PAD + t0 + tb * 128:PAD + t0 + (tb + 1) * 128, i * 520:(i + 1) * 520], vst[vi].rearrange("p h e -> p (h e)"),
                          reads=[("vst", vi)], writes=[("VA", i, su, tb)])
                else:
                    vi = rot(A, "hvst", 2)
                    P.op("dve", lambda e: e.tensor_copy(out=hvst[vi], in_=bank(b)), reads=[("pb", b)], writes=[("hvst", vi)])
                    P.dma("sp", HV[t0 + tb * 128:t0 + (tb + 1) * 128, (i - 3) * 512:(i - 2) * 512], hvst[vi], reads=[("hvst", vi)], writes=[("HV", i, su, tb)])

        def hgrn_proj(h):
            view = lambda sl: sl[:, 0:4096].rearrange("p (k a n) -> p k a n", k=8, a=4)
            s, slot = wload(A, [(lambda sl: sl[:, 0:4096], hgb[h])])
            v = view(slot)
            for j in range(2):
                bg = nb()
                for k in range(8):
                    P.op("pe", lambda e: e.matmul(bank(bg), lhsT=v[:, k, 3, :], rhs=hn[:, k, ts(j)], start=(k == 0), stop=(k == 7)),
                         reads=[("w", s), ("hn", j, k)], writes=[("pb", bg)])
                gi_ = rot(A, "hgst", 2)
                P.op("act", lambda e: e.activation(out=hgst[gi_], in_=bank(bg), func=AF.Silu), reads=[("pb", bg)], writes=[("hgst", gi_)])
                P.dma("sp", HG[h * 128:(h + 1) * 128, t0 + j * 512:t0 + (j + 1) * 512], hgst[gi_], reads=[("hgst", gi_)], writes=[("HGs", h, su * 2 + j)])
            out = []
            for j in range(2):
                bks = [nb(), nb(), nb()]
                for a in range(3):
                    for k in range(8):
                        P.op("pe", lambda e: e.matmul(bank(bks[a]), lhsT=v[:, k, a, :], rhs=hn[:, k, ts(j)], start=(k == 0), stop=(k == 7)),
                             reads=[("w", s), ("hn", j, k)], writes=[("pb", bks[a])])
                out.append(bks)
            return out

        def q_ops(h, j, bq):
            T = TS[j]
            sgq_, kq1 = T["sgq"]
            qsb_, kq2 = T["qsb"]
            return [lambda: P.op("act", lambda e: e.activation(out=sgq_, in_=bank(bq), func=AF.Sigmoid), reads=[("pb", bq)], writes=kq1),
                    lambda: P.op("dve", lambda e: e.tensor_tensor(out=qsb_, in0=bank(bq), in1=sgq_, op=ALU.mult), reads=[("pb", bq)] + kq1, writes=kq2)]

        def chain_ops(h, j, dr, bk, kei):
            T = TS[j]
            tg = su * 2 + j
            x = [T["X"][dr][i][0] for i in range(6)]
            xk = [T["X"][dr][i][1] for i in range(6)]
            qsb_, kq2 = T["qsb"]
            col = dr * 8 + h
            kb, kbkey = KEND[kei]
            ops = []
            ops.append(lambda: P.op("act", lambda e: e.activation(out=x[0], in_=bank(bk), func=AF.Sigmoid), reads=[("pb", bk)], writes=xk[0]))
            ops.append(lambda: P.op("act", lambda e: e.activation(out=x[1], in_=x[0], func=AF.Ln, scale=oml[:, col:col + 1], bias=lbs[:, col:col + 1]),
                                    reads=xk[0] + ["oml", "lbs"], writes=xk[1]))
            ops.append(lambda: P.op("dve", lambda e: e.tensor_scalar(out=x[2], in0=x[0], scalar1=noml[:, col:col + 1], scalar2=oml[:, col:col + 1], op0=ALU.mult, op1=ALU.add),
                                    reads=xk[0] + ["oml", "noml"], writes=xk[2]))
            ops.append(lambda: P.op("dve", lambda e: e.tensor_tensor_scan(out=x[3], data0=rmask[:], data1=x[1], initial=0.0, op0=ALU.mult, op1=ALU.add),
                                    reads=xk[1] + ["rmask"], writes=xk[3]))
            if dr == 0:
                bfin, bkey = x[3], xk[3]
            else:
                x33 = x[3].rearrange("p (c t) -> p c t", t=64)
                x43 = x[4].rearrange("p (c t) -> p c t", t=64)
                ops.append(lambda: P.op("dve", lambda e: e.scalar_tensor_tensor(out=x[4], in0=x[3], scalar=-1.0, in1=x[1], op0=ALU.mult, op1=ALU.add),
                                        reads=xk[3] + xk[1], writes=xk[4]))
                ops.append(lambda: P.op("dve", lambda e: e.tensor_tensor(out=x43, in0=x43, in1=x33[:, :, 63:64].broadcast_to([128, 8, 64]), op=ALU.add),
                                        reads=xk[3] + xk[4], writes=xk[4]))
                bfin, bkey = x[4], xk[4]
            ops.append(lambda: P.op("act", lambda e: e.activation(out=x[5], in_=bfin, func=AF.Exp), reads=bkey, writes=xk[5]))
            ops.append(lambda: P.op("act", lambda e: e.activation(out=x[0], in_=bfin, func=AF.Exp, scale=-1.0), reads=bkey, writes=xk[0]))
            eb3 = x[5].rearrange("p (c t) -> p c t", t=64)
            dsel = eb3[:, :, 63:64] if dr == 0 else eb3[:, :, 0:1]
            ops.append(lambda: P.op("dve", lambda e: e.tensor_copy(out=dec_sb[:, dr, h, tg * 8:(tg + 1) * 8].rearrange("p (c o) -> p c o", o=1), in_=dsel),
                                    reads=xk[5], writes=[("dec", dr, h, tg)]))

            def qtil():
                qi = rot(A, "hqst", 4)
                P.op("dve", lambda e: e.tensor_tensor(out=hqst[qi], in0=qsb_, in1=x[5], op=ALU.mult), reads=kq2 + xk[5], writes=[("hqst", qi)])
                P.dma("sp", HQ[dr, h * 128:(h + 1) * 128, t0 + j * 512:t0 + (j + 1) * 512], hqst[qi], reads=[("hqst", qi)], writes=[("HQ", dr, h, tg)])
            ops.append(qtil)
            ops.append(lambda: P.op("dve", lambda e: e.tensor_tensor(out=x[1], in0=x[2], in1=x[0], op=ALU.mult), reads=xk[2] + xk[0], writes=xk[1]))

            def ktil():
                ki = rot(A, "hkst", 4)
                P.op("act", lambda e: e.activation(out=hkst[ki], in_=x[1], func=AF.Copy), reads=xk[1], writes=[("hkst", ki)])
                P.dma("sp", HK[dr, h * 128:(h + 1) * 128, t0 + j * 512:t0 + (j + 1) * 512], hkst[ki], reads=[("hkst", ki)], writes=[("HK", dr, h, tg)])
            ops.append(ktil)
            x13 = x[1].rearrange("p (c t) -> p c t", t=64)
            ops.append(lambda: P.op("dve", lambda e: e.tensor_tensor(out=kb.rearrange("p (c t) -> p c t", t=64), in0=x13, in1=dsel.broadcast_to([128, 8, 64]), op=ALU.mult),
                                    reads=xk[1] + xk[5], writes=[kbkey]))
            return ops

        def hgrn_tr(h, info):
            for (j, dr, kei) in info:
                tg = su * 2 + j
                kb, kbkey = KEND[kei]
                bt = nb()
                pbf = bank(bt).bitcast(BF16)
                for jb in range(4):
                    P.op("pe", lambda e: e.transpose(out=pbf[:, jb * 128:(jb + 1) * 128], in_=kb[:, jb * 128:(jb + 1) * 128], identity=ident[:]),
                         reads=[kbkey, "ident"], writes=[("pb", bt)])
                ksi = rot(A, "kst", 2)
                P.op("act", lambda e: e.activation(out=kst[ksi], in_=pbf[:, 0:512], func=AF.Copy), reads=[("pb", bt)], writes=[("kst", ksi)])
                P.dma("sp", HKT[dr, h, t0 + j * 512:t0 + (j + 1) * 512, :].rearrange("(jb p) k -> p jb k", p=128), kst[ksi].rearrange("p (jb k) -> p jb k", jb=4),
                      reads=[("kst", ksi)], writes=[("HKT", dr, h, tg)])

        other = [("aqk", 0, b_) for b_ in range(6)] + [("aqk", 1, b_) for b_ in range(6)] + [("tok", i_) for i_ in range(5)]

        def emit_other(n):
            for _ in range(n):
                if other:
                    it = other.pop(0)
                    if it[0] == "aqk":
                        aqk_block(it[1], it[2])
                    else:
                        tok_block(it[1])
        pend = {}
        for h in range(8):
            bks = hgrn_proj(h)
            lists = []
            info = []
            qlists = []
            for j in range(2):
                qlists.append(q_ops(h, j, bks[j][0]))
                for dr in range(2):
                    kei = A["cnt"]["kendT"] % len(KEND)
                    A["cnt"]["kendT"] += 1
                    lists.append(chain_ops(h, j, dr, bks[j][1 + dr], kei))
                    info.append((j, dr, kei))
            for i_ in range(2):
                for l_ in qlists:
                    l_[i_]()
            for l_ in lists:
                l_[0]()
            emit_other(2)
            mx = max(len(l_) for l_ in lists)
            for i_ in range(1, mx):
                for l_ in lists:
                    if i_ < len(l_):
                        l_[i_]()
            pend[h] = info
            if h - 2 in pend:
                hgrn_tr(h - 2, pend.pop(h - 2))
        emit_other(len(other))
        for h in sorted(pend):
            hgrn_tr(h, pend[h])
    P.barrier()
    a_reset()

    if os.environ.get("KSTOP", "") == "1":
        print("PROG", P.analyze(), flush=True)
        P.emit()
        return nc
    Qg = a_bf(4 * S).rearrange("p (c t) -> p c t", c=4)
    Kg = a_bf(4 * (S + 2 * PAD)).rearrange("p (c t) -> p c t", c=4)
    Eg = a_f32(2 * 1024).rearrange("p (k n) -> p k n", k=2)
    pex = [a_f32(1024) for _ in range(4)]
    pT = [a_bf(1024) for _ in range(4)]
    vt = [a_bf(520).rearrange("p (h e) -> p h e", h=8) for _ in range(4)]
    ust = [a_f32(520) for _ in range(2)]
    Qz = [[a_bf(512).rearrange("p (c q) -> p c q", c=4) for _ in range(2)] for _ in range(2)]
    for bq in range(2):
        for hp in range(2):
            P.op("dve", lambda e: e.memset(Qz[bq][hp], 0.0), writes=[("Qz", bq, hp)])
    cnt2 = {"pex": 0, "pT": 0, "vt": 0, "ust": 0, "po": 0, "qz": 0}
    _katt = [int(v) for v in os.environ.get('KATT', '3,99,9999').split(',')]
    for g in range(min(3, _katt[0])):
        d = DILS[g]
        L = S // d
        PADd = PAD // d
        Bs = SEG // d
        for c in range(4):
            P.dma("sp", Qg[:, c, :], QT[g * 512 + c * 128:g * 512 + (c + 1) * 128, :], writes=[("Qg", c)])
            P.dma("sp", Kg[:, c, :], KT[g * 512 + c * 128:g * 512 + (c + 1) * 128, :], writes=[("Kg", c)])
        for kt in range(2):
            P.dma("sp", Eg[:, kt, :], etab_d[g * 2 + kt], writes=[("Eg", kt)])
        Qv = Qg.rearrange("p c (i r) -> p c r i", r=d)
        Kv = Kg.rearrange("p c (i r) -> p c r i", r=d)
        VAv = VA.rearrange("(i r) f -> r i f", r=d)
        Uv = U[g].rearrange("(i r) f -> r i f", r=d)
        blocks = [(r_, m_) for r_ in range(min(d, _katt[1])) for m_ in range(min(L // 128, _katt[2]))]
        vslots = {}

        qzof = {}

        def att_qz(r, m):
            bq = cnt2["qz"] % 2
            cnt2["qz"] += 1
            qzof[(r, m)] = bq
            for hp in range(2):
                P.op("dve",
                     lambda e: e.tensor_copy(out=Qz[bq][hp][hp * 64:(hp + 1) * 64, :, :], in_=Qv[hp * 64:(hp + 1) * 64, :, r, 128 * m:128 * m + 128]),
                     reads=[("Qg", c_) for c_ in range(4)], writes=[("Qz", bq, hp)])

        def att_front(r, m, nb_=None):
            pts = []
            if (r, m) not in qzof:
                att_qz(r, m)
            bq = qzof[(r, m)]
            if nb_ is not None:
                att_qz(*nb_)
            for kt in range(2):
                jt = m + kt
                if (r, jt) not in vslots:
                    vi = cnt2["vt"] % 4
                    cnt2["vt"] += 1
                    vslots[(r, jt)] = vi
                    i0 = PADd + 128 * jt - 64
                    P.dma("sp", vt[vi].rearrange("p h e -> p (h e)"), VAv[r, i0:i0 + 128, g * 520:(g + 1) * 520], writes=[("vt", vi)])
                k0 = PADd + 128 * jt - 64
                for h in (0, 2, 4, 6, 1, 3, 5, 7):
                    c, hp = h // 2, h % 2
                    P.op("pe", lambda e: e.matmul(PB[kt][:, h * 128:(h + 1) * 128], lhsT=Kv[:, c, r, k0:k0 + 128],
                                                  rhs=Qz[bq][hp][:, c, :], start=True, stop=True),
                         reads=[("Kg", c), ("Qz", bq, hp)], writes=[("pS", kt, hp)])
                pi = cnt2["pex"] % 4
                cnt2["pex"] += 1
                for hb in range(2):
                    P.op("act", lambda e: e.activation(out=pex[pi][:, hb * 512:(hb + 1) * 512], in_=PB[kt][:, hb * 512:(hb + 1) * 512], func=AF.Exp),
                         reads=[("pS", kt, 0), ("pS", kt, 1)], writes=[("pex", pi)])
                ti = cnt2["pT"] % 4
                cnt2["pT"] += 1
                if 128 * jt == Bs:
                    lc = 74 if kt == 1 else 73
                    P.op("dve", lambda e: e.scalar_tensor_tensor(out=pT[ti], in0=pex[pi], scalar=prm[:, lc:lc + 1], in1=Eg[:, kt, :], op0=ALU.mult, op1=ALU.mult),
                         reads=[("pex", pi), ("Eg", kt), "prm"], writes=[("pT", ti)])
                else:
                    P.op("dve", lambda e: e.tensor_tensor(out=pT[ti], in0=pex[pi], in1=Eg[:, kt, :], op=ALU.mult), reads=[("pex", pi), ("Eg", kt)], writes=[("pT", ti)])
                pts.append(ti)
            return pts

        def att_back(r, m, pts):
            po = cnt2["po"] % 2
            cnt2["po"] += 1
            pO = PB[2 + po]
            for h in range(8):
                col = (h // 4) * 512 + (h % 4) * 65
                for kt in range(2):
                    vsl = vslots[(r, m + kt)]
                    P.op("pe", lambda e: e.matmul(pO[:, col:col + 65], lhsT=pT[pts[kt]][:, h * 128:(h + 1) * 128], rhs=vt[vsl][:, h, :],
                                                  start=(kt == 0), stop=(kt == 1)),
                         reads=[("pT", pts[kt]), ("vt", vsl)], writes=[("pO", po)])
            ui = cnt2["ust"] % 2
            cnt2["ust"] += 1
            for hb in range(2):
                P.op("act", lambda e: e.activation(out=ust[ui][:, hb * 260:(hb + 1) * 260], in_=pO[:, hb * 512:hb * 512 + 260], func=AF.Copy),
                     reads=[("pO", po)], writes=[("ust", ui)])
            P.dma("sp", Uv[r, 128 * m:128 * m + 128, :], ust[ui], reads=[("ust", ui)], writes=[("U", g, r, m)])

        nxt = att_front(*blocks[0], nb_=(blocks[1] if len(blocks) > 1 else None))
        for bi, (r, m) in enumerate(blocks):
            cur = nxt
            if bi + 1 < len(blocks):
                nxt = att_front(*blocks[bi + 1], nb_=(blocks[bi + 2] if bi + 2 < len(blocks) else None))
            att_back(r, m, cur)
    P.barrier()
    a_reset()

    if os.environ.get("KSTOP", "") == "2":
        print("PROG", P.analyze(), flush=True)
        P.emit()
        return nc
    qT_s = [a_bf(8 * 512).rearrange("p (h t) -> p h t", h=8) for _ in range(2)]
    kT_s = [a_bf(8 * 512).rearrange("p (h t) -> p h t", h=8) for _ in range(2)]
    ktok_s = [a_bf(8 * 1024).rearrange("p (c f) -> p c f", c=8) for _ in range(2)]
    v_s = [a_bf(8 * 1024).rearrange("p (c f) -> p c f", c=8) for _ in range(2)]
    st_f = a_f32(1024).rearrange("p (h v) -> p h v", h=8)
    st_b = a_bf(1024).rearrange("p (h v) -> p h v", h=8)
    atm = [a_bf(512).rearrange("p (h t) -> p h t", h=8) for _ in range(2)]
    o_sb = [a_f32(8 * 512).rearrange("p (h t) -> p h t", h=8) for _ in range(2)]
    cm = a_f32(2 * 512).rearrange("p (d n) -> p d n", d=2)
    for dr in range(2):
        P.dma("sp", cm[0:64, dr, :], cmask_d[dr], writes=["cm"])
    st_b2 = [st_b, a_bf(1024).rearrange("p (h v) -> p h v", h=8)]
    gcount = 0
    nseq = 0
    for dr in (1, 0):
        for h in range(8):
            P.op("dve", lambda e: e.memset(st_f[:, h, :], 0.0), writes=[("st_f", h)])
        P.op("dve", lambda e: e.memset(st_b2[nseq % 2], 0.0), writes=[("st_b", nseq % 2)])
        glist = list(range(NT)) if dr == 0 else list(range(NT - 1, -1, -1))
        steps = []
        for gi in glist:
            sl = gcount % 2
            gcount += 1
            clist = list(range(8)) if dr == 0 else list(range(7, -1, -1))
            for ci, c in enumerate(clist):
                steps.append(dict(gi=gi, sl=sl, c=c, first=(ci == 0), last=(ci == 7), n=nseq, gidx=len(steps) // 8))
                nseq += 1

        def emit_loads(stp):
            gi, sl = stp["gi"], stp["sl"]
            t0 = gi * 512
            P.dma("sp", qT_s[sl], HQ[dr].rearrange("(h k) t -> k h t", k=128)[:, :, t0:t0 + 512], reads=[("HQ", dr, h, gi) for h in range(8)], writes=[("qT_s", sl)])
            P.dma("sp", kT_s[sl], HK[dr].rearrange("(h k) t -> k h t", k=128)[:, :, t0:t0 + 512], reads=[("HK", dr, h, gi) for h in range(8)], writes=[("kT_s", sl)])
            for h in range(8):
                P.dma("sp", ktok_s[sl][0:64, :, h * 128:(h + 1) * 128], HKT[dr, h, t0:t0 + 512, :].rearrange("(c s) k -> s c k", s=64),
                      reads=[("HKT", dr, h, gi)], writes=[("ktok_s", sl)])
            P.dma("sp", v_s[sl][0:64, :, :], HV[t0:t0 + 512, :].rearrange("(c s) f -> s c f", s=64), writes=[("v_s", sl)])

        def emit_front(stp):
            sl, c, n = stp["sl"], stp["c"], stp["n"]
            par = n % 2
            pAT = PB[0][:, par * 512:(par + 1) * 512]
            pKV = PB[2 + par]
            cs = slice(c * 64, (c + 1) * 64)
            for h in range(8):
                P.op("pe", lambda e: e.matmul(pAT[0:64, h * 64:(h + 1) * 64], lhsT=kT_s[sl][:, h, cs], rhs=qT_s[sl][:, h, cs], start=True, stop=True),
                     reads=[("kT_s", sl), ("qT_s", sl)], writes=[("pAT", par)])
            for h in range(8):
                P.op("pe", lambda e: e.matmul(pKV[:, h * 128:(h + 1) * 128], lhsT=ktok_s[sl][0:64, c, h * 128:(h + 1) * 128], rhs=v_s[sl][0:64, c, h * 128:(h + 1) * 128],
                                              start=True, stop=True),
                     reads=[("ktok_s", sl), ("v_s", sl)], writes=[("pKV", par, h // 4)])
            P.op("dve", lambda e: e.tensor_tensor(out=atm[par][0:64].rearrange("p h t -> p (h t)"), in0=pAT[0:64, :], in1=cm[0:64, dr, :], op=ALU.mult),
                 reads=[("pAT", par), "cm"], writes=[("atm", par)])

        def emit_back(stp):
            gi, sl, c, n = stp["gi"], stp["sl"], stp["c"], stp["n"]
            par = n % 2
            cg = gi * 8 + c
            pOo = PB[1][:, par * 512:(par + 1) * 512]
            pKV = PB[2 + par]
            cs = slice(c * 64, (c + 1) * 64)
            sb_in, sb_out = st_b2[n % 2], st_b2[(n + 1) % 2]
            for h in range(8):
                P.op("pe", lambda e: e.matmul(pOo[:, h * 64:(h + 1) * 64], lhsT=v_s[sl][0:64, c, h * 128:(h + 1) * 128], rhs=atm[par][0:64, h, :], start=True, stop=False),
                     reads=[("v_s", sl), ("atm", par)], writes=[("pOo", par)])
                P.op("pe", lambda e: e.matmul(pOo[:, h * 64:(h + 1) * 64], lhsT=sb_in[:, h, :], rhs=qT_s[sl][:, h, cs], start=False, stop=True),
                     reads=[("qT_s", sl), ("st_b", n % 2)], writes=[("pOo", par)])
            for h in range(8):
                P.op("dve", lambda e: e.scalar_tensor_tensor(out=st_f[:, h, :], in0=st_f[:, h, :], scalar=dec_sb[:, dr, h, cg:cg + 1], in1=pKV[:, h * 128:(h + 1) * 128],
                                                             op0=ALU.mult, op1=ALU.add),
                     reads=[("st_f", h), ("pKV", par, h // 4), ("dec", dr, h, gi)], writes=[("st_f", h)])
            bnd = (dr == 0 and cg == SEG // 64 - 1) or (dr == 1 and cg == SEG // 64)
            if bnd:
                P.op("dve", lambda e: e.tensor_scalar(out=st_f.rearrange("p h v -> p (h v)"), in0=st_f.rearrange("p h v -> p (h v)"), scalar1=prm[:, 72:73], scalar2=None, op0=ALU.mult),
                     reads=[("st_f", h) for h in range(8)] + ["prm"], writes=[("st_f", h) for h in range(8)])
            P.op("act", lambda e: e.activation(out=sb_out.rearrange("p h v -> p (h v)"), in_=st_f.rearrange("p h v -> p (h v)"), func=AF.Copy),
                 reads=[("st_f", h) for h in range(8)], writes=[("st_b", (n + 1) % 2)])
            P.op("act", lambda e: e.activation(out=o_sb[sl][:, :, cs], in_=pOo.rearrange("p (h t) -> p h t", h=8), func=AF.Copy), reads=[("pOo", par)], writes=[("o_sb", sl)])
            if stp["last"]:
                t0 = gi * 512
                P.dma("sp", OFB[dr].rearrange("(h v) t -> v h t", v=128)[:, :, t0:t0 + 512], o_sb[sl], reads=[("o_sb", sl)], writes=[("OFB", dr, gi)])

        emit_loads(steps[0])
        emit_front(steps[0])
        for i, stp in enumerate(steps):
            if stp["first"] and i + 8 < len(steps):
                emit_loads(steps[i + 8])
            if i + 1 < len(steps):
                emit_front(steps[i + 1])
            emit_back(stp)
    P.barrier()
    a_reset()

    if os.environ.get("KSTOP", "") == "3":
        print("PROG", P.analyze(), flush=True)
        P.emit()
        return nc
    A = alloc_common(2)
    xs, hn = A["xs"], A["hn"]
    yaT = a_bf(4 * 1024).rearrange("p (c t) -> p c t", c=4)
    ybT = a_bf(8 * 1024).rearrange("p (c t) -> p c t", c=8)
    mg = a_bf(8 * 1024).rearrange("p (c t) -> p c t", c=8)
    ul = [a_f32(3 * 520).rearrange("p (g f) -> p g f", g=3) for _ in range(2)]
    rden = a_f32(8)
    yatok = [a_bf(512) for _ in range(2)]
    ofl = [a_f32(2 * 512).rearrange("p (d t) -> p d t", d=2) for _ in range(2)]
    T3 = [a_f32(512) for _ in range(4)]
    c3 = {"ul": 0, "yatok": 0, "ofl": 0}
    rden2 = [rden, a_f32(8)]
    _g3 = A["gact"]
    _gk = lambda t: [("gact", 0, t), ("gact", 1, t)]
    OFL = [(_g3[:, 2 * i:2 * i + 2, :].bitcast(F32).rearrange("p d t -> p d t"), _gk(2 * i) + _gk(2 * i + 1)) for i in range(4)]
    RS4 = [(_g3[:, 8 + i, :].bitcast(F32), _gk(8 + i)) for i in range(4)]
    SIL4 = [(_g3[:, 12 + i, :].bitcast(F32), _gk(12 + i)) for i in range(4)]
    SQ4 = [(_g3[:, 16 + i // 2, ts(i % 2)], [("gact", i % 2, 16 + i // 2)]) for i in range(4)]
    Ut = U.rearrange("g t f -> t g f")
    OFBv = OFB.rearrange("d (h v) t -> v d h t", v=128)
    HGv = HG.rearrange("(h v) t -> v h t", v=128)
    _mgf = mg.rearrange("p c t -> p (c t)").bitcast(F32)
    USETS = []
    for k_ in range(2):
        base = k_ * 2048
        keys = [("mg", j_, dc_) for dc_ in range(4 * k_, 4 * k_ + 4) for j_ in range(2)]
        USETS.append(dict(of=_mgf[:, base:base + 1024].rearrange("p (d t) -> p d t", d=2), rs=_mgf[:, base + 1024:base + 1536],
                          sq=_mgf[:, base + 1536:base + 1792].bitcast(BF16), hg=_mgf[:, base + 1792:base + 2048].bitcast(BF16), keys=keys))
    for k_ in range(2):
        USETS.append(dict(of=ofl[k_], rs=T3[2 * k_], sq=T3[2 * k_ + 1][:, 0:256].bitcast(BF16), hg=T3[2 * k_ + 1][:, 256:512].bitcast(BF16),
                          keys=[("ofl", k_), ("T3", 2 * k_), ("T3", 2 * k_ + 1)]))
    ucnt = {"yb": 0, "ya": 0}

    def yb_unit(su_, h, j):
        U_ = USETS[ucnt["yb"] % 4]
        ucnt["yb"] += 1
        of_, rs_, sq_, hg_, uk = U_["of"], U_["rs"], U_["sq"], U_["hg"], U_["keys"]
        tg = su_ * 2 + j
        tt0 = su_ * 1024
        st = {}

        def F1():
            P.dma("sp", of_, OFBv[:, :, h, tt0 + j * 512:tt0 + (j + 1) * 512], reads=[("OFB", 0, tg), ("OFB", 1, tg)], writes=uk)
            P.dma("sp", hg_, HGv[:, h, tt0 + j * 512:tt0 + (j + 1) * 512], writes=uk)

        def F():
            P.op("dve", lambda e: e.tensor_tensor(out=of_[:, 0, :], in0=of_[:, 0, :], in1=of_[:, 1, :], op=ALU.add), reads=uk, writes=uk)
            P.op("act", lambda e: e.activation(out=sq_, in_=of_[:, 0, :], func=AF.Square), reads=uk, writes=uk)

        def M():
            st["bn"] = nb()
            P.op("pe", lambda e: e.matmul(bank(st["bn"]), lhsT=onesH[:], rhs=sq_, start=True, stop=True), reads=uk + ["onesH"], writes=[("pb", st["bn"])])

        def B():
            bn = st["bn"]
            P.op("act", lambda e: e.activation(out=rs_, in_=bank(bn), func=AF.Ln, bias=EPS), reads=[("pb", bn)] + uk, writes=uk)
            P.op("act", lambda e: e.activation(out=rs_, in_=rs_, func=AF.Exp, scale=-0.5), reads=uk, writes=uk)
            P.op("dve", lambda e: e.scalar_tensor_tensor(out=of_[:, 1, :], in0=of_[:, 0, :], scalar=prm[:, 32 + h:33 + h], in1=rs_, op0=ALU.mult, op1=ALU.mult),
                 reads=uk + ["prm"], writes=uk)
            P.op("dve", lambda e: e.tensor_tensor(out=ybT[:, h, ts(j)], in0=of_[:, 1, :], in1=hg_, op=ALU.mult), reads=uk, writes=[("ybT", j, h)])
        return [F1, F, M, B]

    def ya_unit(su_, tb):
        j = tb // 4
        li = ucnt["ya"] % 2
        ucnt["ya"] += 1
        rd = rden2[li]
        u3 = ul[li][:, 0, :].rearrange("p (h e) -> p h e", h=8)
        yi = li
        tt0 = su_ * 1024
        st = {}

        def F1():
            P.dma("sp", ul[li], Ut[tt0 + tb * 128:tt0 + (tb + 1) * 128, :, :], writes=[("ul", li)])

        def F():
            P.op("dve", lambda e: e.tensor_tensor(out=ul[li][:, 0, :], in0=ul[li][:, 0, :], in1=ul[li][:, 1, :], op=ALU.add), reads=[("ul", li)], writes=[("ul", li)])
            P.op("dve", lambda e: e.tensor_tensor(out=ul[li][:, 0, :], in0=ul[li][:, 0, :], in1=ul[li][:, 2, :], op=ALU.add), reads=[("ul", li)], writes=[("ul", li)])
            P.op("dve", lambda e: e.reciprocal(out=rd.rearrange("p (h o) -> p h o", o=1), in_=u3[:, :, 64:65]), reads=[("ul", li)], writes=[("rden", li)])
            P.op("dve", lambda e: e.tensor_tensor(out=yatok[yi].rearrange("p (h e) -> p h e", h=8), in0=u3[:, :, 0:64],
                                                  in1=rd.rearrange("p (h o) -> p h o", o=1).broadcast_to([128, 8, 64]), op=ALU.mult),
                 reads=[("ul", li), ("rden", li)], writes=[("yatok", yi)])

        def M():
            st["bt"] = nb()
            pbf = bank(st["bt"]).bitcast(BF16)
            for fc in range(4):
                P.op("pe", lambda e: e.transpose(out=pbf[:, fc * 128:(fc + 1) * 128], in_=yatok[yi][:, fc * 128:(fc + 1) * 128], identity=ident[:]),
                     reads=[("yatok", yi), "ident"], writes=[("pb", st["bt"])])

        def B():
            pbf = bank(st["bt"]).bitcast(BF16)
            P.op("act", lambda e: e.activation(out=yaT[:, :, tb * 128:(tb + 1) * 128], in_=pbf[:, 0:512].rearrange("p (c t) -> p c t", c=4), func=AF.Copy),
                 reads=[("pb", st["bt"])], writes=[("yaT", j, tb)])
        return [F1, F, M, B]

    def make_sched(su_):
        sched = [[] for _ in range(24)]
        ybu = [(h, j) for h in range(8) for j in range(2)]
        for i_, (h, j) in enumerate(ybu):
            F1, F, M, B = yb_unit(su_, h, j)
            sched[i_].append(F1); sched[i_ + 1].append(F); sched[i_ + 3].append(M); sched[i_ + 3].append(B)
            if i_ % 2 == 0:
                F1, F, M, B = ya_unit(su_, i_ // 2)
                sched[i_].append(F1); sched[i_ + 1].append(F); sched[i_ + 3].append(M); sched[i_ + 3].append(B)
        return sched

    def run_sched_all(sched):
        for lst in sched:
            for f_ in lst:
                f_()

    def s3_xload(su_):
        for j in range(2):
            P.dma("sp", xs[:, :, ts(j)], x1Tv[:, :, su_ * 1024 + j * 512:su_ * 1024 + (j + 1) * 512], reads=[("x1T", su_, j)], writes=[("xs", j, c) for c in range(8)])
    s3_xload(0)
    run_sched_all(make_sched(0))
    for su in range(NSU):
        t0 = su * 1024
        for j in range(2):
            norm(A, 8, j)
        for dc in range(8):
            def view(sl_):
                return (sl_[:, 0:1024].rearrange("p (k n) -> p k n", k=8), sl_[:, 1024:2048].rearrange("p (k n) -> p k n", k=8),
                        sl_[:, 2048:2560].rearrange("p (k n) -> p k n", k=4), sl_[:, 2560:3584].rearrange("p (k n) -> p k n", k=8))
            s, slot = wload(A, [(lambda sl_: sl_[:, 0:3584], gtb[dc])])
            vga, vgb, va, vb = view(slot)
            for j in range(2):
                b1, b2, b3, b4 = nb(), nb(), nb(), nb()
                for k in range(8):
                    P.op("pe", lambda e: e.matmul(bank(b1), lhsT=vga[:, k, :], rhs=hn[:, k, ts(j)], start=(k == 0), stop=(k == 7)), reads=[("w", s), ("hn", j, k)], writes=[("pb", b1)])
                for k in range(8):
                    P.op("pe", lambda e: e.matmul(bank(b2), lhsT=vgb[:, k, :], rhs=hn[:, k, ts(j)], start=(k == 0), stop=(k == 7)), reads=[("w", s), ("hn", j, k)], writes=[("pb", b2)])
                for k in range(4):
                    P.op("pe", lambda e: e.matmul(bank(b3), lhsT=va[:, k, :], rhs=yaT[:, k, ts(j)], start=(k == 0), stop=(k == 3)),
                         reads=[("w", s)] + [("yaT", j, tb) for tb in range(4 * j, 4 * j + 4)], writes=[("pb", b3)])
                for k in range(8):
                    P.op("pe", lambda e: e.matmul(bank(b4), lhsT=vb[:, k, :], rhs=ybT[:, k, ts(j)], start=(k == 0), stop=(k == 7)), reads=[("w", s), ("ybT", j, k)], writes=[("pb", b4)])
                P.op("act", lambda e: e.activation(out=T3[0], in_=bank(b1), func=AF.Sigmoid), reads=[("pb", b1)], writes=[("T3", 0)])
                P.op("act", lambda e: e.activation(out=T3[1], in_=bank(b2), func=AF.Sigmoid), reads=[("pb", b2)], writes=[("T3", 1)])
                P.op("dve", lambda e: e.tensor_tensor(out=T3[2], in0=T3[0], in1=bank(b3), op=ALU.mult), reads=[("T3", 0), ("pb", b3)], writes=[("T3", 2)])
                P.op("dve", lambda e: e.tensor_tensor(out=T3[3], in0=T3[1], in1=bank(b4), op=ALU.mult), reads=[("T3", 1), ("pb", b4)], writes=[("T3", 3)])
                P.op("dve", lambda e: e.tensor_tensor(out=mg[:, dc, ts(j)], in0=T3[2], in1=T3[3], op=ALU.add), reads=[("T3", 2), ("T3", 3)], writes=[("mg", j, dc)])
        for dc in range(8):
            view = lambda sl_: sl_[:, 0:1024].rearrange("p (k n) -> p k n", k=8)
            s, slot = wload(A, [(lambda sl_: sl_[:, 0:1024], outb[dc])])
            v = view(slot)
            for j in range(2):
                b = nb()
                for k in range(8):
                    P.op("pe", lambda e: e.matmul(bank(b), lhsT=v[:, k, :], rhs=mg[:, k, ts(j)], start=(k == 0), stop=(k == 7)), reads=[("w", s), ("mg", j, k)], writes=[("pb", b)])
                P.op("dve", lambda e: e.tensor_tensor(out=xs[:, dc, ts(j)], in0=xs[:, dc, ts(j)], in1=bank(b), op=ALU.add), reads=[("pb", b), ("xs", j, dc)], writes=[("xs", j, dc)])
        for j in range(2):
            norm(A, 16, j)
        if su + 1 < NSU:
            sched = make_sched(su + 1)
            hk = {"i": 0}

            def hook():
                if hk["i"] < len(sched):
                    for f_ in sched[hk["i"]]:
                        f_()
                hk["i"] += 1
            ffn(A, gu2b, d2b, hook=hook)
            while hk["i"] < len(sched):
                hook()
        else:
            ffn(A, gu2b, d2b)
        for j in range(2):
            norm(A, 24, j, final=True)
        if su + 1 < NSU:
            s3_xload(su + 1)
        for j in range(2):
            yst = A["gact"][:, 8 * j:8 * j + 8, :].bitcast(F32)
            P.dma("sp", yTv[:, :, t0 + j * 512:t0 + (j + 1) * 512], yst,
                  reads=[("gact", jj, 8 * j + c) for c in range(8) for jj in range(2)], writes=[("yT", su, j)])
    stats = P.analyze()
    print("PROG", stats, flush=True)
    P.emit()
    return nc


_CACHE = {}


def _consts():
    et = np.zeros((6, 128, 8, 128), np.float32)
    p = np.arange(128)[:, None]
    q = np.arange(128)[None, :]
    for g in range(3):
        for kt in range(2):
            rel = p - q - 64 + 128 * kt
            valid = (np.abs(rel) <= 64)
            for h in range(8):
                et[g * 2 + kt, :, h, :] = np.where(valid, np.exp(-(SLOPES[h] * DILS[g]) * np.abs(rel).astype(np.float64)), 0.0)
    s = np.arange(64)[:, None]
    t = np.arange(64)[None, :]
    cm = np.zeros((2, 64, 8, 64), np.float32)
    cm[0] = (t >= s)[:, None, :]
    cm[1] = (t <= s)[:, None, :]
    rm = np.ones((128, 512), np.float32)
    rm[:, ::64] = 0.0
    ident = np.eye(128, dtype=np.float32).astype(ml_dtypes.bfloat16)
    return et.reshape(6, 128, 1024), cm.reshape(2, 64, 512), rm, ident


def _prep_weights(ffn1_w_gu, ffn1_w_down, w_in, w_branch_a, w_branch_b, w_out, ffn2_w_gu, ffn2_w_down):
    f = lambda a: np.asarray(a, np.float32)
    c = np.ascontiguousarray

    def gu(w):
        return c(f(w).reshape(8, 128, 2, 11, 256).transpose(3, 1, 0, 2, 4).reshape(11, 128, 4096))

    def kn(w, nk, nblk, n):
        return f(w).reshape(nk, 128, nblk, n).transpose(2, 1, 0, 3).reshape(nblk, 128, nk * n)
    wi = f(w_in)
    aqk = c(kn(wi[:, 0:3072], 8, 12, 256))
    parts = [wi[:, C_HQ:C_HQ + 1024], wi[:, C_HFF:C_HFF + 1024], wi[:, C_HFB:C_HFB + 1024], wi[:, C_HG:C_HG + 1024]]
    hg = np.stack([p_.reshape(8, 128, 8, 128) for p_ in parts], axis=2)
    hg = c(hg.transpose(3, 1, 0, 2, 4).reshape(8, 128, 4096))
    tok = c(kn(np.concatenate([wi[:, C_AV:C_AV + 1536], wi[:, C_HI:C_HI + 1024]], axis=1), 8, 5, 512))
    gt = c(np.concatenate([kn(wi[:, C_GA:C_GA + 1024], 8, 8, 128), kn(wi[:, C_GB:C_GB + 1024], 8, 8, 128),
                           kn(w_branch_a, 4, 8, 128), kn(w_branch_b, 8, 8, 128)], axis=2))
    return dict(gu1b=gu(ffn1_w_gu), d1b=c(kn(ffn1_w_down, 22, 8, 128)), aqkb=aqk, hgb=hg, tokb=tok, gtb=gt,
                outb=c(kn(w_out, 8, 8, 128)), gu2b=gu(ffn2_w_gu), d2b=c(kn(ffn2_w_down, 22, 8, 128)))


def kernel(x_prompt, x_sample, ffn1_norm, ffn1_w_gu, ffn1_w_down, mix_norm, w_in, hgrn_lb_fwd, hgrn_lb_bwd, hgrn_norm,
           w_branch_a, w_branch_b, w_out, ffn2_norm, ffn2_w_gu, ffn2_w_down, final_norm):
    x_prompt = np.asarray(x_prompt, np.float32)
    x_sample = np.asarray(x_sample, np.float32)
    S = x_sample.shape[1]
    SEG = S // 2
    assert x_prompt.shape[1] == SEG and x_prompt.shape[0] == 4 and x_sample.shape[0] == 4
    if S not in _CACHE:
        _CACHE[S] = build(S)
    nc = _CACHE[S]
    f = lambda a: np.ascontiguousarray(np.asarray(a, np.float32))
    col = lambda vec: np.asarray(vec, np.float32).reshape(8, 128).T
    et, cm, rm, ident = _consts()
    seqs = []
    for b in range(4):
        seqs.append((x_sample[b], 1.0))
    for i in range(2):
        seqs.append((np.concatenate([x_prompt[2 * i], x_prompt[2 * i + 1]], axis=0), 0.0))
    seqs.append(seqs[4]); seqs.append(seqs[5])
    shared = _prep_weights(ffn1_w_gu[0], ffn1_w_down[0], w_in[0], w_branch_a[0], w_branch_b[0], w_out[0], ffn2_w_gu[0], ffn2_w_down[0])
    shared.update(etab=et, cmask=cm, rmask=rm, ident=ident)
    in_maps = []
    for xs_, lk in seqs:
        prm = np.zeros((128, NPRM), np.float32)
        prm[:, 0:8] = col(ffn1_norm[0]); prm[:, 8:16] = col(mix_norm[0]); prm[:, 16:24] = col(ffn2_norm[0]); prm[:, 24:32] = col(final_norm)
        prm[:, 32:40] = col(hgrn_norm[0])
        prm[:, 40:48] = col(hgrn_lb_fwd[0]); prm[:, 48:56] = col(hgrn_lb_fwd[1])
        prm[:, 56:64] = col(hgrn_lb_bwd[0]); prm[:, 64:72] = col(hgrn_lb_bwd[1])
        prm[:, 72] = lk
        prm[:, 73] = 1.0; prm[:64, 73] = lk
        prm[:, 74] = 1.0; prm[64:, 74] = lk
        m = dict(shared)
        m["xT"] = np.ascontiguousarray(xs_.T)
        m["prm"] = prm
        in_maps.append(m)
    res = run_bass_kernel_spmd(nc, in_maps, core_ids=list(range(8)))
    outs = [np.ascontiguousarray(res.results[c]["yT"].T) for c in range(8)]
    y_sample = np.stack(outs[0:4], axis=0)
    y_prompt = np.stack([outs[4][:SEG], outs[4][SEG:], outs[5][:SEG], outs[5][SEG:]], axis=0)
    return (y_prompt.astype(np.float32), y_sample.astype(np.float32))
```
